# Optimizing a Trainium2 kernel written in Bass

```python
import jax, jax.numpy as jnp
from jax import lax
import numpy as np

D_MODEL = 1024
BATCH = 2
SEQ = 8192
DEPTH = 2
DEC_BATCH = 32
DEC_SEQ = 64
PAST_LEN = 1024

CHUNK = 64
Q_BLOCK = 128
A_HEADS = 4
A_HEAD_DIM = 128
A_WIDTH = A_HEADS * A_HEAD_DIM
B_HEADS = 4
B_HEAD_DIM = 64
B_WIDTH = B_HEADS * B_HEAD_DIM
C_GROUPS = 4
C_GROUP_DIM = 64
C_WIDTH = C_GROUPS * C_GROUP_DIM
POOL_WINDOWS = (2, 4, 8, 16)
POOL_STATE = 15
D_MIX = A_WIDTH + B_WIDTH + C_WIDTH
SPLIT_SIZES = (A_WIDTH, A_WIDTH, A_WIDTH, A_WIDTH, B_WIDTH, B_WIDTH, B_WIDTH, B_WIDTH, C_WIDTH, C_WIDTH)
D_IN = 4 * A_WIDTH + 4 * B_WIDTH + 2 * C_WIDTH
N_MEM = 256
X_HEADS = 4
X_HEAD_DIM = D_MODEL // X_HEADS
EPS = 1e-6

kernel_name = 'hybrid_stream_encoder_step'

F32 = jnp.float32


def rms_norm(x, g):
    xf = x.astype(F32)
    y = xf * lax.rsqrt(jnp.mean(xf * xf, axis=-1, keepdims=True) + EPS)
    return (y * g.astype(F32)).astype(x.dtype)


def hgrn2_scan(q, k, v, g, s0):
    bsz, L, H, _ = q.shape
    c = L if L <= CHUNK else CHUNK
    n = L // c
    causal = jnp.tril(jnp.ones((c, c), dtype=bool))[None, :, :, None, None]

    def to_chunks(t):
        return jnp.moveaxis(t.reshape(bsz, n, c, H, t.shape[-1]), 1, 0)

    def step(s, inp):
        qc, kc, vc, gc = inp
        G = jnp.cumsum(gc, axis=1)
        o = jnp.einsum('bthk,bhkv->bthv', qc * jnp.exp(G), s)
        diff = G[:, :, None] - G[:, None, :]
        decay = jnp.exp(jnp.where(causal, diff, -jnp.inf))
        att = jnp.einsum('bthk,btshk,bshk->bhts', qc, decay, kc)
        o = o + jnp.einsum('bhts,bshv->bthv', att, vc)
        GL = G[:, -1]
        s_new = jnp.exp(GL)[..., None] * s + jnp.einsum('bshk,bshv->bhkv', kc * jnp.exp(GL[:, None] - G), vc)
        return s_new, o

    s_fin, o = lax.scan(step, s0, (to_chunks(q), to_chunks(k), to_chunks(v), to_chunks(g)))
    return jnp.moveaxis(o, 0, 1).reshape(bsz, L, H, -1), s_fin


def sb_block(q, q_pos, k, v, k_pos):
    z = jnp.einsum('bqhd,bkhd->bhqk', q, k).astype(F32) * (B_HEAD_DIM ** -0.5)
    mask = (k_pos[None, :] < q_pos[:, None])[None, None]
    log_1m = jnp.where(mask, jax.nn.log_sigmoid(-z), 0.0)
    after = lax.cumsum(log_1m, axis=3, reverse=True) - log_1m
    w = jnp.where(mask, jnp.exp(jax.nn.log_sigmoid(z) + after), 0.0)
    return jnp.einsum('bhqk,bkhd->bqhd', w.astype(v.dtype), v)


def sb_attention(q, k, v, q_pos, k_pos):
    bsz, Lq, H, D = q.shape
    if Lq <= Q_BLOCK:
        return sb_block(q, q_pos, k, v, k_pos)
    nb = Lq // Q_BLOCK
    qb = jnp.moveaxis(q.reshape(bsz, nb, Q_BLOCK, H, D), 1, 0)
    pb = q_pos.reshape(nb, Q_BLOCK)
    ob = lax.map(lambda a: sb_block(a[0], a[1], k, v, k_pos), (qb, pb))
    return jnp.moveaxis(ob, 0, 1).reshape(bsz, Lq, H, D)


def pool_mix(u, prev, pos0, w_pool, pool_scale):
    bsz, L, _ = u.shape
    ext = jnp.concatenate([prev, u], axis=1)
    ef = ext.astype(F32)
    P = jnp.concatenate([jnp.zeros((bsz, 1, C_WIDTH), F32), jnp.cumsum(ef, axis=1)], axis=1)
    pos = pos0 + jnp.arange(L)
    outs = []
    for gi, w in enumerate(POOL_WINDOWS):
        lo_c, hi_c = gi * C_GROUP_DIM, (gi + 1) * C_GROUP_DIM
        hi = P[:, POOL_STATE + 1:POOL_STATE + 1 + L, lo_c:hi_c]
        lo = P[:, POOL_STATE + 1 - w:POOL_STATE + 1 - w + L, lo_c:hi_c]
        cnt = jnp.minimum(pos + 1, w).astype(F32)[None, :, None]
        outs.append((hi - lo) / cnt - ef[:, POOL_STATE:, lo_c:hi_c])
    d = jnp.concatenate(outs, axis=-1).reshape(bsz, L, C_GROUPS, C_GROUP_DIM)
    y = jnp.einsum('blgc,gcd->blgd', d, w_pool.astype(F32)).reshape(bsz, L, C_WIDTH) * pool_scale.astype(F32)
    return y.astype(u.dtype), ext[:, -POOL_STATE:]


def mem_kv(mem, g, w_k, w_v):
    bsz = mem.shape[0]
    m = rms_norm(mem, g)
    return ((m @ w_k).reshape(bsz, N_MEM, X_HEADS, X_HEAD_DIM),
            (m @ w_v).reshape(bsz, N_MEM, X_HEADS, X_HEAD_DIM))


def cross_attn(h, mem_k, mem_v, w_q, w_o):
    bsz, L, _ = h.shape
    q = (h @ w_q).reshape(bsz, L, X_HEADS, X_HEAD_DIM)
    s = jnp.einsum('blhd,bmhd->bhlm', q, mem_k.astype(h.dtype)).astype(F32) * (X_HEAD_DIM ** -0.5)
    p = jax.nn.softmax(s, axis=-1).astype(h.dtype)
    o = jnp.einsum('bhlm,bmhd->blhd', p, mem_v.astype(h.dtype)).reshape(bsz, L, D_MODEL)
    return o @ w_o


def trunk_layer(x, mem_k, mem_v, sb_k_past, sb_v_past, s0, pool_prev,
                n_pre, n_post, w_in, lb, onorm_g, w_pool, pool_scale, w_out,
                nx_pre, nx_post, w_xq, w_xo):
    bsz, L, _ = x.shape
    past = sb_k_past.shape[1]
    h = rms_norm(x, n_pre)
    split_idx = [int(i) for i in np.cumsum(SPLIT_SIZES)[:-1]]
    aq, af, ai, ag, bq, bk, bv, bg, cu, cg = jnp.split(h @ w_in, split_idx, axis=-1)

    fl = af.astype(F32)
    log_f = jnp.logaddexp(jnp.log(lb), jnp.log1p(-lb) + jax.nn.log_sigmoid(fl))
    k_a = (1.0 - lb) * jax.nn.sigmoid(-fl)
    q_a = jax.nn.silu(aq.astype(F32)) * (A_HEAD_DIM ** -0.5)
    ah = lambda t: t.reshape(bsz, L, A_HEADS, A_HEAD_DIM)
    o_a, s_new = hgrn2_scan(ah(q_a), ah(k_a), ah(ai.astype(F32)), ah(log_f), s0.astype(F32))
    o_a = o_a * lax.rsqrt(jnp.mean(o_a * o_a, axis=-1, keepdims=True) + EPS) * onorm_g.astype(F32).reshape(A_HEADS, A_HEAD_DIM)
    a_out = (o_a.reshape(bsz, L, A_WIDTH) * jax.nn.silu(ag.astype(F32))).astype(x.dtype)

    bh = lambda t: t.reshape(bsz, L, B_HEADS, B_HEAD_DIM)
    k_b, v_b = bh(bk), bh(bv)
    k_all = jnp.concatenate([sb_k_past.astype(x.dtype), k_b], axis=1)
    v_all = jnp.concatenate([sb_v_past.astype(x.dtype), v_b], axis=1)
    o_b = sb_attention(bh(bq), k_all, v_all, past + jnp.arange(L), jnp.arange(past + L))
    b_out = o_b.reshape(bsz, L, B_WIDTH) * jax.nn.silu(bg)

    c_pool, pool_new = pool_mix(cu, pool_prev.astype(x.dtype), past, w_pool, pool_scale)
    c_out = c_pool * jax.nn.silu(cg)

    mix = jnp.concatenate([a_out, b_out, c_out], axis=-1) @ w_out
    x = x + rms_norm(mix, n_post)
    x = x + rms_norm(cross_attn(rms_norm(x, nx_pre), mem_k, mem_v, w_xq, w_xo), nx_post)
    return x, k_b, v_b, s_new.astype(x.dtype), pool_new


def setup_inputs(seed: int = 0) -> dict:
    key = jax.random.key(seed)
    ks = jax.random.split(key, 24)

    def nrm(k, shape, scale=1.0):
        return jax.random.normal(k, shape, F32) * scale

    def gain(k, shape, s=0.05):
        return 1.0 + s * jax.random.normal(k, shape, F32)

    return {
        'x_prompt': nrm(ks[0], (BATCH, SEQ, D_MODEL)),
        'x_sample': nrm(ks[1], (DEC_BATCH, DEC_SEQ, D_MODEL)),
        'mem_prompt': nrm(ks[2], (BATCH, N_MEM, D_MODEL)),
        'cache_sb_k': nrm(ks[3], (DEPTH, DEC_BATCH, PAST_LEN, B_HEADS, B_HEAD_DIM)),
        'cache_sb_v': nrm(ks[4], (DEPTH, DEC_BATCH, PAST_LEN, B_HEADS, B_HEAD_DIM)),
        'state_hgrn': nrm(ks[5], (DEPTH, DEC_BATCH, A_HEADS, A_HEAD_DIM, A_HEAD_DIM), 0.3),
        'state_pool': nrm(ks[6], (DEPTH, DEC_BATCH, POOL_STATE, C_WIDTH)),
        'cache_mem_k': nrm(ks[7], (DEPTH, DEC_BATCH, N_MEM, X_HEADS, X_HEAD_DIM)),
        'cache_mem_v': nrm(ks[8], (DEPTH, DEC_BATCH, N_MEM, X_HEADS, X_HEAD_DIM)),
        'norm_mix_pre': gain(ks[9], (DEPTH, D_MODEL)),
        'norm_mix_post': gain(ks[10], (DEPTH, D_MODEL)),
        'w_in': nrm(ks[11], (DEPTH, D_MODEL, D_IN), D_MODEL ** -0.5),
        'hgrn_lb_logits': nrm(ks[12], (DEPTH, A_WIDTH), 0.5),
        'hgrn_onorm_g': gain(ks[13], (DEPTH, A_WIDTH)),
        'w_pool': nrm(ks[14], (DEPTH, C_GROUPS, C_GROUP_DIM, C_GROUP_DIM), C_GROUP_DIM ** -0.5),
        'pool_scale': gain(ks[15], (DEPTH, C_WIDTH), 0.1),
        'w_out': nrm(ks[16], (DEPTH, D_MIX, D_MODEL), D_MIX ** -0.5),
        'norm_x_pre': gain(ks[17], (DEPTH, D_MODEL)),
        'norm_x_post': gain(ks[18], (DEPTH, D_MODEL)),
        'norm_mem': gain(ks[19], (DEPTH, D_MODEL)),
        'w_xq': nrm(ks[20], (DEPTH, D_MODEL, D_MODEL), D_MODEL ** -0.5),
        'w_xk': nrm(ks[21], (DEPTH, D_MODEL, D_MODEL), D_MODEL ** -0.5),
        'w_xv': nrm(ks[22], (DEPTH, D_MODEL, D_MODEL), D_MODEL ** -0.5),
        'w_xo': nrm(ks[23], (DEPTH, D_MODEL, D_MODEL), D_MODEL ** -0.5),
    }


def reference(x_prompt, x_sample, mem_prompt, cache_sb_k, cache_sb_v, state_hgrn, state_pool,
              cache_mem_k, cache_mem_v, norm_mix_pre, norm_mix_post, w_in, hgrn_lb_logits,
              hgrn_onorm_g, w_pool, pool_scale, w_out, norm_x_pre, norm_x_post, norm_mem,
              w_xq, w_xk, w_xv, w_xo):
    lbs = jax.nn.softmax(hgrn_lb_logits.astype(F32), axis=0)
    lower_bounds = jnp.maximum(jnp.cumsum(lbs, axis=0) - lbs[0], 0.0)

    bsz_p = x_prompt.shape[0]
    empty_kv = jnp.zeros((bsz_p, 0, B_HEADS, B_HEAD_DIM), x_prompt.dtype)
    s_zero = jnp.zeros((bsz_p, A_HEADS, A_HEAD_DIM, A_HEAD_DIM), F32)
    pool_zero = jnp.zeros((bsz_p, POOL_STATE, C_WIDTH), x_prompt.dtype)

    xp, xs = x_prompt, x_sample
    p_k, p_v, p_s, p_pool, p_mk, p_mv = [], [], [], [], [], []
    s_k, s_v, s_s, s_pool = [], [], [], []
    for l in range(DEPTH):
        lw = (norm_mix_pre[l], norm_mix_post[l], w_in[l], lower_bounds[l], hgrn_onorm_g[l],
              w_pool[l], pool_scale[l], w_out[l], norm_x_pre[l], norm_x_post[l], w_xq[l], w_xo[l])
        mk, mv = mem_kv(mem_prompt, norm_mem[l], w_xk[l], w_xv[l])
        xp, kb, vb, sn, pn = trunk_layer(xp, mk, mv, empty_kv, empty_kv, s_zero, pool_zero, *lw)
        p_k.append(kb); p_v.append(vb); p_s.append(sn); p_pool.append(pn); p_mk.append(mk); p_mv.append(mv)
        xs, kb, vb, sn, pn = trunk_layer(xs, cache_mem_k[l], cache_mem_v[l], cache_sb_k[l], cache_sb_v[l],
                                         state_hgrn[l], state_pool[l], *lw)
        s_k.append(kb); s_v.append(vb); s_s.append(sn); s_pool.append(pn)

    return (xp, xs,
            jnp.stack(p_k), jnp.stack(p_v), jnp.stack(p_s), jnp.stack(p_pool), jnp.stack(p_mk), jnp.stack(p_mv),
            jnp.stack(s_k), jnp.stack(s_v), jnp.stack(s_s), jnp.stack(s_pool))
```

```python
import os
import numpy as np
from contextlib import ExitStack
import concourse.bass as bass
import concourse.mybir as mybir
from concourse.bass_utils import run_bass_kernel_spmd

F32 = mybir.dt.float32
BF16 = mybir.dt.bfloat16
AF = mybir.ActivationFunctionType
ALU = mybir.AluOpType
AX = mybir.AxisListType

DEPTH = 2
NT = 256
NRP = 8
NR = NRP + 1
TOKP = NRP * NT
TOK = TOKP + NT
EPS = 1e-6
SROWS = 512
TROWS = 32
NDMA = 24
SAME_ENGINE_NOSYNC = ('pe',)

PRM_L = 50
P_NPRE, P_NPOST, P_XPRE, P_XPOST, P_NMEM, P_LB, P_OG, P_PS = 0, 8, 16, 24, 32, 40, 44, 48
C_ID = 0
C_TRI = 128
C_SBM = 256
C_SM = 768
C_HM = 832
C_SEL = 896
C_INV = 904
C_RST = 1160
C_W = 1416
NCST = 1418


class Prog:
    ENG = ('pe', 'act', 'dve', 'pool', 'sp')

    def __init__(self, nc, es):
        self.nc, self.es = nc, es
        self.streams = {e: [] for e in self.ENG}
        self.count = {e: 0 for e in self.ENG}
        self.waited = {e: {} for e in self.ENG}
        self.bufs = {}
        self.sems = {}
        for e in ('pe', 'act', 'dve', 'pool'):
            self.sems[e] = es.enter_context(nc.semaphore('s_' + e))
        self.dma_cnt = [0] * NDMA
        self.dma_next = 0
        for i in range(NDMA):
            self.sems['dq%d' % i] = es.enter_context(nc.semaphore('dq%d' % i))
        self.ncc = 0

    def _deps(self, eng, reads, writes, extra=()):
        deps = list(extra)
        for b in reads:
            st = self.bufs.get(b)
            if st and st[0]:
                deps.append(st[0])
            if st and b[0] == 'p' and b[1] in 'gzco' and (b[2:3].isdigit() or b == 'po'):
                deps.extend(st[1].items())
        for b in writes:
            st = self.bufs.get(b)
            if st:
                if st[0]:
                    deps.append(st[0])
                deps.extend(st[1].items())
        w = self.waited[eng]
        for (sk, v) in deps:
            if sk == eng and eng in SAME_ENGINE_NOSYNC:
                continue
            if w.get(sk, 0) < v:
                w[sk] = v
                self.streams[eng].append(('wait', sk, v))

    def _update(self, tok, reads, writes):
        for b in reads:
            st = self.bufs.setdefault(b, [None, {}])
            if st[1].get(tok[0], 0) < tok[1]:
                st[1][tok[0]] = tok[1]
        for b in writes:
            self.bufs[b] = [tok, {}]

    def op(self, eng, fn, reads=(), writes=()):
        self._deps(eng, reads, writes)
        self.count[eng] += 1
        tok = (eng, self.count[eng])
        self.streams[eng].append(('op', fn))
        self._update(tok, reads, writes)

    def dma(self, out, in_, reads=(), writes=(), slow=False):
        i = self.dma_next
        self.dma_next = (i + 1) % NDMA
        sk = 'dq%d' % i
        prev = self.dma_cnt[i]
        self._deps('sp', reads, writes, extra=[(sk, prev)] if prev else [])
        self.dma_cnt[i] = prev + 16
        self.streams['sp'].append(('dma', out, in_, sk, slow))
        self._update((sk, prev + 16), reads, writes)

    def allgather(self, in_ap, out_ap, reads, writes):
        sk = 'cc%d' % self.ncc
        self.ncc += 1
        self.sems[sk] = self.es.enter_context(self.nc.semaphore(sk))
        self._deps('pool', reads, writes)
        self.streams['pool'].append(('cc', in_ap, out_ap, sk))
        self._update((sk, 1), reads, writes)

    def finish(self):
        for i in range(NDMA):
            if self.dma_cnt[i]:
                self.streams['sp'].append(('wait', 'dq%d' % i, self.dma_cnt[i]))

    def build(self):
        nc = self.nc
        engs = {'pe': 'tensor', 'act': 'scalar', 'dve': 'vector', 'pool': 'gpsimd', 'sp': 'sync'}
        with nc.Block() as block:
            for ename, bname in engs.items():
                stream = self.streams[ename]
                sems = self.sems

                def body(eng, stream=stream, ename=ename):
                    for it in stream:
                        if it[0] == 'wait':
                            eng.wait_ge(sems[it[1]], it[2])
                        elif it[0] == 'op':
                            it[1](eng).then_inc(sems[ename], 1)
                        elif it[0] == 'dma':
                            if it[4]:
                                with nc.allow_non_contiguous_dma(reason="small transposing dma"):
                                    eng.dma_start(out=it[1], in_=it[2]).then_inc(sems[it[3]], 16)
                            else:
                                eng.dma_start(out=it[1], in_=it[2]).then_inc(sems[it[3]], 16)
                        elif it[0] == 'cc':
                            eng.collective_compute(
                                "AllGather", ALU.bypass,
                                replica_groups=[[0, 1, 2, 3], [4, 5, 6, 7]],
                                ins=[it[1]], outs=[it[2]]).then_inc(sems[it[3]])
                getattr(block, bname)(body)


def build_program():
    nc = bass.Bass("TRN2", target_bir_lowering=False)
    es = ExitStack()
    P = Prog(nc, es)

    def din(name, shape, dt=F32):
        return nc.dram_tensor(name, list(shape), dt, kind="ExternalInput").ap()

    def dout(name, shape, dt=F32):
        return nc.dram_tensor(name, list(shape), dt, kind="ExternalOutput").ap()

    def dscr(name, shape, dt=F32):
        return nc.dram_tensor(name, list(shape), dt).ap()

    x_p = din("x_p", [TOKP, 1024]); x_s = din("x_s", [NT, 1024]); memp = din("memp", [256, 1024])
    csk = din("csk", [2, 4, 1024, 256]); csv = din("csv", [2, 4, 1024, 256])
    sth = din("sth", [2, 4, 4, 128, 128]); stp = din("stp", [2, 4, 15, 256])
    cmk = din("cmk", [2, 4, 256, 1024]); cmv = din("cmv", [2, 4, 256, 1024])
    w_in = din("w_in", [2, 1024, 3584])
    wsq = [din(n, [2, 1024, 1024]) for n in ("w_out", "w_xq", "w_xo", "w_xk", "w_xv")]
    prm = din("prm", [128, 2 * PRM_L]); lbl = din("lbl", [128, 8]); wpl = din("wpl", [128, 2 * 2 * 64])
    cst = din("cst", [128, NCST])

    y_p = dout("y_p", [TOKP, 1024]); y_s = dout("y_s", [NT, 1024])
    o_pk = dout("o_pk", [2, TOKP, 256]); o_pv = dout("o_pv", [2, TOKP, 256])
    o_phs = dout("o_phs", [2, 4, 128, 128]); o_pps = dout("o_pps", [2, 15, 256])
    o_pmk = dout("o_pmk", [2, 256, 1024]); o_pmv = dout("o_pmv", [2, 256, 1024])
    o_sk = dout("o_sk", [2, NT, 256]); o_sv = dout("o_sv", [2, NT, 256])
    o_shs = dout("o_shs", [2, 4, 4, 128, 128]); o_sps = dout("o_sps", [2, 4, 15, 256])

    win_bf = [dscr("win_bf%d" % l, [128, 8, 3584], BF16) for l in range(2)]
    wsq_bf = [[dscr("wsq_bf%d_%d" % (l, m), [128, 8, 1024], BF16) for m in range(5)] for l in range(2)]
    xs = dscr("xs", [128, 8, TOK])
    kv_loc = [[dscr("kvl%d_%d" % (l, r), [512, 256], BF16) for r in range(NRP)] for l in range(2)]
    kv_g = [[dscr("kvg%d_%d" % (l, r), [2048, 256], BF16) for r in range(NRP)] for l in range(2)]
    s_loc = [[dscr("sl%d_%d" % (l, r), [SROWS, 512]) for r in range(NRP)] for l in range(2)]
    s_g = [[dscr("sg%d_%d" % (l, r), [4 * SROWS, 512]) for r in range(NRP)] for l in range(2)]
    t_loc = [[dscr("tl%d_%d" % (l, r), [TROWS, 512]) for r in range(NRP)] for l in range(2)]
    t_g = [[dscr("tg%d_%d" % (l, r), [4 * TROWS, 512]) for r in range(NRP)] for l in range(2)]

    def sb(name, shape, dt=F32):
        return es.enter_context(nc.sbuf_tensor(name, list(shape), dt))

    def ps(name, shape=(128, 512), dt=F32):
        return es.enter_context(nc.psum_tensor(name, list(shape), dt))

    cstt = sb("cstt", [128, NCST]); prmt = sb("prmt", [128, 2 * PRM_L]); lbt = sb("lbt", [128, 8])
    lbv = sb("lbv", [128, 2, 4]); omlv = sb("omlv", [128, 2, 4]); lbtmp = sb("lbtmp", [128, 3, 4])
    wpt = sb("wpt", [128, 2, 2, 64]); wpb = sb("wpb", [128, 2, 2, 64], BF16)
    idb = sb("idb", [128, 128], BF16); trib = sb("trib", [128, 128], BF16); oneb = sb("oneb", [128, 128], BF16)
    epst = sb("epst", [128, 1])
    stg = [sb("stg%d" % i, [128, 2048]) for i in range(2)]
    bstg = [sb("bstg%d" % i, [128, 2048], BF16) for i in range(2)]
    wg = [sb("wg%d" % i, [128, 8, 512], BF16) for i in range(2)]
    xT = sb("xT", [128, 8, NT]); hT = sb("hT", [128, 8, NT], BF16)
    xin = [sb("xin%d" % i, [128, 1024]) for i in range(2)]
    sqb = [sb("sqb%d" % i, [128, NT], BF16) for i in range(2)]
    lnv = sb("lnv", [128, NT]); rstd = sb("rstd", [128, NT])
    q_sb = sb("q_sb", [128, 4, NT]); fgate = sb("fgate", [128, 4, NT]); gate_a = sb("gate_a", [128, 4, NT], BF16)
    v_tok = sb("v_tok", [64, 4, 512], BF16); qTb = sb("qTb", [128, 2, NT], BF16); kTl = sb("kTl", [128, 2, NT], BF16)
    vnew = sb("vnew", [128, 4, 256], BF16); knew = sb("knew", [128, 2, 4, 128], BF16); qm = sb("qm", [128, 4, NT], BF16); ktok = sb("ktok", [128, 2, 256]); vtok = sb("vtok", [128, 2, 256]); vbl = sb("vbl", [128, 2, 256], BF16)
    gate_b = sb("gate_b", [128, 2, NT], BF16); gate_c = sb("gate_c", [128, 2, NT], BF16)
    Ep = sb("Ep", [128, 2, 2, 143]); Es = sb("Es", [128, 2, 4, 79])
    Sst = sb("Sst", [128, 4, 512])
    hs = {n: sb("hs_" + n, [128, NT]) for n in ("f", "lg", "kk", "G", "Gr", "Gl", "e1", "e2")}
    qt2 = [sb("qt%d" % i, [128, NT], BF16) for i in range(2)]; kt2 = [sb("kt%d" % i, [128, NT], BF16) for i in range(2)]; ke2 = [sb("ke%d" % i, [128, NT], BF16) for i in range(2)]
    qG = sb("qG", [128, 4, NT], BF16); Dt = sb("Dt", [128, 4, 4]); attT = sb("attT", [64, 4, NT], BF16)
    ketok2 = [sb("ketok%d" % i, [64, 4, 128], BF16) for i in range(2)]
    Sm = [sb("Sm%d" % i, [128, 512]) for i in range(2)]; Dg = sb("Dg", [128, 4, 16]); Srun = sb("Srun", [128, 512])
    Ssel = [sb("Ssel%d" % i, [128, 512]) for i in range(2)]; Sbf = sb("Sbf", [128, 4, 512], BF16)
    Sin_s = Ssel[0]
    Tg = sb("Tg", [128, 4, 60]); Tprev = sb("Tprev", [128, 2, 15])
    pa = [sb("pa%d" % i, [128, 640]) for i in range(4)]
    dT = sb("dT", [128, 2, NT], BF16); ptmp = sb("ptmp", [128, 128])
    e_sb = [[sb("e%d_%d" % (a, b), [128, NT]) for b in range(2)] for a in range(2)]
    sp_sb = [[sb("sp%d_%d" % (a, b), [128, NT], BF16) for b in range(2)] for a in range(2)]
    A_sb = [[sb("A%d_%d" % (a, b), [128, NT], BF16) for b in range(2)] for a in range(2)]
    Rsum = [sb("Rsum%d" % a, [128, NT]) for a in range(2)]; Rbf = [sb("Rbf%d" % a, [128, NT], BF16) for a in range(2)]
    kTg = [sb("kTg%d" % i, [128, 4, 128], BF16) for i in range(2)]; vg = [sb("vg%d" % i, [128, 4, 128], BF16) for i in range(2)]
    mixT = sb("mixT", [128, 8, NT], BF16); yT = sb("yT", [128, 8, NT]); qxT = sb("qxT", [128, 8, NT], BF16)
    oxT = sb("oxT", [128, 8, NT], BF16); pexp = sb("pexp", [128, 256]); pbf = sb("pbf", [128, 256], BF16)
    pT_sb = sb("pT_sb", [128, 8, 128], BF16); memKT = sb("memKT", [128, 8, 256], BF16); memV = sb("memV", [128, 2, 1024], BF16)
    sm4 = sb("sm4", [128, 4]); t1 = sb("t1", [128, NT]); ystage = xin[1]
    mT = xT; mhT = hT

    pg = [ps("pg%d" % i) for i in range(3)]
    pz = [ps("pz%d" % i) for i in range(2)]; pc = [ps("pc%d" % i) for i in range(2)]; po = ps("po")
    pgi = [0]

    def nextpg():
        pgi[0] = (pgi[0] + 1) % 3
        return pg[pgi[0]], "pg%d" % pgi[0]

    K = lambda t: t.name if hasattr(t, "name") else t
    WIK = lambda l: ["wi%d_%d_%d" % (l, kc, hf) for kc in range(8) for hf in range(2)]
    WQK = lambda l, m: ["wq%d_%d_%d" % (l, m, k) for k in range(4)]

    def mm(out, lhsT, rhs, start, stop, r, w):
        P.op('pe', lambda e: e.matmul(out, lhsT, rhs, start=start, stop=stop), r, w)

    def tp(out, in_, ident, r, w):
        P.op('pe', lambda e: e.transpose(out, in_, ident), r, w)

    def act(out, in_, func, r, w, bias=None, scale=None, accum=None):
        kw = {}
        if bias is not None: kw['bias'] = bias
        if scale is not None: kw['scale'] = scale
        if accum is not None: kw['accum_out'] = accum
        P.op('act', lambda e: e.activation(out, in_, func, **kw), r, w)

    def tt(eng, out, in0, in1, op, r, w):
        P.op(eng, lambda e: e.tensor_tensor(out, in0, in1, op), r, w)

    def ts(eng, out, in0, s1, s2, op0, op1, r, w):
        if op1 is None:
            P.op(eng, lambda e: e.tensor_scalar(out, in0, s1, None, op0), r, w)
        else:
            P.op(eng, lambda e: e.tensor_scalar(out, in0, s1, s2, op0, op1), r, w)

    def stt(out, in0, scalar, in1, op0, op1, r, w):
        P.op('dve', lambda e: e.scalar_tensor_tensor(out, in0, scalar, in1, op0, op1), r, w)

    def cpy(eng, out, in_, r, w):
        if eng == 'act':
            P.op('act', lambda e: e.copy(out, in_), r, w)
        else:
            P.op(eng, lambda e: e.tensor_copy(out, in_), r, w)

    def mset(eng, out, val, w):
        P.op(eng, lambda e: e.memset(out, val), (), w)

    P.dma(cstt[:], cst, (), ["cstt"]); P.dma(prmt[:], prm, (), ["prmt"]); P.dma(lbt[:], lbl, (), ["lbt"])
    P.dma(wpt[:].rearrange("p a b c -> p (a b c)"), wpl, (), ["wpt"])
    cpy('dve', idb[:], cstt[:, C_ID:C_ID + 128], ["cstt"], ["idb"])
    cpy('dve', trib[:], cstt[:, C_TRI:C_TRI + 128], ["cstt"], ["trib"])
    cpy('dve', wpb[:], wpt[:], ["wpt"], ["wpb"])
    mset('pool', oneb[:], 1.0, ["oneb"]); mset('pool', epst[:], EPS, ["epst"])
    zt = sb("zt", [128, 52]); mset('pool', zt[:], 0.0, ["zt"])
    mset('pool', vnew[:], 0.0, ["vnew"]); mset('pool', knew[:], 0.0, ["knew"]); mset('pool', qm[:], 0.0, ["qm"])
    for l_ in range(2):
        for r_ in range(NRP):
            P.dma(t_loc[l_][r_][19:32, :].rearrange("r c -> (r c)").rearrange("(p b) -> p b", b=52), zt[:], ["zt"], ["sl_pad%d_%d" % (l_, r_)])
    mset('pool', Srun[:], 0.0, ["Srun"]); mset('pool', Tprev[:], 0.0, ["Tprev"])
    idf = cstt[:, C_ID:C_ID + 128]
    STOP = int(os.environ.get('KDBG_STOP', 99))
    if STOP <= 0:
        P.finish(); return nc, P, es
    act(lbtmp[:, 0, :], lbt[:, 0:4], AF.Exp, ["lbt"], ["lbtmp"])
    act(lbtmp[:, 1, :], lbt[:, 4:8], AF.Exp, ["lbt"], ["lbtmp"])
    tt('dve', lbtmp[:, 2, :], lbtmp[:, 0, :], lbtmp[:, 1, :], ALU.add, ["lbtmp"], ["lbtmp"])
    act(lbtmp[:, 2, :], lbtmp[:, 2, :], AF.Ln, ["lbtmp"], ["lbtmp"])
    act(lbtmp[:, 2, :], lbtmp[:, 2, :], AF.Exp, ["lbtmp"], ["lbtmp"], scale=-1.0)
    tt('dve', lbtmp[:, 0, :], lbtmp[:, 0, :], lbtmp[:, 2, :], ALU.mult, ["lbtmp"], ["lbtmp"])
    tt('dve', lbtmp[:, 1, :], lbtmp[:, 1, :], lbtmp[:, 2, :], ALU.mult, ["lbtmp"], ["lbtmp"])
    tt('dve', lbv[:, 0, :], lbtmp[:, 0, :], lbtmp[:, 0, :], ALU.subtract, ["lbtmp"], ["lbv"])
    tt('dve', lbtmp[:, 2, :], lbtmp[:, 0, :], lbtmp[:, 1, :], ALU.add, ["lbtmp"], ["lbtmp"])
    tt('dve', lbv[:, 1, :], lbtmp[:, 2, :], lbtmp[:, 0, :], ALU.subtract, ["lbtmp"], ["lbv"])
    ts('dve', lbv[:], lbv[:], 0.0, None, ALU.max, None, ["lbv"], ["lbv"])
    ts('dve', omlv[:], lbv[:], -1.0, 1.0, ALU.mult, ALU.add, ["lbv"], ["omlv"])

    if STOP <= 1:
        P.finish(); return nc, P, es
    ceng = ['dve', 'act', 'dve']
    wi = [0]

    def wcast(src, dst, n, wkey):
        i = wi[0]; wi[0] += 1
        s, b = stg[i % 2], bstg[i % 2]
        P.dma(s[:, :n], src, (), [K(s)])
        cpy(ceng[i % 3], b[:, :n], s[:, :n], [K(s)], [K(b)])
        P.dma(dst, b[:, :n], [K(b)], [wkey])

    for l in range(2):
        for kc in range(8):
            for hf in range(2):
                wcast(w_in[l, kc * 128:(kc + 1) * 128, hf * 1792:(hf + 1) * 1792],
                      win_bf[l][:, kc, hf * 1792:(hf + 1) * 1792], 1792, "wi%d_%d_%d" % (l, kc, hf))
        for m in range(5):
            for kc2 in range(4):
                wcast(wsq[m][l, kc2 * 256:(kc2 + 1) * 256, :].rearrange("(k p) n -> p k n", p=128),
                      wsq_bf[l][m][:, 2 * kc2:2 * kc2 + 2, :], 2048, "wq%d_%d_%d" % (l, m, kc2))

    if STOP <= 2:
        P.finish(); return nc, P, es
    wgi = [0]

    def load_wg(src, rkeys):
        i = wgi[0] % 2; wgi[0] += 1
        P.dma(wg[i][:], src, rkeys, [K(wg[i])])
        return wg[i], K(wg[i])

    def rmsnorm_to_bf(src, skey, n, gcol, dst, dkey):
        stp_, stk = nextpg()
        for c in range(8):
            q = sqb[c % 2]
            act(q[:, :n], src[:, c, :n], AF.Square, [skey], [K(q)])
            mm(stp_[:, :n], oneb[:], q[:, :n], c == 0, c == 7, [K(q), "oneb"], [stk])
        act(lnv[:, :n], stp_[:, :n], AF.Ln, [stk, "epst"], ["lnv"], bias=epst[:, 0:1], scale=1.0 / 1024)
        act(rstd[:, :n], lnv[:, :n], AF.Exp, ["lnv"], ["rstd"], scale=-0.5)
        for c in range(8):
            stt(dst[:, c, :n], src[:, c, :n], prmt[:, gcol + c:gcol + c + 1], rstd[:, :n], ALU.mult, ALU.mult,
                [skey, "prmt", "rstd"], [dkey])

    def proj_fm(wt, wk, col, rhsT, rkey, n):
        p_, pk = nextpg()
        for kc in range(8):
            mm(p_[:, :n], wt[:, kc, col:col + 128], rhsT[:, kc, :n], kc == 0, kc == 7, [wk, rkey], [pk])
        return p_, pk

    def linear_norm_res(m, l, rhsT, rkey, gcol):
        stp_ = pc[0]; stk = "pc0"
        for g2 in range(2):
            wt, wk = load_wg(wsq_bf[l][m][:, :, g2 * 512:(g2 + 1) * 512], WQK(l, m))
            for oc in range(4):
                c = g2 * 4 + oc
                p_, pk = proj_fm(wt, wk, oc * 128, rhsT, rkey, NT)
                cpy('act', yT[:, c, :], p_[:, :NT], [pk], ["yT"])
                q = sqb[c % 2]
                act(q[:], p_[:, :NT], AF.Square, [pk], [K(q)])
                mm(stp_[:, :NT], oneb[:], q[:], c == 0, c == 7, [K(q), "oneb"], [stk])
        act(lnv[:], stp_[:, :NT], AF.Ln, [stk, "epst"], ["lnv"], bias=epst[:, 0:1], scale=1.0 / 1024)
        act(rstd[:], lnv[:], AF.Exp, ["lnv"], ["rstd"], scale=-0.5)
        for c in range(8):
            stt(t1[:], yT[:, c, :], prmt[:, gcol + c:gcol + c + 1], rstd[:], ALU.mult, ALU.mult,
                ["yT", "prmt", "rstd"], ["t1"])
            tt('pool', xT[:, c, :], xT[:, c, :], t1[:], ALU.add, ["xT", "t1"], ["xT"])

    def load_tokmajor_T(src_rows, nblk, dst, dkey):
        for blk in range(nblk):
            xi = xin[blk % 2]
            P.dma(xi[:], src_rows(blk), (), [K(xi)])
            for hf in range(2):
                p_, pk = nextpg()
                for c4 in range(4):
                    c = hf * 4 + c4
                    tp(p_[:, c4 * 128:(c4 + 1) * 128], xi[:, c * 128:(c + 1) * 128], idf, [K(xi), "cstt"], [pk])
                cpy('act' if hf else 'dve', dst[:, hf * 4:hf * 4 + 4, blk * 128:(blk + 1) * 128],
                    p_[:, :].rearrange("p (c t) -> p c t", c=4), [pk], [dkey])

    def sb_stages(lanes, nk, first, last, par, pre=None):
        def s_z():
            if pre is not None:
                pre()
            for ln in lanes:
                a = ln['idx']
                for (lh, rh, c0, c1, rk) in ln['z']:
                    mm(pz[a][:nk, c0:c1], lh, rh, True, True, rk, ["pz%d" % a])

        def s_exp():
            for ln in lanes:
                a = ln['idx']; W = ln['W']; e_ = e_sb[a][par]
                act(e_[:nk, :W], pz[a][:nk, :W], AF.Exp, ["pz%d" % a], [K(e_)])

        def s_mask():
            for ln in lanes:
                a = ln['idx']; e_ = e_sb[a][par]
                for (c0, c1, mk, bc) in ln['masks']:
                    if mk is None:
                        mset('pool', e_[:nk, c0:c1], 0.0, [K(e_)])
                    elif bc:
                        tt('pool', e_[:nk, c0:c1].rearrange("p (h t) -> p h t", h=bc),
                           e_[:nk, c0:c1].rearrange("p (h t) -> p h t", h=bc), mk, ALU.mult, [K(e_), "cstt"], [K(e_)])
                    else:
                        tt('pool', e_[:nk, c0:c1], e_[:nk, c0:c1], mk, ALU.mult, [K(e_), "cstt"], [K(e_)])

        def s_ln():
            for ln in lanes:
                a = ln['idx']; W = ln['W']; e_, sp_ = e_sb[a][par], sp_sb[a][par]
                act(sp_[:nk, :W], e_[:nk, :W], AF.Ln, [K(e_)], [K(sp_)], bias=1.0)

        def s_cs():
            for ln in lanes:
                a = ln['idx']; W = ln['W']; sp_ = sp_sb[a][par]
                mm(pc[a][:nk, :W], trib[:nk, :nk], sp_[:nk, :W], True, first, [K(sp_), "trib"], ["pc%d" % a])
                if not first:
                    mm(pc[a][:nk, :W], oneb[:, :nk], Rbf[a][:, :W], False, True, [K(Rbf[a]), "oneb"], ["pc%d" % a])

        def s_exp2():
            for ln in lanes:
                a = ln['idx']; W = ln['W']
                act(pc[a][:nk, :W], pc[a][:nk, :W], AF.Exp, ["pc%d" % a], ["pc%d" % a], scale=-1.0)

        def s_A():
            for ln in lanes:
                a = ln['idx']; W = ln['W']; e_, A_ = e_sb[a][par], A_sb[a][par]
                tt('dve', A_[:nk, :W], pc[a][:nk, :W], e_[:nk, :W], ALU.mult, ["pc%d" % a, K(e_)], [K(A_)])

        def s_R():
            if last:
                return
            for ln in lanes:
                a = ln['idx']; W = ln['W']; sp_ = sp_sb[a][par]
                if first:
                    cpy('dve', Rbf[a][:nk, :W], sp_[:nk, :W], [K(sp_)], [K(Rbf[a])])
                    cpy('dve', Rsum[a][:nk, :W], sp_[:nk, :W], [K(sp_)], [K(Rsum[a])])
                else:
                    tt('dve', Rbf[a][:nk, :W], Rsum[a][:nk, :W], sp_[:nk, :W], ALU.add, [K(Rsum[a]), K(sp_)], [K(Rbf[a])])
                    tt('dve', Rsum[a][:nk, :W], Rsum[a][:nk, :W], sp_[:nk, :W], ALU.add, [K(Rsum[a]), K(sp_)], [K(Rsum[a])])

        def s_pv():
            for ln in lanes:
                a = ln['idx']; A_ = A_sb[a][par]
                for (lh, c0, c1, oap, rk, ok_) in ln['pv']:
                    mm(oap, lh, A_[:nk, c0:c1], first, last, rk + [K(A_)], [ok_])

        return [s_z, s_exp, s_mask, s_ln, s_cs, s_R, s_exp2, s_A, s_pv]

    SKEW = int(os.environ.get('KDBG_SKEW', 3))

    def run_steps(step_list, extras=None):
        n = len(step_list)
        ns = 9
        T = (n - 1) * SKEW + ns
        extras = list(extras) if extras else []
        ne = len(extras)
        for slot in range(T):
            for si in range(n):
                st = slot - si * SKEW
                if 0 <= st < ns:
                    step_list[si][st]()
            if ne:
                lo = T // 2
                if slot >= lo:
                    want = ((slot - lo + 1) * ne) // (T - lo)
                    while len(extras) > ne - want:
                        extras.pop(0)()
        while extras:
            extras.pop(0)()

    NLAY = int(os.environ.get('KDBG_LAYERS', DEPTH))
    for l in range(NLAY):
        pb = l * PRM_L
        load_tokmajor_T(lambda blk: memp[blk * 128:(blk + 1) * 128, :], 2, mT, "xT")
        if STOP <= 3:
            P.dma(xs[:, :, 0:256], xT[:], ["xT"], ()); P.finish(); return nc, P, es
        rmsnorm_to_bf(mT, "xT", 256, pb + P_NMEM, mhT, "hT")
        if STOP <= 4:
            P.dma(win_bf[0][:, :, 0:256], hT[:], ["hT"], ()); P.finish(); return nc, P, es
        for which, m in (("k", 3), ("v", 4)):
            if STOP == 5 and which == "v":
                break
            for g2 in range(2):
                wt, wk = load_wg(wsq_bf[l][m][:, :, g2 * 512:(g2 + 1) * 512], WQK(l, m))
                for mb in range(2):
                    p_, pk = nextpg()
                    for kc in range(8):
                        mm(p_[:, :512], mhT[:, kc, mb * 128:(mb + 1) * 128], wt[:, kc, :], kc == 0, kc == 7, [wk, "hT"], [pk])
                    cpy('act', ystage[:, :512], p_[:, :512], [pk], [K(xin[1])])
                    dst = (o_pmk if which == "k" else o_pmv)[l, mb * 128:(mb + 1) * 128, g2 * 512:(g2 + 1) * 512]
                    P.dma(dst, ystage[:, :512], [K(xin[1])], ())
                    if which == "v":
                        cpy('pool', memV[:, mb, g2 * 512:(g2 + 1) * 512], ystage[:, :512], [K(xin[1])], ["memV"])
                if which == "k" and STOP != 6:
                    for oc in range(4):
                        p_, pk = proj_fm(wt, wk, oc * 128, mhT, "hT", 256)
                        cpy('dve', memKT[:, g2 * 4 + oc, :], p_[:, :256], [pk], ["memKT"])

        if l == 1:
            mset('pool', Srun[:], 0.0, ["Srun"]); mset('pool', Tprev[:], 0.0, ["Tprev"])

        for R in [int(x) for x in os.environ['KDBG_ROUNDS'].split(',') if x != ''] if 'KDBG_ROUNDS' in os.environ else range(NR):
            samp = (R == NRP)
            t0 = R * NT
            NBK = 4 if samp else 2
            L = 64 if samp else 128
            E = Es if samp else Ep
            Ek = K(E)
            if l == 0:
                src = x_s if samp else x_p
                tb = 0 if samp else t0
                load_tokmajor_T(lambda blk: src[tb + blk * 128: tb + (blk + 1) * 128, :], 2, xT, "xT")
            else:
                P.dma(xT[:], xs[:, :, t0:t0 + NT], ["xs%d" % R], ["xT"])
            rmsnorm_to_bf(xT, "xT", NT, pb + P_NPRE, hT, "hT")
            wt, wk = load_wg(win_bf[l][:, :, 512:1024], WIK(l))
            for h in range(4):
                p_, pk = proj_fm(wt, wk, h * 128, hT, "hT", NT)
                act(fgate[:, h, :], p_[:, :NT], AF.Sigmoid, [pk], ["fg%d" % h])
            wt, wk = load_wg(win_bf[l][:, :, 0:512], WIK(l))
            for h in range(4):
                p_, pk = proj_fm(wt, wk, h * 128, hT, "hT", NT)
                act(q_sb[:, h, :], p_[:, :NT], AF.Silu, [pk], ["q_sb"])
            wt, wk = load_wg(win_bf[l][:, :, 1536:2048], WIK(l))
            for h in range(4):
                p_, pk = proj_fm(wt, wk, h * 128, hT, "hT", NT)
                act(gate_a[:, h, :], p_[:, :NT], AF.Silu, [pk], ["gate_a"])
            wt, wk = load_wg(win_bf[l][:, :, 2048:2560], WIK(l))
            for p in range(2):
                p_, pk = proj_fm(wt, wk, p * 128, hT, "hT", NT)
                ts('dve', qTb[:, p, :], p_[:, :NT], 0.125, None, ALU.mult, None, [pk], ["qTb"])
                p_, pk = proj_fm(wt, wk, 256 + p * 128, hT, "hT", NT)
                cpy('act', kTl[:, p, :], p_[:, :NT], [pk], ["kTl"])
            for blk in range(2):
                p_, pk = nextpg()
                for kc in range(8):
                    mm(p_[:, :256], hT[:, kc, blk * 128:(blk + 1) * 128], wt[:, kc, 256:512], kc == 0, kc == 7, [wk, "hT"], [pk])
                cpy('act', ktok[:, blk, :], p_[:, :256], [pk], ["ktok"])
            okd = (o_sk if samp else o_pk); ovd = (o_sv if samp else o_pv); ot0 = 0 if samp else t0
            P.dma(okd[l, ot0:ot0 + NT, :].rearrange("(b q) d -> q b d", q=128), ktok[:], ["ktok"], ())
            wt, wk = load_wg(win_bf[l][:, :, 2560:3072], WIK(l))
            for blk in range(2):
                p_, pk = nextpg()
                for kc in range(8):
                    mm(p_[:, :256], hT[:, kc, blk * 128:(blk + 1) * 128], wt[:, kc, 0:256], kc == 0, kc == 7, [wk, "hT"], [pk])
                cpy('act', vtok[:, blk, :], p_[:, :256], [pk], ["vtok"])
                cpy('pool', vbl[:, blk, :], vtok[:, blk, :], ["vtok"], ["vbl"])
            P.dma(ovd[l, ot0:ot0 + NT, :].rearrange("(b q) d -> q b d", q=128), vtok[:], ["vtok"], ())
            if samp:
                for b in range(4):
                    p_, pk = nextpg()
                    for kc in range(8):
                        mm(p_[:64, :256], hT[:, kc, b * 64:(b + 1) * 64], wt[:, kc, 0:256], kc == 0, kc == 7, [wk, "hT"], [pk])
                    cpy('dve', vnew[:64, b, :], p_[:64, :256], [pk], ["vnew"])
            for p in range(2):
                p_, pk = proj_fm(wt, wk, 256 + p * 128, hT, "hT", NT)
                act(gate_b[:, p, :], p_[:, :NT], AF.Silu, [pk], ["gate_b"])
            if not samp:
                kl, kg = kv_loc[l][R], kv_g[l][R]
                P.dma(kl[0:256, :].rearrange("(p q) t -> q p t", q=128), kTl[:], ["kTl"], ["kvl_k"])
                P.dma(kl[256:512, :].rearrange("(b q) d -> q b d", q=128), vbl[:], ["vbl"], ["kvl_v"])
                P.allgather(kl.opt(), kg.opt(), ["kvl_k", "kvl_v"], ["kvg%d_%d" % (l, R)])
            wt, wk = load_wg(win_bf[l][:, :, 3072:3584], WIK(l))
            for cc in range(2):
                p_, pk = proj_fm(wt, wk, cc * 128, hT, "hT", NT)
                cpy('dve', E[:, cc, :, 15:15 + L], p_[:, :NT].rearrange("p (b t) -> p b t", b=NBK), [pk], [Ek])
                p_, pk = proj_fm(wt, wk, 256 + cc * 128, hT, "hT", NT)
                act(gate_c[:, cc, :], p_[:, :NT], AF.Silu, [pk], ["gate_c"])
            wt, wk = load_wg(win_bf[l][:, :, 1024:1536], WIK(l))
            for ch in range(4):
                p_, pk = nextpg()
                for kc in range(8):
                    mm(p_[:64, :512], hT[:, kc, ch * 64:(ch + 1) * 64], wt[:, kc, :], kc == 0, kc == 7, [wk, "hT"], [pk])
                cpy('act' if ch % 2 else 'dve', v_tok[:, ch, :], p_[:64, :512], [pk], ["v_tok"])

            if STOP == 10:
                P.finish(); return nc, P, es
            c3 = lambda t: t[:, :].rearrange("p (c t) -> p c t", c=4)
            QS = 128.0 ** -0.5
            for g0 in (0, 2):
                hl = (g0, g0 + 1)
                T = {h: [hs[n] for n in (("f", "lg", "kk", "G") if h % 2 == 0 else ("Gr", "Gl", "e1", "e2"))] for h in hl}
                TK = {h: [K(t) for t in T[h]] for h in hl}
                QT = {h: (qt2[h % 2], kt2[h % 2], ke2[h % 2]) for h in hl}
                for h in hl:
                    ts('dve', fgate[:, h, :], fgate[:, h, :], omlv[:, l, h:h + 1], lbv[:, l, h:h + 1], ALU.mult, ALU.add,
                       ["fg%d" % h, "omlv", "lbv"], ["fg%d" % h])
                for h in hl:
                    act(T[h][0][:], fgate[:, h, :], AF.Ln, ["fg%d" % h], [TK[h][0]])
                for h in hl:
                    P.op('dve', lambda e, G=T[h][1], lg=T[h][0]: e.tensor_tensor_scan(G[:], cstt[:, C_RST:C_RST + NT], lg[:], 0.0, ALU.mult, ALU.add),
                         [TK[h][0], "cstt"], [TK[h][1]])
                    ts('pool', fgate[:, h, :], fgate[:, h, :], -1.0, 1.0, ALU.mult, ALU.add, ["fg%d" % h, TK[h][0]], ["fg%d" % h])
                for h in hl:
                    G = T[h][1]
                    tt('pool', c3(T[h][2]), c3(G), G[:, 31::64].unsqueeze(2).to_broadcast([128, 4, 64]), ALU.subtract, [TK[h][1]], [TK[h][2]])
                    tt('pool', c3(T[h][0]), c3(G), G[:, 63::64].unsqueeze(2).to_broadcast([128, 4, 64]), ALU.subtract, [TK[h][1], TK[h][0]], [TK[h][0]])
                for h in hl:
                    act(T[h][3][:], T[h][2][:], AF.Exp, [TK[h][2]], [TK[h][3]])
                for h in hl:
                    stt(QT[h][0][:], q_sb[:, h, :], QS, T[h][3][:], ALU.mult, ALU.mult, ["q_sb", TK[h][3]], [K(QT[h][0])])
                for h in hl:
                    act(T[h][2][:], T[h][2][:], AF.Exp, [TK[h][2]], [TK[h][2]], scale=-1.0)
                for h in hl:
                    tt('pool', QT[h][1][:], fgate[:, h, :], T[h][2][:], ALU.mult, ["fg%d" % h, TK[h][2]], [K(QT[h][1])])
                for h in hl:
                    act(T[h][3][:], T[h][1][:], AF.Exp, [TK[h][1]], [TK[h][3]])
                for h in hl:
                    stt(qG[:, h, :], q_sb[:, h, :], QS, T[h][3][:], ALU.mult, ALU.mult, ["q_sb", TK[h][3]], ["qG%d" % h])
                    cpy('pool', Dt[:, :, h], T[h][3][:, 63::64], [TK[h][3]], ["Dt"])
                for h in hl:
                    act(T[h][0][:], T[h][0][:], AF.Exp, [TK[h][0]], [TK[h][0]], scale=-1.0)
                for h in hl:
                    tt('pool', QT[h][2][:], fgate[:, h, :], T[h][0][:], ALU.mult, ["fg%d" % h, TK[h][0]], [K(QT[h][2])])
                pks = {}
                for h in hl:
                    p_, pk = nextpg(); pks[h] = (p_, pk)
                    for ch in range(4):
                        cs_ = slice(ch * 64, (ch + 1) * 64)
                        mm(p_[:64, cs_], QT[h][1][:, cs_], QT[h][0][:, cs_], True, True, [K(QT[h][1]), K(QT[h][0])], [pk])
                for h in hl:
                    p_, pk = pks[h]
                    tt('dve', attT[:, h, :].rearrange("p (c t) -> p c t", c=4), p_[:64, :NT].rearrange("p (c t) -> p c t", c=4),
                       cstt[:64, C_HM:C_HM + 64].unsqueeze(1).to_broadcast([64, 4, 64]), ALU.mult, [pk, "cstt"], ["attT%d" % h])
                for h in hl:
                    p_, pk = nextpg()
                    pb16 = p_[:, :].bitcast(BF16)
                    for ch in range(4):
                        tp(pb16[:64, ch * 128:(ch + 1) * 128], QT[h][2][:, ch * 64:(ch + 1) * 64], idb[:], [K(QT[h][2]), "idb"], [pk])
                    cpy('act', ketok2[h % 2][:].rearrange("p c k -> p (c k)"), pb16[:64, :512], [pk], [K(ketok2[h % 2])])
                for h in hl:
                    p_, pk = nextpg()
                    for ch in range(4):
                        mm(p_[:, ch * 128:(ch + 1) * 128], ketok2[h % 2][:, ch, :], v_tok[:, ch, h * 128:(h + 1) * 128], True, True,
                           [K(ketok2[h % 2]), "v_tok"], [pk])
                    cpy('act', Sst[:, :, h * 128:(h + 1) * 128], p_[:, :].rearrange("p (c v) -> p c v", c=4), [pk], ["Sst"])

            if STOP == 11:
                P.finish(); return nc, P, es
            if not samp:
                sl, sg_ = s_loc[l][R], s_g[l][R]
                P.dma(sl[0:512, :].rearrange("(c k) n -> k c n", k=128), Sst[:], ["Sst"], ["sl_s"])
                tlc, tgc = t_loc[l][R], t_g[l][R]
                P.dma(tlc[0:4, :].rearrange("r (a b) -> (r a) b", b=16), Dt[:].rearrange("p c h -> p (c h)"), ["Dt"], ["sl_d"])
                tl_ = tlc[4:19, :].rearrange("r c -> (r c)").rearrange("(p b) -> p b", b=60)
                for cc in range(2):
                    P.dma(tl_[:, cc * 30:(cc + 1) * 30].rearrange("p (b t) -> p b t", b=2),
                          E[:, cc, :, 128:143], [Ek], ["sl_t%d" % cc], slow=True)
                P.allgather(sl.opt(), sg_.opt(), ["sl_s"], ["sg%d_%d" % (l, R)])
                P.allgather(tlc.opt(), tgc.opt(), ["sl_d", "sl_t0", "sl_t1", "sl_pad%d_%d" % (l, R)], ["tg%d_%d" % (l, R)])

            if STOP == 12:
                P.finish(); return nc, P, es
            def emit_pooling():
                if not samp:
                    T4 = lambda r: Tg[:, r, :].rearrange("p (c b t) -> p c b t", c=2, b=2)
                    sel = lambda i: cstt[:, C_SEL + i:C_SEL + i + 1]
                    hal = E[:, :, :, 0:15]
                    ts('dve', hal, T4(0), sel(0), None, ALU.mult, None, ["Tg", "cstt"], [Ek])
                    stt(hal, T4(1), sel(1), hal, ALU.mult, ALU.add, ["Tg", "cstt", Ek], [Ek])
                    stt(hal, T4(2), sel(2), hal, ALU.mult, ALU.add, ["Tg", "cstt", Ek], [Ek])
                    stt(E[:, :, 1, 0:15], T4(3)[:, :, 0, :], sel(3), E[:, :, 1, 0:15], ALU.mult, ALU.add, ["Tg", "cstt", Ek], [Ek])
                    stt(E[:, :, 0, 0:15], Tprev[:], sel(3), E[:, :, 0, 0:15], ALU.mult, ALU.add, ["Tprev", "cstt", Ek], [Ek])
                    cpy('pool', Tprev[:], T4(3)[:, :, 1, :], ["Tg"], ["Tprev"])
                    if R == NRP - 1:
                        for cc in range(2):
                            P.dma(o_pps[l][:, cc * 128:(cc + 1) * 128].rearrange("t p -> p t"), E[:, cc, 1, 128:143], [Ek], (), slow=True)
                else:
                    for b in range(4):
                        for cc in range(2):
                            P.dma(E[:, cc, b, 0:15], stp[l, b][:, cc * 128:(cc + 1) * 128].rearrange("t p -> p t"), (), [Ek], slow=True)
                            P.dma(o_sps[l, b][:, cc * 128:(cc + 1) * 128].rearrange("t p -> p t"), E[:, cc, b, 64:79], [Ek], (), slow=True)
                LE = L + 15
                a2, a4, a8, a16 = (pa[i][:, :2 * NBK * (L + 14)].rearrange("p (c b t) -> p c b t", c=2, b=NBK) for i in range(4))
                tt('pool', a2[:, :, :, 0:LE - 1], E[:, :, :, 1:LE], E[:, :, :, 0:LE - 1], ALU.add, [Ek], ["pa0"])
                tt('pool', a4[:, :, :, 0:LE - 3], a2[:, :, :, 2:LE - 1], a2[:, :, :, 0:LE - 3], ALU.add, ["pa0"], ["pa1"])
                tt('pool', a8[:, :, :, 0:LE - 7], a4[:, :, :, 4:LE - 3], a4[:, :, :, 0:LE - 7], ALU.add, ["pa1"], ["pa2"])
                tt('pool', a16[:, :, :, 0:LE - 15], a8[:, :, :, 8:LE - 7], a8[:, :, :, 0:LE - 15], ALU.add, ["pa2"], ["pa3"])
                grp = [(0, 0, a2, 14, 2, "pa0"), (0, 64, a4, 12, 4, "pa1"), (1, 0, a8, 8, 8, "pa2"), (1, 64, a16, 0, 16, "pa3")]
                for (cc, p0, aw, off, w, ak) in grp:
                    stt(dT[p0:p0 + 64, cc, :].rearrange("p (b t) -> p b t", b=NBK), aw[p0:p0 + 64, cc, :, off:off + L], 1.0 / w,
                        E[p0:p0 + 64, cc, :, 15:15 + L], ALU.mult, ALU.subtract, [ak, Ek], ["dT"])
                    if (not samp) and R == 0:
                        tt('pool', ptmp[p0:p0 + 64, :], aw[p0:p0 + 64, cc, 0, off:off + 128],
                           cstt[p0:p0 + 64, C_INV + cc * 128:C_INV + (cc + 1) * 128], ALU.mult, [ak, "cstt"], ["ptmp"])
                        tt('pool', dT[p0:p0 + 64, cc, 0:128], ptmp[p0:p0 + 64, :], E[p0:p0 + 64, cc, 0, 15:143], ALU.subtract,
                           ["ptmp", Ek, "dT"], ["dT"])
                for cc in range(2):
                    p_, pk = nextpg()
                    for gg in range(2):
                        mm(p_[64 * gg:64 * gg + 64, :NT], wpb[64 * gg:64 * gg + 64, l, cc, :], dT[64 * gg:64 * gg + 64, cc, :], True, True,
                           ["wpb", "dT"], [pk])
                    stt(mixT[:, 6 + cc, :], p_[:, :NT], prmt[:, pb + P_PS + cc:pb + P_PS + cc + 1], gate_c[:, cc, :], ALU.mult, ALU.mult,
                        [pk, "prmt", "gate_c"], ["mixT"])

            prefix_items = []
            if not samp:
                sgk = "sg%d_%d" % (l, R); tgk = "tg%d_%d" % (l, R)

                def _pf_loads(l=l, R=R, tgk=tgk):
                    for r in range(4):
                        P.dma(Dg[:, r, :], t_g[l][R][r * TROWS:r * TROWS + 4, :].rearrange("r (a b) -> (r a) b", b=16),
                              [tgk], ["Dg"])
                        P.dma(Tg[:, r, :], t_g[l][R][r * TROWS + 4:r * TROWS + 19, :].rearrange("r c -> (r c)").rearrange("(p b) -> p b", b=60),
                              [tgk], ["Tg"])
                prefix_items.append(_pf_loads)
                mi = 0
                for rr in range(2):
                    for r in range(4):
                        for c in range(2):
                            def _pf_chunk(l=l, R=R, rr=rr, r=r, c=c, mi=mi, sgk=sgk):
                                ch = 2 * rr + c
                                smt = Sm[mi % 2]
                                P.dma(smt[:], s_g[l][R][r * SROWS + ch * 128: r * SROWS + (ch + 1) * 128, :], [sgk], [K(smt)])
                                oh = cstt[:, C_SEL + 4 + r:C_SEL + 5 + r]
                                if r == 0:
                                    ts('dve', Ssel[c][:], Srun[:], oh, None, ALU.mult, None, ["Srun", "cstt"], [K(Ssel[c])])
                                else:
                                    stt(Ssel[c][:], Srun[:], oh, Ssel[c][:], ALU.mult, ALU.add, ["Srun", "cstt", K(Ssel[c])], [K(Ssel[c])])
                                sv = Srun[:, :].rearrange("p (h v) -> p h v", h=4)
                                tt('dve', sv, sv, Dg[:, r, ch * 4:(ch + 1) * 4].unsqueeze(2).to_broadcast([128, 4, 128]), ALU.mult,
                                   ["Srun", "Dg"], ["Srun"])
                                tt('dve', Srun[:], Srun[:], smt[:], ALU.add, ["Srun", K(smt)], ["Srun"])
                                if r == 3:
                                    cpy('act', Sbf[:, ch, :], Ssel[c][:], [K(Ssel[c])], ["Sbf"])
                            prefix_items.append(_pf_chunk)
                            mi += 1
                if R == NRP - 1:
                    prefix_items.append(lambda l=l: P.dma(o_phs[l].rearrange("h k v -> k h v"), Srun[:, :].rearrange("p (h v) -> p h v", h=4), ["Srun"], ()))
                prefix_items.append(emit_pooling)
            if STOP == 14:
                P.finish(); return nc, P, es
            if not samp:
                for p in range(2):
                    groups = list(range(2 * R + 1, -1, -1))
                    steps = [(g, r) for g in groups for r in (3, 2, 1, 0)]
                    sl_ = []
                    for si, (g, r) in enumerate(steps):
                        kt_, vt_ = kTg[(si // 4) % 2], vg[(si // 4) % 2]
                        pre = None
                        if r == 3:
                            Rg = g // 2; gl = g % 2
                            gk = "kvg%d_%d" % (l, Rg)
                            src = kv_g[l][Rg]

                            def pre(kt_=kt_, vt_=vt_, src=src, gl=gl, gk=gk, p=p):
                                P.dma(kt_[:], src.rearrange("(r x) t -> x r t", r=4)[p * 128:(p + 1) * 128, :, gl * 128:(gl + 1) * 128],
                                      [gk], [K(kt_)])
                                P.dma(vt_[:], src.rearrange("(r x) d -> x r d", r=4)[256 + gl * 128:256 + (gl + 1) * 128, :, p * 128:(p + 1) * 128],
                                      [gk], [K(vt_)])
                        lanes = []
                        for hh in range(2):
                            hs_ = slice(64 * hh, 64 * hh + 64)
                            masks = []
                            if g >= 2 * R:
                                rr = g - 2 * R
                                if rr == 1:
                                    masks.append((0, 128, None, 0))
                                masks.append((rr * 128, rr * 128 + 128, cstt[:, C_SBM + r * 128:C_SBM + (r + 1) * 128], 0))
                            lanes.append(dict(idx=hh, W=NT,
                                              z=[(kt_[hs_, r, :], qTb[hs_, p, :], 0, NT, [K(kt_), "qTb"])],
                                              masks=masks,
                                              pv=[(vt_[:, r, hs_], 0, NT, po[hs_, :NT], [K(vt_)], "po")]))
                        sl_.append(sb_stages(lanes, 128, si == 0, si == len(steps) - 1, si % 2, pre))
                    if p == 1:
                        run_steps(sl_, prefix_items); prefix_items = []
                    else:
                        run_steps(sl_)
                    tt('dve', mixT[:, 4 + p, :], po[:, :NT], gate_b[:, p, :], ALU.mult, ["po", "gate_b"], ["mixT"])
            else:
                for p in range(2):
                    cpy('pool', knew[:, p, :, 0:64], kTl[:, p, :].rearrange("x (b t) -> x b t", b=4), ["kTl"], ["knew"])
                for h in range(4):
                    hs_ = slice(64 * (h % 2), 64 * (h % 2) + 64)
                    cpy('pool', qm[hs_, h, :], qTb[hs_, h // 2, :], ["qTb"], ["qm"])
                for b in range(4):
                    kst, vst = stg[0], stg[1]
                    P.dma(kst[:, :].rearrange("q (n d) -> q n d", n=8), csk[l, b].rearrange("(n q) d -> q n d", q=128), (), [K(kst)])
                    P.dma(vst[:, :].rearrange("q (n d) -> q n d", n=8), csv[l, b].rearrange("(n q) d -> q n d", q=128), (), [K(vst)])
                    kTp = bstg[0][:, :].rearrange("x (p t) -> x p t", p=2)
                    vp = bstg[1][:, :].rearrange("q (n d) -> q n d", n=8)
                    cpy('pool', bstg[1][:], vst[:], [K(vst)], [K(bstg[1])])
                    for p in range(2):
                        for n4 in range(2):
                            p_, pk = nextpg()
                            for n_ in range(4):
                                n = n4 * 4 + n_
                                tp(p_[:, n_ * 128:(n_ + 1) * 128], kst[:, n * 256 + p * 128: n * 256 + (p + 1) * 128], idf,
                                   [K(kst), "cstt"], [pk])
                            cpy('act', kTp[:, p, n4 * 512:(n4 + 1) * 512], p_[:, :], [pk], [K(bstg[0])])
                    if STOP == 150:
                        P.finish(); return nc, P, es
                    bs_ = slice(b * 64, b * 64 + 64)
                    steps = [8] + list(range(7, -1, -1))
                    sl_ = []
                    for si, n in enumerate(steps):
                        nk = 128
                        z = []; pv = []
                        for h in range(4):
                            hh, p = h % 2, h // 2
                            hs_ = slice(64 * hh, 64 * hh + 64)
                            if n == 8:
                                z.append((knew[:, p, b, :], qm[:, h, bs_], h * 64, h * 64 + 64, ["knew", "qm"]))
                                pv.append((vnew[:, b, h * 64:(h + 1) * 64], h * 64, h * 64 + 64,
                                           (po, pc[1])[p][hs_, 0:64], ["vnew"], ("po", "pc1")[p]))
                            else:
                                z.append((kTp[:, p, n * 128:(n + 1) * 128], qm[:, h, bs_], h * 64, h * 64 + 64, [K(bstg[0]), "qm"]))
                                pv.append((vp[:, n, h * 64:(h + 1) * 64], h * 64, h * 64 + 64, (po, pc[1])[p][hs_, 0:64], [K(bstg[1])], ("po", "pc1")[p]))
                        masks = [(0, 256, cstt[:, C_SM:C_SM + 64].unsqueeze(1).to_broadcast([128, 4, 64]), 4)] if n == 8 else []
                        sl_.append(sb_stages([dict(idx=0, W=256, z=z, masks=masks, pv=pv)], nk, si == 0, si == len(steps) - 1, si % 2))
                    run_steps(sl_)
                    for p in range(2):
                        tt('dve', mixT[:, 4 + p, bs_], (po, pc[1])[p][:, 0:64], gate_b[:, p, bs_], ALU.mult, [("po", "pc1")[p], "gate_b"], ["mixT"])

            if not samp:
                for it_ in prefix_items:
                    it_()
                prefix_items = []
            else:
                for b in range(4):
                    P.dma(Sin_s[:, :].rearrange("p (h v) -> p h v", h=4), sth[l, b].rearrange("h k v -> k h v"), (), [K(Ssel[0])])
                    cpy('act', Sbf[:, b, :], Sin_s[:], [K(Ssel[0])], ["Sbf"])
                    sv = Sin_s[:, :].rearrange("p (h v) -> p h v", h=4)
                    tt('pool', sv, sv, Dt[:, b, :].unsqueeze(2).to_broadcast([128, 4, 128]), ALU.mult, [K(Ssel[0]), "Dt"], [K(Ssel[0])])
                    tt('pool', Sin_s[:], Sin_s[:], Sst[:, b, :], ALU.add, [K(Ssel[0]), "Sst"], [K(Ssel[0])])
                    P.dma(o_shs[l, b].rearrange("h k v -> k h v"), Sin_s[:, :].rearrange("p (h v) -> p h v", h=4), [K(Ssel[0])], ())
            for h in range(4):
                p_, pk = nextpg()
                for ch in range(4):
                    cs_ = slice(ch * 64, (ch + 1) * 64)
                    mm(p_[:, cs_], v_tok[:, ch, h * 128:(h + 1) * 128], attT[:, h, cs_], True, False, ["v_tok", "attT%d" % h], [pk])
                    mm(p_[:, cs_], Sbf[:, ch, h * 128:(h + 1) * 128], qG[:, h, cs_], False, True, ["Sbf", "qG%d" % h], [pk])
                q = sqb[h % 2]
                act(q[:], p_[:, :NT], AF.Square, [pk], [K(q)])
                s_, sk_ = nextpg()
                mm(s_[:, :NT], oneb[:], q[:], True, True, [K(q), "oneb"], [sk_])
                act(lnv[:], s_[:, :NT], AF.Ln, [sk_, "epst"], ["lnv"], bias=epst[:, 0:1], scale=1.0 / 128)
                act(rstd[:], lnv[:], AF.Exp, ["lnv"], ["rstd"], scale=-0.5)
                stt(t1[:], p_[:, :NT], prmt[:, pb + P_OG + h:pb + P_OG + h + 1], rstd[:], ALU.mult, ALU.mult,
                    [pk, "prmt", "rstd"], ["t1"])
                tt('pool', mixT[:, h, :], t1[:], gate_a[:, h, :], ALU.mult, ["t1", "gate_a"], ["mixT"])

            if STOP == 13:
                P.finish(); return nc, P, es
            if samp:
                emit_pooling()
            if STOP == 15:
                P.finish(); return nc, P, es
            linear_norm_res(0, l, mixT, "mixT", pb + P_NPOST)
            rmsnorm_to_bf(xT, "xT", NT, pb + P_XPRE, hT, "hT")
            for g2 in range(2):
                wt, wk = load_wg(wsq_bf[l][1][:, :, g2 * 512:(g2 + 1) * 512], WQK(l, 1))
                for oc in range(4):
                    p_, pk = proj_fm(wt, wk, oc * 128, hT, "hT", NT)
                    ts('dve', qxT[:, g2 * 4 + oc, :], p_[:, :NT], 1.0 / 16, None, ALU.mult, None, [pk], ["qxT"])
            TS = 64 if samp else 128
            for st_ in range(NT // TS):
                tsl = slice(st_ * TS, (st_ + 1) * TS)
                if samp:
                    kst, vst = stg[0], stg[1]
                    P.dma(kst[:, :].rearrange("q (n d) -> q n d", n=2), cmk[l, st_].rearrange("(n q) d -> q n d", q=128), (), [K(kst)])
                    P.dma(vst[:, :].rearrange("q (n d) -> q n d", n=2), cmv[l, st_].rearrange("(n q) d -> q n d", q=128), (), [K(vst)])
                    cpy('pool', memV[:].rearrange("q n d -> q (n d)"), vst[:], [K(vst)], ["memV"])
                    for mb in range(2):
                        for hf in range(2):
                            p_, pk = nextpg()
                            for c4 in range(4):
                                c = hf * 4 + c4
                                tp(p_[:, c4 * 128:(c4 + 1) * 128], kst[:, mb * 1024 + c * 128: mb * 1024 + (c + 1) * 128], idf,
                                   [K(kst), "cstt"], [pk])
                            cpy('act', memKT[:, hf * 4:hf * 4 + 4, mb * 128:(mb + 1) * 128],
                                p_[:, :].rearrange("p (c t) -> p c t", c=4), [pk], ["memKT"])
                pt_, ptk = pz[0], "pz0"
                ptb16 = pt_[:, :].bitcast(BF16)
                for hx in range(4):
                    p_, pk = nextpg()
                    mm(p_[:TS, :256], qxT[:, 2 * hx, tsl], memKT[:, 2 * hx, :], True, False, ["qxT", "memKT"], [pk])
                    mm(p_[:TS, :256], qxT[:, 2 * hx + 1, tsl], memKT[:, 2 * hx + 1, :], False, True, ["qxT", "memKT"], [pk])
                    P.op('dve', lambda e, p_=p_, TS=TS: e.reduce_max(sm4[:TS, 0:1], p_[:TS, :256], AX.X), [pk], ["sm4"])
                    ts('dve', sm4[:TS, 1:2], sm4[:TS, 0:1], -1.0, None, ALU.mult, None, ["sm4"], ["sm4"])
                    act(pexp[:TS, :], p_[:TS, :256], AF.Exp, [pk, "sm4"], ["pexp", "sm4"], bias=sm4[:TS, 1:2], accum=sm4[:TS, 2:3])
                    P.op('dve', lambda e, TS=TS: e.reciprocal(sm4[:TS, 3:4], sm4[:TS, 2:3]), ["sm4"], ["sm4"])
                    ts('dve', pbf[:TS, :], pexp[:TS, :], sm4[:TS, 3:4], None, ALU.mult, None, ["pexp", "sm4"], ["pbf"])
                    for mc in range(2):
                        j_ = hx * 2 + mc
                        tp(ptb16[:, j_ * 128:j_ * 128 + TS], pbf[:TS, mc * 128:(mc + 1) * 128], idb[:TS, :TS], ["pbf", "idb"], [ptk])
                cpy('act', pT_sb[:, :, :TS], ptb16[:, :1024].rearrange("p (j t) -> p j t", j=8)[:, :, :TS], [ptk], ["pT_sb"])
                for hf in range(2):
                    p_, pk = nextpg()
                    for o4 in range(4):
                        oc = hf * 4 + o4; hx = oc // 2
                        for mc in range(2):
                            mm(p_[:, o4 * TS:(o4 + 1) * TS], memV[:, mc, oc * 128:(oc + 1) * 128], pT_sb[:, 2 * hx + mc, :TS],
                               mc == 0, mc == 1, ["memV", "pT_sb"], [pk])
                    cpy('dve', oxT[:, hf * 4:hf * 4 + 4, tsl], p_[:, :4 * TS].rearrange("p (o t) -> p o t", o=4), [pk], ["oxT"])
            linear_norm_res(2, l, oxT, "oxT", pb + P_XPOST)
            if STOP == 16:
                P.finish(); return nc, P, es
            if l < NLAY - 1:
                P.dma(xs[:, :, t0:t0 + NT], xT[:], ["xT"], ["xs%d" % R])
            else:
                yd = y_s if samp else y_p
                for blk in range(2):
                    for hf in range(2):
                        p_, pk = nextpg()
                        for c4 in range(4):
                            c = hf * 4 + c4
                            tp(p_[:, c4 * 128:(c4 + 1) * 128], xT[:, c, blk * 128:(blk + 1) * 128], idf, ["xT", "cstt"], [pk])
                        cpy('act' if hf else 'dve', ystage[:, hf * 512:(hf + 1) * 512], p_[:, :], [pk], [K(xin[1])])
                    P.dma(yd[ot0 + blk * 128: ot0 + (blk + 1) * 128, :], ystage[:], [K(xin[1])], ())

    P.finish()
    return nc, P, es


def _prepend_mask_copy(P):
    pass


_CACHE = {}


def _consts(core):
    j = core % 4
    c = np.zeros((128, NCST), np.float32)
    c[:, C_ID:C_ID + 128] = np.eye(128, dtype=np.float32)
    jj, ss = np.meshgrid(np.arange(128), np.arange(128), indexing="ij")
    c[:, C_TRI:C_TRI + 128] = (jj >= ss).astype(np.float32)
    for r in range(4):
        if r < j:
            m = np.ones((128, 128), np.float32)
        elif r == j:
            m = (jj < ss).astype(np.float32)
        else:
            m = np.zeros((128, 128), np.float32)
        c[:, C_SBM + r * 128:C_SBM + (r + 1) * 128] = m
    k6, q6 = np.meshgrid(np.arange(64), np.arange(64), indexing="ij")
    c[:64, C_SM:C_SM + 64] = (k6 < q6).astype(np.float32)
    c[:64, C_HM:C_HM + 64] = (k6 <= q6).astype(np.float32)
    for r in range(3):
        c[:, C_SEL + r] = 1.0 if r == j - 1 else 0.0
    c[:, C_SEL + 3] = 1.0 if j == 0 else 0.0
    for r in range(4):
        c[:, C_SEL + 4 + r] = 1.0 if r == j else 0.0
    wins = {(0, 0): 2, (0, 1): 4, (1, 0): 8, (1, 1): 16}
    pos = np.arange(128)
    for cc in range(2):
        for gg in range(2):
            w = wins[(cc, gg)]
            inv = 1.0 / np.minimum(pos + 1, w) if j == 0 else np.full(128, 1.0 / w)
            c[64 * gg:64 * gg + 64, C_INV + cc * 128:C_INV + (cc + 1) * 128] = inv[None, :].astype(np.float32)
            c[64 * gg:64 * gg + 64, C_W + cc] = 1.0 / w
    rst = np.ones(NT, np.float32); rst[::64] = 0.0
    c[:, C_RST:C_RST + NT] = rst[None, :]
    return c


def kernel(x_prompt, x_sample, mem_prompt, cache_sb_k, cache_sb_v, state_hgrn, state_pool,
           cache_mem_k, cache_mem_v, norm_mix_pre, norm_mix_post, w_in, hgrn_lb_logits,
           hgrn_onorm_g, w_pool, pool_scale, w_out, norm_x_pre, norm_x_post, norm_mem,
           w_xq, w_xk, w_xv, w_xo):
    f = lambda a: np.ascontiguousarray(np.asarray(a, dtype=np.float32))
    x_prompt, x_sample, mem_prompt = f(x_prompt), f(x_sample), f(mem_prompt)
    if "nc" not in _CACHE:
        _CACHE["nc"] = build_program()
        _CACHE["nc"][1].build()
    nc = _CACHE["nc"][0]
    prm = np.zeros((128, 2 * PRM_L), np.float32)
    for l in range(2):
        b = l * PRM_L
        for off, arr in ((P_NPRE, norm_mix_pre), (P_NPOST, norm_mix_post), (P_XPRE, norm_x_pre),
                         (P_XPOST, norm_x_post), (P_NMEM, norm_mem)):
            prm[:, b + off:b + off + 8] = f(arr)[l].reshape(8, 128).T
        prm[:, b + P_OG:b + P_OG + 4] = f(hgrn_onorm_g)[l].reshape(4, 128).T
        prm[:, b + P_PS:b + P_PS + 2] = f(pool_scale)[l].reshape(2, 128).T
    lbl = np.concatenate([f(hgrn_lb_logits)[l].reshape(4, 128).T for l in range(2)], axis=1)
    wp = f(w_pool).reshape(2, 2, 2, 64, 64)
    wpl = np.ascontiguousarray(wp.transpose(2, 3, 0, 1, 4).reshape(128, 2 * 2 * 64))
    in_maps = []
    for c in range(8):
        b, j = c // 4, c % 4
        sl = slice(4 * c, 4 * c + 4)
        in_maps.append({
            "x_p": np.ascontiguousarray(x_prompt[b].reshape(64, 128, 1024)[j::4].reshape(TOKP, 1024)),
            "x_s": np.ascontiguousarray(x_sample[sl].reshape(NT, 1024)),
            "memp": mem_prompt[b],
            "csk": np.ascontiguousarray(f(cache_sb_k)[:, sl].reshape(2, 4, 1024, 256)),
            "csv": np.ascontiguousarray(f(cache_sb_v)[:, sl].reshape(2, 4, 1024, 256)),
            "sth": np.ascontiguousarray(f(state_hgrn)[:, sl]),
            "stp": np.ascontiguousarray(f(state_pool)[:, sl]),
            "cmk": np.ascontiguousarray(f(cache_mem_k)[:, sl].reshape(2, 4, 256, 1024)),
            "cmv": np.ascontiguousarray(f(cache_mem_v)[:, sl].reshape(2, 4, 256, 1024)),
            "w_in": f(w_in), "w_out": f(w_out), "w_xq": f(w_xq), "w_xo": f(w_xo), "w_xk": f(w_xk), "w_xv": f(w_xv),
            "prm": prm, "lbl": np.ascontiguousarray(lbl), "wpl": wpl, "cst": _consts(c),
        })
    _r = run_bass_kernel_spmd(nc, in_maps, core_ids=list(range(8)), **({'trace': True} if os.environ.get('KDBG_TRACE') else {}))
    if os.environ.get('KDBG_TRACE'):
        print('EXEC_NS', _r.exec_time_ns)
    res = _r.results
    yp = np.zeros((2, 64, 128, 1024), np.float32)
    pk = np.zeros((2, 2, 64, 128, 256), np.float32); pv = np.zeros_like(pk)
    for c in range(8):
        b, j = c // 4, c % 4
        yp[b, j::4] = res[c]["y_p"].reshape(16, 128, 1024)
        pk[:, b, j::4] = res[c]["o_pk"].reshape(2, 16, 128, 256)
        pv[:, b, j::4] = res[c]["o_pv"].reshape(2, 16, 128, 256)
    cat = lambda k, shp: np.concatenate([res[c][k].reshape(shp) for c in range(8)], axis=1)
    ys = np.concatenate([res[c]["y_s"].reshape(4, 64, 1024) for c in range(8)], axis=0)
    return (yp.reshape(2, 8192, 1024), ys,
            pk.reshape(2, 2, 8192, 4, 64), pv.reshape(2, 2, 8192, 4, 64),
            np.stack([res[0]["o_phs"], res[4]["o_phs"]], axis=1),
            np.stack([res[3]["o_pps"], res[7]["o_pps"]], axis=1),
            np.stack([res[0]["o_pmk"], res[4]["o_pmk"]], axis=1).reshape(2, 2, 256, 4, 256),
            np.stack([res[0]["o_pmv"], res[4]["o_pmv"]], axis=1).reshape(2, 2, 256, 4, 256),
            cat("o_sk", (2, 4, 64, 4, 64)), cat("o_sv", (2, 4, 64, 4, 64)),
            cat("o_shs", (2, 4, 4, 128, 128)), cat("o_sps", (2, 4, 15, 256)))
```

```python
import os
import numpy as np
from contextlib import ExitStack
import concourse.bass as bass
import concourse.mybir as mybir
from concourse.bass_utils import run_bass_kernel_spmd

F32 = mybir.dt.float32
BF16 = mybir.dt.bfloat16
AF = mybir.ActivationFunctionType
ALU = mybir.AluOpType
AX = mybir.AxisListType

DEPTH = 2
NT = 256
NRP = 8
NR = NRP + 1
TOKP = NRP * NT
TOK = TOKP + NT
EPS = 1e-6
SROWS = 512
TROWS = 32
NDMA = 24
SAME_ENGINE_NOSYNC = ('pe',)

PRM_L = 50
P_NPRE, P_NPOST, P_XPRE, P_XPOST, P_NMEM, P_LB, P_OG, P_PS = 0, 8, 16, 24, 32, 40, 44, 48
C_ID = 0
C_TRI = 128
C_SBM = 256
C_SM = 768
C_HM = 832
C_SEL = 896
C_INV = 904
C_RST = 1160
C_W = 1416
NCST = 1418


class Prog:
    ENG = ('pe', 'act', 'dve', 'pool', 'sp')

    def __init__(self, nc, es):
        self.nc, self.es = nc, es
        self.streams = {e: [] for e in self.ENG}
        self.count = {e: 0 for e in self.ENG}
        self.waited = {e: {} for e in self.ENG}
        self.bufs = {}
        self.sems = {}
        for e in ('pe', 'act', 'dve', 'pool'):
            self.sems[e] = es.enter_context(nc.semaphore('s_' + e))
        self.dma_cnt = [0] * NDMA
        self.dma_next = 0
        for i in range(NDMA):
            self.sems['dq%d' % i] = es.enter_context(nc.semaphore('dq%d' % i))
        self.ncc = 0

    def _deps(self, eng, reads, writes, extra=()):
        deps = list(extra)
        for b in reads:
            st = self.bufs.get(b)
            if st and st[0]:
                deps.append(st[0])
            if st and b[0] == 'p' and b[1] in 'gzco' and (b[2:3].isdigit() or b == 'po'):
                deps.extend(st[1].items())
        for b in writes:
            st = self.bufs.get(b)
            if st:
                if st[0]:
                    deps.append(st[0])
                deps.extend(st[1].items())
        w = self.waited[eng]
        for (sk, v) in deps:
            if sk == eng and eng in SAME_ENGINE_NOSYNC:
                continue
            if w.get(sk, 0) < v:
                w[sk] = v
                self.streams[eng].append(('wait', sk, v))

    def _update(self, tok, reads, writes):
        for b in reads:
            st = self.bufs.setdefault(b, [None, {}])
            if st[1].get(tok[0], 0) < tok[1]:
                st[1][tok[0]] = tok[1]
        for b in writes:
            self.bufs[b] = [tok, {}]

    def op(self, eng, fn, reads=(), writes=()):
        self._deps(eng, reads, writes)
        self.count[eng] += 1
        tok = (eng, self.count[eng])
        self.streams[eng].append(('op', fn))
        self._update(tok, reads, writes)

    def dma(self, out, in_, reads=(), writes=(), slow=False):
        i = self.dma_next
        self.dma_next = (i + 1) % NDMA
        sk = 'dq%d' % i
        prev = self.dma_cnt[i]
        self._deps('sp', reads, writes, extra=[(sk, prev)] if prev else [])
        self.dma_cnt[i] = prev + 16
        self.streams['sp'].append(('dma', out, in_, sk, slow))
        self._update((sk, prev + 16), reads, writes)

    def allgather(self, in_ap, out_ap, reads, writes):
        sk = 'cc%d' % self.ncc
        self.ncc += 1
        self.sems[sk] = self.es.enter_context(self.nc.semaphore(sk))
        self._deps('pool', reads, writes)
        self.streams['pool'].append(('cc', in_ap, out_ap, sk))
        self._update((sk, 1), reads, writes)

    def finish(self):
        for i in range(NDMA):
            if self.dma_cnt[i]:
                self.streams['sp'].append(('wait', 'dq%d' % i, self.dma_cnt[i]))

    def build(self):
        nc = self.nc
        engs = {'pe': 'tensor', 'act': 'scalar', 'dve': 'vector', 'pool': 'gpsimd', 'sp': 'sync'}
        with nc.Block() as block:
            for ename, bname in engs.items():
                stream = self.streams[ename]
                sems = self.sems

                def body(eng, stream=stream, ename=ename):
                    for it in stream:
                        if it[0] == 'wait':
                            eng.wait_ge(sems[it[1]], it[2])
                        elif it[0] == 'op':
                            it[1](eng).then_inc(sems[ename], 1)
                        elif it[0] == 'dma':
                            if it[4]:
                                with nc.allow_non_contiguous_dma(reason="small transposing dma"):
                                    eng.dma_start(out=it[1], in_=it[2]).then_inc(sems[it[3]], 16)
                            else:
                                eng.dma_start(out=it[1], in_=it[2]).then_inc(sems[it[3]], 16)
                        elif it[0] == 'cc':
                            eng.collective_compute(
                                "AllGather", ALU.bypass,
                                replica_groups=[[0, 1, 2, 3], [4, 5, 6, 7]],
                                ins=[it[1]], outs=[it[2]]).then_inc(sems[it[3]])
                getattr(block, bname)(body)


def build_program():
    nc = bass.Bass("TRN2", target_bir_lowering=False)
    es = ExitStack()
    P = Prog(nc, es)

    def din(name, shape, dt=F32):
        return nc.dram_tensor(name, list(shape), dt, kind="ExternalInput").ap()

    def dout(name, shape, dt=F32):
        return nc.dram_tensor(name, list(shape), dt, kind="ExternalOutput").ap()

    def dscr(name, shape, dt=F32):
        return nc.dram_tensor(name, list(shape), dt).ap()

    x_p = din("x_p", [TOKP, 1024]); x_s = din("x_s", [NT, 1024]); memp = din("memp", [256, 1024])
    csk = din("csk", [2, 4, 1024, 256]); csv = din("csv", [2, 4, 1024, 256])
    sth = din("sth", [2, 4, 4, 128, 128]); stp = din("stp", [2, 4, 15, 256])
    cmk = din("cmk", [2, 4, 256, 1024]); cmv = din("cmv", [2, 4, 256, 1024])
    w_in = din("w_in", [2, 1024, 3584])
    wsq = [din(n, [2, 1024, 1024]) for n in ("w_out", "w_xq", "w_xo", "w_xk", "w_xv")]
    prm = din("prm", [128, 2 * PRM_L]); lbl = din("lbl", [128, 8]); wpl = din("wpl", [128, 2 * 2 * 64])
    cst = din("cst", [128, NCST])

    y_p = dout("y_p", [TOKP, 1024]); y_s = dout("y_s", [NT, 1024])
    o_pk = dout("o_pk", [2, TOKP, 256]); o_pv = dout("o_pv", [2, TOKP, 256])
    o_phs = dout("o_phs", [2, 4, 128, 128]); o_pps = dout("o_pps", [2, 15, 256])
    o_pmk = dout("o_pmk", [2, 256, 1024]); o_pmv = dout("o_pmv", [2, 256, 1024])
    o_sk = dout("o_sk", [2, NT, 256]); o_sv = dout("o_sv", [2, NT, 256])
    o_shs = dout("o_shs", [2, 4, 4, 128, 128]); o_sps = dout("o_sps", [2, 4, 15, 256])

    win_bf = [dscr("win_bf%d" % l, [128, 8, 3584], BF16) for l in range(2)]
    wsq_bf = [[dscr("wsq_bf%d_%d" % (l, m), [128, 8, 1024], BF16) for m in range(5)] for l in range(2)]
    xs = dscr("xs", [128, 8, TOK])
    kv_loc = [[dscr("kvl%d_%d" % (l, r), [512, 256], BF16) for r in range(NRP)] for l in range(2)]
    kv_g = [[dscr("kvg%d_%d" % (l, r), [2048, 256], BF16) for r in range(NRP)] for l in range(2)]
    s_loc = [[dscr("sl%d_%d" % (l, r), [SROWS, 512]) for r in range(NRP)] for l in range(2)]
    s_g = [[dscr("sg%d_%d" % (l, r), [4 * SROWS, 512]) for r in range(NRP)] for l in range(2)]
    t_loc = [[dscr("tl%d_%d" % (l, r), [TROWS, 512]) for r in range(NRP)] for l in range(2)]
    t_g = [[dscr("tg%d_%d" % (l, r), [4 * TROWS, 512]) for r in range(NRP)] for l in range(2)]

    def sb(name, shape, dt=F32):
        return es.enter_context(nc.sbuf_tensor(name, list(shape), dt))

    def ps(name, shape=(128, 512), dt=F32):
        return es.enter_context(nc.psum_tensor(name, list(shape), dt))

    cstt = sb("cstt", [128, NCST]); prmt = sb("prmt", [128, 2 * PRM_L]); lbt = sb("lbt", [128, 8])
    lbv = sb("lbv", [128, 2, 4]); omlv = sb("omlv", [128, 2, 4]); lbtmp = sb("lbtmp", [128, 3, 4])
    wpt = sb("wpt", [128, 2, 2, 64]); wpb = sb("wpb", [128, 2, 2, 64], BF16)
    idb = sb("idb", [128, 128], BF16); trib = sb("trib", [128, 128], BF16); oneb = sb("oneb", [128, 128], BF16)
    epst = sb("epst", [128, 1])
    stg = [sb("stg%d" % i, [128, 2048]) for i in range(2)]
    bstg = [sb("bstg%d" % i, [128, 2048], BF16) for i in range(2)]
    wg = [sb("wg%d" % i, [128, 8, 512], BF16) for i in range(2)]
    xT = sb("xT", [128, 8, NT]); hT = sb("hT", [128, 8, NT], BF16)
    xin = [sb("xin%d" % i, [128, 1024]) for i in range(2)]
    sqb = [sb("sqb%d" % i, [128, NT], BF16) for i in range(2)]
    lnv = sb("lnv", [128, NT]); rstd = sb("rstd", [128, NT])
    q_sb = sb("q_sb", [128, 4, NT]); fgate = sb("fgate", [128, 4, NT]); gate_a = sb("gate_a", [128, 4, NT], BF16)
    v_tok = sb("v_tok", [64, 4, 512], BF16); qTb = sb("qTb", [128, 2, NT], BF16); kTl = sb("kTl", [128, 2, NT], BF16)
    vnew = sb("vnew", [128, 4, 256], BF16); knew = sb("knew", [128, 2, 4, 128], BF16); qm = sb("qm", [128, 4, NT], BF16); ktok = sb("ktok", [128, 2, 256]); vtok = sb("vtok", [128, 2, 256]); vbl = sb("vbl", [128, 2, 256], BF16)
    gate_b = sb("gate_b", [128, 2, NT], BF16); gate_c = sb("gate_c", [128, 2, NT], BF16)
    Ep = sb("Ep", [128, 2, 2, 143]); Es = sb("Es", [128, 2, 4, 79])
    Sst = sb("Sst", [128, 4, 512])
    hs = {n: sb("hs_" + n, [128, NT]) for n in ("f", "lg", "kk", "G", "Gr", "Gl", "e1", "e2")}
    qt2 = [sb("qt%d" % i, [128, NT], BF16) for i in range(2)]; kt2 = [sb("kt%d" % i, [128, NT], BF16) for i in range(2)]; ke2 = [sb("ke%d" % i, [128, NT], BF16) for i in range(2)]
    qG = sb("qG", [128, 4, NT], BF16); Dt = sb("Dt", [128, 4, 4]); attT = sb("attT", [64, 4, NT], BF16)
    ketok2 = [sb("ketok%d" % i, [64, 4, 128], BF16) for i in range(2)]
    Sm = [sb("Sm%d" % i, [128, 512]) for i in range(2)]; Dg = sb("Dg", [128, 4, 16]); Srun = sb("Srun", [128, 512])
    Ssel = [sb("Ssel%d" % i, [128, 512]) for i in range(2)]; Sbf = sb("Sbf", [128, 4, 512], BF16)
    Sin_s = Ssel[0]
    Tg = sb("Tg", [128, 4, 60]); Tprev = sb("Tprev", [128, 2, 15])
    pa = [sb("pa%d" % i, [128, 640]) for i in range(4)]
    dT = sb("dT", [128, 2, NT], BF16); ptmp = sb("ptmp", [128, 128])
    e_sb = [[sb("e%d_%d" % (a, b), [128, NT]) for b in range(2)] for a in range(2)]
    sp_sb = [[sb("sp%d_%d" % (a, b), [128, NT], BF16) for b in range(2)] for a in range(2)]
    A_sb = [[sb("A%d_%d" % (a, b), [128, NT], BF16) for b in range(2)] for a in range(2)]
    Rsum = [sb("Rsum%d" % a, [128, NT]) for a in range(2)]; Rbf = [sb("Rbf%d" % a, [128, NT], BF16) for a in range(2)]
    kTg = [sb("kTg%d" % i, [128, 4, 128], BF16) for i in range(2)]; vg = [sb("vg%d" % i, [128, 4, 128], BF16) for i in range(2)]
    mixT = sb("mixT", [128, 8, NT], BF16); yT = sb("yT", [128, 8, NT]); qxT = sb("qxT", [128, 8, NT], BF16)
    oxT = sb("oxT", [128, 8, NT], BF16); pexp = sb("pexp", [128, 256]); pbf = sb("pbf", [128, 256], BF16)
    pT_sb = sb("pT_sb", [128, 8, 128], BF16); memKT = sb("memKT", [128, 8, 256], BF16); memV = sb("memV", [128, 2, 1024], BF16)
    sm4 = sb("sm4", [128, 4]); t1 = sb("t1", [128, NT]); ystage = xin[1]
    mT = xT; mhT = hT

    pg = [ps("pg%d" % i) for i in range(3)]
    pz = [ps("pz%d" % i) for i in range(2)]; pc = [ps("pc%d" % i) for i in range(2)]; po = ps("po")
    pgi = [0]

    def nextpg():
        pgi[0] = (pgi[0] + 1) % 3
        return pg[pgi[0]], "pg%d" % pgi[0]

    K = lambda t: t.name if hasattr(t, "name") else t
    WIK = lambda l: ["wi%d_%d_%d" % (l, kc, hf) for kc in range(8) for hf in range(2)]
    WQK = lambda l, m: ["wq%d_%d_%d" % (l, m, k) for k in range(4)]

    def mm(out, lhsT, rhs, start, stop, r, w):
        P.op('pe', lambda e: e.matmul(out, lhsT, rhs, start=start, stop=stop), r, w)

    def tp(out, in_, ident, r, w):
        P.op('pe', lambda e: e.transpose(out, in_, ident), r, w)

    def act(out, in_, func, r, w, bias=None, scale=None, accum=None):
        kw = {}
        if bias is not None: kw['bias'] = bias
        if scale is not None: kw['scale'] = scale
        if accum is not None: kw['accum_out'] = accum
        P.op('act', lambda e: e.activation(out, in_, func, **kw), r, w)

    def tt(eng, out, in0, in1, op, r, w):
        P.op(eng, lambda e: e.tensor_tensor(out, in0, in1, op), r, w)

    def ts(eng, out, in0, s1, s2, op0, op1, r, w):
        if op1 is None:
            P.op(eng, lambda e: e.tensor_scalar(out, in0, s1, None, op0), r, w)
        else:
            P.op(eng, lambda e: e.tensor_scalar(out, in0, s1, s2, op0, op1), r, w)

    def stt(out, in0, scalar, in1, op0, op1, r, w):
        P.op('dve', lambda e: e.scalar_tensor_tensor(out, in0, scalar, in1, op0, op1), r, w)

    def cpy(eng, out, in_, r, w):
        if eng == 'act':
            P.op('act', lambda e: e.copy(out, in_), r, w)
        else:
            P.op(eng, lambda e: e.tensor_copy(out, in_), r, w)

    def mset(eng, out, val, w):
        P.op(eng, lambda e: e.memset(out, val), (), w)

    P.dma(cstt[:], cst, (), ["cstt"]); P.dma(prmt[:], prm, (), ["prmt"]); P.dma(lbt[:], lbl, (), ["lbt"])
    P.dma(wpt[:].rearrange("p a b c -> p (a b c)"), wpl, (), ["wpt"])
    cpy('dve', idb[:], cstt[:, C_ID:C_ID + 128], ["cstt"], ["idb"])
    cpy('dve', trib[:], cstt[:, C_TRI:C_TRI + 128], ["cstt"], ["trib"])
    cpy('dve', wpb[:], wpt[:], ["wpt"], ["wpb"])
    mset('pool', oneb[:], 1.0, ["oneb"]); mset('pool', epst[:], EPS, ["epst"])
    zt = sb("zt", [128, 52]); mset('pool', zt[:], 0.0, ["zt"])
    mset('pool', vnew[:], 0.0, ["vnew"]); mset('pool', knew[:], 0.0, ["knew"]); mset('pool', qm[:], 0.0, ["qm"])
    for l_ in range(2):
        for r_ in range(NRP):
            P.dma(t_loc[l_][r_][19:32, :].rearrange("r c -> (r c)").rearrange("(p b) -> p b", b=52), zt[:], ["zt"], ["sl_pad%d_%d" % (l_, r_)])
    mset('pool', Srun[:], 0.0, ["Srun"]); mset('pool', Tprev[:], 0.0, ["Tprev"])
    idf = cstt[:, C_ID:C_ID + 128]
    STOP = int(os.environ.get('KDBG_STOP', 99))
    if STOP <= 0:
        P.finish(); return nc, P, es
    act(lbtmp[:, 0, :], lbt[:, 0:4], AF.Exp, ["lbt"], ["lbtmp"])
    act(lbtmp[:, 1, :], lbt[:, 4:8], AF.Exp, ["lbt"], ["lbtmp"])
    tt('dve', lbtmp[:, 2, :], lbtmp[:, 0, :], lbtmp[:, 1, :], ALU.add, ["lbtmp"], ["lbtmp"])
    act(lbtmp[:, 2, :], lbtmp[:, 2, :], AF.Ln, ["lbtmp"], ["lbtmp"])
    act(lbtmp[:, 2, :], lbtmp[:, 2, :], AF.Exp, ["lbtmp"], ["lbtmp"], scale=-1.0)
    tt('dve', lbtmp[:, 0, :], lbtmp[:, 0, :], lbtmp[:, 2, :], ALU.mult, ["lbtmp"], ["lbtmp"])
    tt('dve', lbtmp[:, 1, :], lbtmp[:, 1, :], lbtmp[:, 2, :], ALU.mult, ["lbtmp"], ["lbtmp"])
    tt('dve', lbv[:, 0, :], lbtmp[:, 0, :], lbtmp[:, 0, :], ALU.subtract, ["lbtmp"], ["lbv"])
    tt('dve', lbtmp[:, 2, :], lbtmp[:, 0, :], lbtmp[:, 1, :], ALU.add, ["lbtmp"], ["lbtmp"])
    tt('dve', lbv[:, 1, :], lbtmp[:, 2, :], lbtmp[:, 0, :], ALU.subtract, ["lbtmp"], ["lbv"])
    ts('dve', lbv[:], lbv[:], 0.0, None, ALU.max, None, ["lbv"], ["lbv"])
    ts('dve', omlv[:], lbv[:], -1.0, 1.0, ALU.mult, ALU.add, ["lbv"], ["omlv"])

    if STOP <= 1:
        P.finish(); return nc, P, es
    ceng = ['dve', 'act', 'dve']
    wi = [0]

    def wcast(src, dst, n, wkey):
        i = wi[0]; wi[0] += 1
        s, b = stg[i % 2], bstg[i % 2]
        P.dma(s[:, :n], src, (), [K(s)])
        cpy(ceng[i % 3], b[:, :n], s[:, :n], [K(s)], [K(b)])
        P.dma(dst, b[:, :n], [K(b)], [wkey])

    for l in range(2):
        for kc in range(8):
            for hf in range(2):
                wcast(w_in[l, kc * 128:(kc + 1) * 128, hf * 1792:(hf + 1) * 1792],
                      win_bf[l][:, kc, hf * 1792:(hf + 1) * 1792], 1792, "wi%d_%d_%d" % (l, kc, hf))
        for m in range(5):
            for kc2 in range(4):
                wcast(wsq[m][l, kc2 * 256:(kc2 + 1) * 256, :].rearrange("(k p) n -> p k n", p=128),
                      wsq_bf[l][m][:, 2 * kc2:2 * kc2 + 2, :], 2048, "wq%d_%d_%d" % (l, m, kc2))

    if STOP <= 2:
        P.finish(); return nc, P, es
    wgi = [0]

    def load_wg(src, rkeys):
        i = wgi[0] % 2; wgi[0] += 1
        P.dma(wg[i][:], src, rkeys, [K(wg[i])])
        return wg[i], K(wg[i])

    def rmsnorm_to_bf(src, skey, n, gcol, dst, dkey):
        stp_, stk = nextpg()
        for c in range(8):
            q = sqb[c % 2]
            act(q[:, :n], src[:, c, :n], AF.Square, [skey], [K(q)])
            mm(stp_[:, :n], oneb[:], q[:, :n], c == 0, c == 7, [K(q), "oneb"], [stk])
        act(lnv[:, :n], stp_[:, :n], AF.Ln, [stk, "epst"], ["lnv"], bias=epst[:, 0:1], scale=1.0 / 1024)
        act(rstd[:, :n], lnv[:, :n], AF.Exp, ["lnv"], ["rstd"], scale=-0.5)
        for c in range(8):
            stt(dst[:, c, :n], src[:, c, :n], prmt[:, gcol + c:gcol + c + 1], rstd[:, :n], ALU.mult, ALU.mult,
                [skey, "prmt", "rstd"], [dkey])

    def proj_fm(wt, wk, col, rhsT, rkey, n):
        p_, pk = nextpg()
        for kc in range(8):
            mm(p_[:, :n], wt[:, kc, col:col + 128], rhsT[:, kc, :n], kc == 0, kc == 7, [wk, rkey], [pk])
        return p_, pk

    def linear_norm_res(m, l, rhsT, rkey, gcol):
        stp_ = pc[0]; stk = "pc0"
        for g2 in range(2):
            wt, wk = load_wg(wsq_bf[l][m][:, :, g2 * 512:(g2 + 1) * 512], WQK(l, m))
            for oc in range(4):
                c = g2 * 4 + oc
                p_, pk = proj_fm(wt, wk, oc * 128, rhsT, rkey, NT)
                cpy('act', yT[:, c, :], p_[:, :NT], [pk], ["yT"])
                q = sqb[c % 2]
                act(q[:], p_[:, :NT], AF.Square, [pk], [K(q)])
                mm(stp_[:, :NT], oneb[:], q[:], c == 0, c == 7, [K(q), "oneb"], [stk])
        act(lnv[:], stp_[:, :NT], AF.Ln, [stk, "epst"], ["lnv"], bias=epst[:, 0:1], scale=1.0 / 1024)
        act(rstd[:], lnv[:], AF.Exp, ["lnv"], ["rstd"], scale=-0.5)
        for c in range(8):
            stt(t1[:], yT[:, c, :], prmt[:, gcol + c:gcol + c + 1], rstd[:], ALU.mult, ALU.mult,
                ["yT", "prmt", "rstd"], ["t1"])
            tt('pool', xT[:, c, :], xT[:, c, :], t1[:], ALU.add, ["xT", "t1"], ["xT"])

    def load_tokmajor_T(src_rows, nblk, dst, dkey):
        for blk in range(nblk):
            xi = xin[blk % 2]
            P.dma(xi[:], src_rows(blk), (), [K(xi)])
            for hf in range(2):
                p_, pk = nextpg()
                for c4 in range(4):
                    c = hf * 4 + c4
                    tp(p_[:, c4 * 128:(c4 + 1) * 128], xi[:, c * 128:(c + 1) * 128], idf, [K(xi), "cstt"], [pk])
                cpy('act' if hf else 'dve', dst[:, hf * 4:hf * 4 + 4, blk * 128:(blk + 1) * 128],
                    p_[:, :].rearrange("p (c t) -> p c t", c=4), [pk], [dkey])

    def sb_stages(lanes, nk, first, last, par, pre=None):
        def s_z():
            if pre is not None:
                pre()
            for ln in lanes:
                a = ln['idx']
                for (lh, rh, c0, c1, rk) in ln['z']:
                    mm(pz[a][:nk, c0:c1], lh, rh, True, True, rk, ["pz%d" % a])

        def s_exp():
            for ln in lanes:
                a = ln['idx']; W = ln['W']; e_ = e_sb[a][par]
                act(e_[:nk, :W], pz[a][:nk, :W], AF.Exp, ["pz%d" % a], [K(e_)])

        def s_mask():
            for ln in lanes:
                a = ln['idx']; e_ = e_sb[a][par]
                for (c0, c1, mk, bc) in ln['masks']:
                    if mk is None:
                        mset('pool', e_[:nk, c0:c1], 0.0, [K(e_)])
                    elif bc:
                        tt('pool', e_[:nk, c0:c1].rearrange("p (h t) -> p h t", h=bc),
                           e_[:nk, c0:c1].rearrange("p (h t) -> p h t", h=bc), mk, ALU.mult, [K(e_), "cstt"], [K(e_)])
                    else:
                        tt('pool', e_[:nk, c0:c1], e_[:nk, c0:c1], mk, ALU.mult, [K(e_), "cstt"], [K(e_)])

        def s_ln():
            for ln in lanes:
                a = ln['idx']; W = ln['W']; e_, sp_ = e_sb[a][par], sp_sb[a][par]
                act(sp_[:nk, :W], e_[:nk, :W], AF.Ln, [K(e_)], [K(sp_)], bias=1.0)

        def s_cs():
            for ln in lanes:
                a = ln['idx']; W = ln['W']; sp_ = sp_sb[a][par]
                mm(pc[a][:nk, :W], trib[:nk, :nk], sp_[:nk, :W], True, first, [K(sp_), "trib"], ["pc%d" % a])
                if not first:
                    mm(pc[a][:nk, :W], oneb[:, :nk], Rbf[a][:, :W], False, True, [K(Rbf[a]), "oneb"], ["pc%d" % a])

        def s_exp2():
            for ln in lanes:
                a = ln['idx']; W = ln['W']
                act(pc[a][:nk, :W], pc[a][:nk, :W], AF.Exp, ["pc%d" % a], ["pc%d" % a], scale=-1.0)

        def s_A():
            for ln in lanes:
                a = ln['idx']; W = ln['W']; e_, A_ = e_sb[a][par], A_sb[a][par]
                tt('dve', A_[:nk, :W], pc[a][:nk, :W], e_[:nk, :W], ALU.mult, ["pc%d" % a, K(e_)], [K(A_)])

        def s_R():
            if last:
                return
            for ln in lanes:
                a = ln['idx']; W = ln['W']; sp_ = sp_sb[a][par]
                if first:
                    cpy('dve', Rsum[a][:nk, :W], sp_[:nk, :W], [K(sp_)], [K(Rsum[a])])
                else:
                    tt('dve', Rsum[a][:nk, :W], Rsum[a][:nk, :W], sp_[:nk, :W], ALU.add, [K(Rsum[a]), K(sp_)], [K(Rsum[a])])
                cpy('pool', Rbf[a][:, :W], Rsum[a][:, :W], [K(Rsum[a])], [K(Rbf[a])])

        def s_pv():
            for ln in lanes:
                a = ln['idx']; A_ = A_sb[a][par]
                for (lh, c0, c1, oap, rk, ok_) in ln['pv']:
                    mm(oap, lh, A_[:nk, c0:c1], first, last, rk + [K(A_)], [ok_])

        return [s_z, s_exp, s_mask, s_ln, s_cs, s_R, s_exp2, s_A, s_pv]

    SKEW = int(os.environ.get('KDBG_SKEW', 3))

    def run_steps(step_list, extras=None):
        n = len(step_list)
        ns = 9
        T = (n - 1) * SKEW + ns
        extras = list(extras) if extras else []
        ne = len(extras)
        for slot in range(T):
            for si in range(n):
                st = slot - si * SKEW
                if 0 <= st < ns:
                    step_list[si][st]()
            if ne:
                lo = T // 2
                if slot >= lo:
                    want = ((slot - lo + 1) * ne) // (T - lo)
                    while len(extras) > ne - want:
                        extras.pop(0)()
        while extras:
            extras.pop(0)()

    NLAY = int(os.environ.get('KDBG_LAYERS', DEPTH))
    for l in range(NLAY):
        pb = l * PRM_L
        load_tokmajor_T(lambda blk: memp[blk * 128:(blk + 1) * 128, :], 2, mT, "xT")
        if STOP <= 3:
            P.dma(xs[:, :, 0:256], xT[:], ["xT"], ()); P.finish(); return nc, P, es
        rmsnorm_to_bf(mT, "xT", 256, pb + P_NMEM, mhT, "hT")
        if STOP <= 4:
            P.dma(win_bf[0][:, :, 0:256], hT[:], ["hT"], ()); P.finish(); return nc, P, es
        for which, m in (("k", 3), ("v", 4)):
            if STOP == 5 and which == "v":
                break
            for g2 in range(2):
                wt, wk = load_wg(wsq_bf[l][m][:, :, g2 * 512:(g2 + 1) * 512], WQK(l, m))
                for mb in range(2):
                    p_, pk = nextpg()
                    for kc in range(8):
                        mm(p_[:, :512], mhT[:, kc, mb * 128:(mb + 1) * 128], wt[:, kc, :], kc == 0, kc == 7, [wk, "hT"], [pk])
                    cpy('act', ystage[:, :512], p_[:, :512], [pk], [K(xin[1])])
                    dst = (o_pmk if which == "k" else o_pmv)[l, mb * 128:(mb + 1) * 128, g2 * 512:(g2 + 1) * 512]
                    P.dma(dst, ystage[:, :512], [K(xin[1])], ())
                    if which == "v":
                        cpy('pool', memV[:, mb, g2 * 512:(g2 + 1) * 512], ystage[:, :512], [K(xin[1])], ["memV"])
                if which == "k" and STOP != 6:
                    for oc in range(4):
                        p_, pk = proj_fm(wt, wk, oc * 128, mhT, "hT", 256)
                        cpy('dve', memKT[:, g2 * 4 + oc, :], p_[:, :256], [pk], ["memKT"])

        if l == 1:
            mset('pool', Srun[:], 0.0, ["Srun"]); mset('pool', Tprev[:], 0.0, ["Tprev"])

        for R in [int(x) for x in os.environ['KDBG_ROUNDS'].split(',') if x != ''] if 'KDBG_ROUNDS' in os.environ else range(NR):
            samp = (R == NRP)
            t0 = R * NT
            NBK = 4 if samp else 2
            L = 64 if samp else 128
            E = Es if samp else Ep
            Ek = K(E)
            if l == 0:
                src = x_s if samp else x_p
                tb = 0 if samp else t0
                load_tokmajor_T(lambda blk: src[tb + blk * 128: tb + (blk + 1) * 128, :], 2, xT, "xT")
            else:
                P.dma(xT[:], xs[:, :, t0:t0 + NT], ["xs%d" % R], ["xT"])
            rmsnorm_to_bf(xT, "xT", NT, pb + P_NPRE, hT, "hT")
            wt, wk = load_wg(win_bf[l][:, :, 512:1024], WIK(l))
            for h in range(4):
                p_, pk = proj_fm(wt, wk, h * 128, hT, "hT", NT)
                act(fgate[:, h, :], p_[:, :NT], AF.Sigmoid, [pk], ["fg%d" % h])
            wt, wk = load_wg(win_bf[l][:, :, 0:512], WIK(l))
            for h in range(4):
                p_, pk = proj_fm(wt, wk, h * 128, hT, "hT", NT)
                act(q_sb[:, h, :], p_[:, :NT], AF.Silu, [pk], ["q_sb"])
            wt, wk = load_wg(win_bf[l][:, :, 1536:2048], WIK(l))
            for h in range(4):
                p_, pk = proj_fm(wt, wk, h * 128, hT, "hT", NT)
                act(gate_a[:, h, :], p_[:, :NT], AF.Silu, [pk], ["gate_a"])
            wt, wk = load_wg(win_bf[l][:, :, 2048:2560], WIK(l))
            for p in range(2):
                p_, pk = proj_fm(wt, wk, p * 128, hT, "hT", NT)
                ts('dve', qTb[:, p, :], p_[:, :NT], 0.125, None, ALU.mult, None, [pk], ["qTb"])
                p_, pk = proj_fm(wt, wk, 256 + p * 128, hT, "hT", NT)
                cpy('act', kTl[:, p, :], p_[:, :NT], [pk], ["kTl"])
            for blk in range(2):
                p_, pk = nextpg()
                for kc in range(8):
                    mm(p_[:, :256], hT[:, kc, blk * 128:(blk + 1) * 128], wt[:, kc, 256:512], kc == 0, kc == 7, [wk, "hT"], [pk])
                cpy('act', ktok[:, blk, :], p_[:, :256], [pk], ["ktok"])
            okd = (o_sk if samp else o_pk); ovd = (o_sv if samp else o_pv); ot0 = 0 if samp else t0
            P.dma(okd[l, ot0:ot0 + NT, :].rearrange("(b q) d -> q b d", q=128), ktok[:], ["ktok"], ())
            wt, wk = load_wg(win_bf[l][:, :, 2560:3072], WIK(l))
            for blk in range(2):
                p_, pk = nextpg()
                for kc in range(8):
                    mm(p_[:, :256], hT[:, kc, blk * 128:(blk + 1) * 128], wt[:, kc, 0:256], kc == 0, kc == 7, [wk, "hT"], [pk])
                cpy('act', vtok[:, blk, :], p_[:, :256], [pk], ["vtok"])
                cpy('pool', vbl[:, blk, :], vtok[:, blk, :], ["vtok"], ["vbl"])
            P.dma(ovd[l, ot0:ot0 + NT, :].rearrange("(b q) d -> q b d", q=128), vtok[:], ["vtok"], ())
            if samp:
                for b in range(4):
                    p_, pk = nextpg()
                    for kc in range(8):
                        mm(p_[:64, :256], hT[:, kc, b * 64:(b + 1) * 64], wt[:, kc, 0:256], kc == 0, kc == 7, [wk, "hT"], [pk])
                    cpy('dve', vnew[:64, b, :], p_[:64, :256], [pk], ["vnew"])
            for p in range(2):
                p_, pk = proj_fm(wt, wk, 256 + p * 128, hT, "hT", NT)
                act(gate_b[:, p, :], p_[:, :NT], AF.Silu, [pk], ["gate_b"])
            if not samp:
                kl, kg = kv_loc[l][R], kv_g[l][R]
                P.dma(kl[0:256, :].rearrange("(p q) t -> q p t", q=128), kTl[:], ["kTl"], ["kvl_k"])
                P.dma(kl[256:512, :].rearrange("(b q) d -> q b d", q=128), vbl[:], ["vbl"], ["kvl_v"])
                P.allgather(kl.opt(), kg.opt(), ["kvl_k", "kvl_v"], ["kvg%d_%d" % (l, R)])
            wt, wk = load_wg(win_bf[l][:, :, 3072:3584], WIK(l))
            for cc in range(2):
                p_, pk = proj_fm(wt, wk, cc * 128, hT, "hT", NT)
                cpy('dve', E[:, cc, :, 15:15 + L], p_[:, :NT].rearrange("p (b t) -> p b t", b=NBK), [pk], [Ek])
                p_, pk = proj_fm(wt, wk, 256 + cc * 128, hT, "hT", NT)
                act(gate_c[:, cc, :], p_[:, :NT], AF.Silu, [pk], ["gate_c"])
            wt, wk = load_wg(win_bf[l][:, :, 1024:1536], WIK(l))
            for ch in range(4):
                p_, pk = nextpg()
                for kc in range(8):
                    mm(p_[:64, :512], hT[:, kc, ch * 64:(ch + 1) * 64], wt[:, kc, :], kc == 0, kc == 7, [wk, "hT"], [pk])
                cpy('act' if ch % 2 else 'dve', v_tok[:, ch, :], p_[:64, :512], [pk], ["v_tok"])

            if STOP == 10:
                P.finish(); return nc, P, es
            c3 = lambda t: t[:, :].rearrange("p (c t) -> p c t", c=4)
            QS = 128.0 ** -0.5
            for g0 in (0, 2):
                hl = (g0, g0 + 1)
                T = {h: [hs[n] for n in (("f", "lg", "kk", "G") if h % 2 == 0 else ("Gr", "Gl", "e1", "e2"))] for h in hl}
                TK = {h: [K(t) for t in T[h]] for h in hl}
                QT = {h: (qt2[h % 2], kt2[h % 2], ke2[h % 2]) for h in hl}
                for h in hl:
                    ts('dve', fgate[:, h, :], fgate[:, h, :], omlv[:, l, h:h + 1], lbv[:, l, h:h + 1], ALU.mult, ALU.add,
                       ["fg%d" % h, "omlv", "lbv"], ["fg%d" % h])
                for h in hl:
                    act(T[h][0][:], fgate[:, h, :], AF.Ln, ["fg%d" % h], [TK[h][0]])
                for h in hl:
                    P.op('dve', lambda e, G=T[h][1], lg=T[h][0]: e.tensor_tensor_scan(G[:], cstt[:, C_RST:C_RST + NT], lg[:], 0.0, ALU.mult, ALU.add),
                         [TK[h][0], "cstt"], [TK[h][1]])
                    ts('pool', fgate[:, h, :], fgate[:, h, :], -1.0, 1.0, ALU.mult, ALU.add, ["fg%d" % h, TK[h][0]], ["fg%d" % h])
                for h in hl:
                    G = T[h][1]
                    tt('pool', c3(T[h][2]), c3(G), G[:, 31::64].unsqueeze(2).to_broadcast([128, 4, 64]), ALU.subtract, [TK[h][1]], [TK[h][2]])
                    tt('pool', c3(T[h][0]), c3(G), G[:, 63::64].unsqueeze(2).to_broadcast([128, 4, 64]), ALU.subtract, [TK[h][1], TK[h][0]], [TK[h][0]])
                for h in hl:
                    act(T[h][3][:], T[h][2][:], AF.Exp, [TK[h][2]], [TK[h][3]])
                for h in hl:
                    stt(QT[h][0][:], q_sb[:, h, :], QS, T[h][3][:], ALU.mult, ALU.mult, ["q_sb", TK[h][3]], [K(QT[h][0])])
                for h in hl:
                    act(T[h][2][:], T[h][2][:], AF.Exp, [TK[h][2]], [TK[h][2]], scale=-1.0)
                for h in hl:
                    tt('pool', QT[h][1][:], fgate[:, h, :], T[h][2][:], ALU.mult, ["fg%d" % h, TK[h][2]], [K(QT[h][1])])
                for h in hl:
                    act(T[h][3][:], T[h][1][:], AF.Exp, [TK[h][1]], [TK[h][3]])
                for h in hl:
                    stt(qG[:, h, :], q_sb[:, h, :], QS, T[h][3][:], ALU.mult, ALU.mult, ["q_sb", TK[h][3]], ["qG%d" % h])
                    cpy('pool', Dt[:, :, h], T[h][3][:, 63::64], [TK[h][3]], ["Dt"])
                for h in hl:
                    act(T[h][0][:], T[h][0][:], AF.Exp, [TK[h][0]], [TK[h][0]], scale=-1.0)
                for h in hl:
                    tt('pool', QT[h][2][:], fgate[:, h, :], T[h][0][:], ALU.mult, ["fg%d" % h, TK[h][0]], [K(QT[h][2])])
                pks = {}
                for h in hl:
                    p_, pk = nextpg(); pks[h] = (p_, pk)
                    for ch in range(4):
                        cs_ = slice(ch * 64, (ch + 1) * 64)
                        mm(p_[:64, cs_], QT[h][1][:, cs_], QT[h][0][:, cs_], True, True, [K(QT[h][1]), K(QT[h][0])], [pk])
                for h in hl:
                    p_, pk = pks[h]
                    tt('dve', attT[:, h, :].rearrange("p (c t) -> p c t", c=4), p_[:64, :NT].rearrange("p (c t) -> p c t", c=4),
                       cstt[:64, C_HM:C_HM + 64].unsqueeze(1).to_broadcast([64, 4, 64]), ALU.mult, [pk, "cstt"], ["attT%d" % h])
                for h in hl:
                    p_, pk = nextpg()
                    pb16 = p_[:, :].bitcast(BF16)
                    for ch in range(4):
                        tp(pb16[:64, ch * 128:(ch + 1) * 128], QT[h][2][:, ch * 64:(ch + 1) * 64], idb[:], [K(QT[h][2]), "idb"], [pk])
                    cpy('act', ketok2[h % 2][:].rearrange("p c k -> p (c k)"), pb16[:64, :512], [pk], [K(ketok2[h % 2])])
                for h in hl:
                    p_, pk = nextpg()
                    for ch in range(4):
                        mm(p_[:, ch * 128:(ch + 1) * 128], ketok2[h % 2][:, ch, :], v_tok[:, ch, h * 128:(h + 1) * 128], True, True,
                           [K(ketok2[h % 2]), "v_tok"], [pk])
                    cpy('act', Sst[:, :, h * 128:(h + 1) * 128], p_[:, :].rearrange("p (c v) -> p c v", c=4), [pk], ["Sst"])

            if STOP == 11:
                P.finish(); return nc, P, es
            if not samp:
                sl, sg_ = s_loc[l][R], s_g[l][R]
                P.dma(sl[0:512, :].rearrange("(c k) n -> k c n", k=128), Sst[:], ["Sst"], ["sl_s"])
                tlc, tgc = t_loc[l][R], t_g[l][R]
                P.dma(tlc[0:4, :].rearrange("r (a b) -> (r a) b", b=16), Dt[:].rearrange("p c h -> p (c h)"), ["Dt"], ["sl_d"])
                tl_ = tlc[4:19, :].rearrange("r c -> (r c)").rearrange("(p b) -> p b", b=60)
                for cc in range(2):
                    P.dma(tl_[:, cc * 30:(cc + 1) * 30].rearrange("p (b t) -> p b t", b=2),
                          E[:, cc, :, 128:143], [Ek], ["sl_t%d" % cc], slow=True)
                P.allgather(sl.opt(), sg_.opt(), ["sl_s"], ["sg%d_%d" % (l, R)])
                P.allgather(tlc.opt(), tgc.opt(), ["sl_d", "sl_t0", "sl_t1", "sl_pad%d_%d" % (l, R)], ["tg%d_%d" % (l, R)])

            if STOP == 12:
                P.finish(); return nc, P, es
            def emit_pooling():
                if not samp:
                    T4 = lambda r: Tg[:, r, :].rearrange("p (c b t) -> p c b t", c=2, b=2)
                    sel = lambda i: cstt[:, C_SEL + i:C_SEL + i + 1]
                    hal = E[:, :, :, 0:15]
                    ts('dve', hal, T4(0), sel(0), None, ALU.mult, None, ["Tg", "cstt"], [Ek])
                    stt(hal, T4(1), sel(1), hal, ALU.mult, ALU.add, ["Tg", "cstt", Ek], [Ek])
                    stt(hal, T4(2), sel(2), hal, ALU.mult, ALU.add, ["Tg", "cstt", Ek], [Ek])
                    stt(E[:, :, 1, 0:15], T4(3)[:, :, 0, :], sel(3), E[:, :, 1, 0:15], ALU.mult, ALU.add, ["Tg", "cstt", Ek], [Ek])
                    stt(E[:, :, 0, 0:15], Tprev[:], sel(3), E[:, :, 0, 0:15], ALU.mult, ALU.add, ["Tprev", "cstt", Ek], [Ek])
                    cpy('pool', Tprev[:], T4(3)[:, :, 1, :], ["Tg"], ["Tprev"])
                    if R == NRP - 1:
                        for cc in range(2):
                            P.dma(o_pps[l][:, cc * 128:(cc + 1) * 128].rearrange("t p -> p t"), E[:, cc, 1, 128:143], [Ek], (), slow=True)
                else:
                    for b in range(4):
                        for cc in range(2):
                            P.dma(E[:, cc, b, 0:15], stp[l, b][:, cc * 128:(cc + 1) * 128].rearrange("t p -> p t"), (), [Ek], slow=True)
                            P.dma(o_sps[l, b][:, cc * 128:(cc + 1) * 128].rearrange("t p -> p t"), E[:, cc, b, 64:79], [Ek], (), slow=True)
                LE = L + 15
                a2, a4, a8, a16 = (pa[i][:, :2 * NBK * (L + 14)].rearrange("p (c b t) -> p c b t", c=2, b=NBK) for i in range(4))
                tt('pool', a2[:, :, :, 0:LE - 1], E[:, :, :, 1:LE], E[:, :, :, 0:LE - 1], ALU.add, [Ek], ["pa0"])
                tt('pool', a4[:, :, :, 0:LE - 3], a2[:, :, :, 2:LE - 1], a2[:, :, :, 0:LE - 3], ALU.add, ["pa0"], ["pa1"])
                tt('pool', a8[:, :, :, 0:LE - 7], a4[:, :, :, 4:LE - 3], a4[:, :, :, 0:LE - 7], ALU.add, ["pa1"], ["pa2"])
                tt('pool', a16[:, :, :, 0:LE - 15], a8[:, :, :, 8:LE - 7], a8[:, :, :, 0:LE - 15], ALU.add, ["pa2"], ["pa3"])
                grp = [(0, 0, a2, 14, 2, "pa0"), (0, 64, a4, 12, 4, "pa1"), (1, 0, a8, 8, 8, "pa2"), (1, 64, a16, 0, 16, "pa3")]
                for (cc, p0, aw, off, w, ak) in grp:
                    stt(dT[p0:p0 + 64, cc, :].rearrange("p (b t) -> p b t", b=NBK), aw[p0:p0 + 64, cc, :, off:off + L], 1.0 / w,
                        E[p0:p0 + 64, cc, :, 15:15 + L], ALU.mult, ALU.subtract, [ak, Ek], ["dT"])
                    if (not samp) and R == 0:
                        tt('pool', ptmp[p0:p0 + 64, :], aw[p0:p0 + 64, cc, 0, off:off + 128],
                           cstt[p0:p0 + 64, C_INV + cc * 128:C_INV + (cc + 1) * 128], ALU.mult, [ak, "cstt"], ["ptmp"])
                        tt('pool', dT[p0:p0 + 64, cc, 0:128], ptmp[p0:p0 + 64, :], E[p0:p0 + 64, cc, 0, 15:143], ALU.subtract,
                           ["ptmp", Ek, "dT"], ["dT"])
                for cc in range(2):
                    p_, pk = nextpg()
                    for gg in range(2):
                        mm(p_[64 * gg:64 * gg + 64, :NT], wpb[64 * gg:64 * gg + 64, l, cc, :], dT[64 * gg:64 * gg + 64, cc, :], True, True,
                           ["wpb", "dT"], [pk])
                    stt(mixT[:, 6 + cc, :], p_[:, :NT], prmt[:, pb + P_PS + cc:pb + P_PS + cc + 1], gate_c[:, cc, :], ALU.mult, ALU.mult,
                        [pk, "prmt", "gate_c"], ["mixT"])

            prefix_items = []
            if not samp:
                sgk = "sg%d_%d" % (l, R); tgk = "tg%d_%d" % (l, R)

                def _pf_loads(l=l, R=R, tgk=tgk):
                    for r in range(4):
                        P.dma(Dg[:, r, :], t_g[l][R][r * TROWS:r * TROWS + 4, :].rearrange("r (a b) -> (r a) b", b=16),
                              [tgk], ["Dg"])
                        P.dma(Tg[:, r, :], t_g[l][R][r * TROWS + 4:r * TROWS + 19, :].rearrange("r c -> (r c)").rearrange("(p b) -> p b", b=60),
                              [tgk], ["Tg"])
                prefix_items.append(_pf_loads)
                mi = 0
                for rr in range(2):
                    for r in range(4):
                        for c in range(2):
                            def _pf_chunk(l=l, R=R, rr=rr, r=r, c=c, mi=mi, sgk=sgk):
                                ch = 2 * rr + c
                                smt = Sm[mi % 2]
                                P.dma(smt[:], s_g[l][R][r * SROWS + ch * 128: r * SROWS + (ch + 1) * 128, :], [sgk], [K(smt)])
                                oh = cstt[:, C_SEL + 4 + r:C_SEL + 5 + r]
                                if r == 0:
                                    ts('dve', Ssel[c][:], Srun[:], oh, None, ALU.mult, None, ["Srun", "cstt"], [K(Ssel[c])])
                                else:
                                    stt(Ssel[c][:], Srun[:], oh, Ssel[c][:], ALU.mult, ALU.add, ["Srun", "cstt", K(Ssel[c])], [K(Ssel[c])])
                                sv = Srun[:, :].rearrange("p (h v) -> p h v", h=4)
                                tt('dve', sv, sv, Dg[:, r, ch * 4:(ch + 1) * 4].unsqueeze(2).to_broadcast([128, 4, 128]), ALU.mult,
                                   ["Srun", "Dg"], ["Srun"])
                                tt('dve', Srun[:], Srun[:], smt[:], ALU.add, ["Srun", K(smt)], ["Srun"])
                                if r == 3:
                                    cpy('act', Sbf[:, ch, :], Ssel[c][:], [K(Ssel[c])], ["Sbf"])
                            prefix_items.append(_pf_chunk)
                            mi += 1
                if R == NRP - 1:
                    prefix_items.append(lambda l=l: P.dma(o_phs[l].rearrange("h k v -> k h v"), Srun[:, :].rearrange("p (h v) -> p h v", h=4), ["Srun"], ()))
                prefix_items.append(emit_pooling)
            if STOP == 14:
                P.finish(); return nc, P, es
            if not samp:
                for p in range(2):
                    groups = list(range(2 * R + 1, -1, -1))
                    steps = [(g, r) for g in groups for r in (3, 2, 1, 0)]
                    sl_ = []
                    for si, (g, r) in enumerate(steps):
                        kt_, vt_ = kTg[(si // 4) % 2], vg[(si // 4) % 2]
                        pre = None
                        if r == 3:
                            Rg = g // 2; gl = g % 2
                            gk = "kvg%d_%d" % (l, Rg)
                            src = kv_g[l][Rg]

                            def pre(kt_=kt_, vt_=vt_, src=src, gl=gl, gk=gk, p=p):
                                P.dma(kt_[:], src.rearrange("(r x) t -> x r t", r=4)[p * 128:(p + 1) * 128, :, gl * 128:(gl + 1) * 128],
                                      [gk], [K(kt_)])
                                P.dma(vt_[:], src.rearrange("(r x) d -> x r d", r=4)[256 + gl * 128:256 + (gl + 1) * 128, :, p * 128:(p + 1) * 128],
                                      [gk], [K(vt_)])
                        lanes = []
                        for hh in range(2):
                            hs_ = slice(64 * hh, 64 * hh + 64)
                            masks = []
                            if g >= 2 * R:
                                rr = g - 2 * R
                                if rr == 1:
                                    masks.append((0, 128, None, 0))
                                masks.append((rr * 128, rr * 128 + 128, cstt[:, C_SBM + r * 128:C_SBM + (r + 1) * 128], 0))
                            lanes.append(dict(idx=hh, W=NT,
                                              z=[(kt_[hs_, r, :], qTb[hs_, p, :], 0, NT, [K(kt_), "qTb"])],
                                              masks=masks,
                                              pv=[(vt_[:, r, hs_], 0, NT, po[hs_, :NT], [K(vt_)], "po")]))
                        sl_.append(sb_stages(lanes, 128, si == 0, si == len(steps) - 1, si % 2, pre))
                    if p == 1:
                        run_steps(sl_, prefix_items); prefix_items = []
                    else:
                        run_steps(sl_)
                    tt('dve', mixT[:, 4 + p, :], po[:, :NT], gate_b[:, p, :], ALU.mult, ["po", "gate_b"], ["mixT"])
            else:
                for p in range(2):
                    cpy('pool', knew[:, p, :, 0:64], kTl[:, p, :].rearrange("x (b t) -> x b t", b=4), ["kTl"], ["knew"])
                for h in range(4):
                    hs_ = slice(64 * (h % 2), 64 * (h % 2) + 64)
                    cpy('pool', qm[hs_, h, :], qTb[hs_, h // 2, :], ["qTb"], ["qm"])
                for b in range(4):
                    kst, vst = stg[0], stg[1]
                    P.dma(kst[:, :].rearrange("q (n d) -> q n d", n=8), csk[l, b].rearrange("(n q) d -> q n d", q=128), (), [K(kst)])
                    P.dma(vst[:, :].rearrange("q (n d) -> q n d", n=8), csv[l, b].rearrange("(n q) d -> q n d", q=128), (), [K(vst)])
                    kTp = bstg[0][:, :].rearrange("x (p t) -> x p t", p=2)
                    vp = bstg[1][:, :].rearrange("q (n d) -> q n d", n=8)
                    cpy('pool', bstg[1][:], vst[:], [K(vst)], [K(bstg[1])])
                    for p in range(2):
                        for n4 in range(2):
                            p_, pk = nextpg()
                            for n_ in range(4):
                                n = n4 * 4 + n_
                                tp(p_[:, n_ * 128:(n_ + 1) * 128], kst[:, n * 256 + p * 128: n * 256 + (p + 1) * 128], idf,
                                   [K(kst), "cstt"], [pk])
                            cpy('act', kTp[:, p, n4 * 512:(n4 + 1) * 512], p_[:, :], [pk], [K(bstg[0])])
                    if STOP == 150:
                        P.finish(); return nc, P, es
                    bs_ = slice(b * 64, b * 64 + 64)
                    steps = [8] + list(range(7, -1, -1))
                    sl_ = []
                    for si, n in enumerate(steps):
                        nk = 128
                        z = []; pv = []
                        for h in range(4):
                            hh, p = h % 2, h // 2
                            hs_ = slice(64 * hh, 64 * hh + 64)
                            if n == 8:
                                z.append((knew[:, p, b, :], qm[:, h, bs_], h * 64, h * 64 + 64, ["knew", "qm"]))
                                pv.append((vnew[:, b, h * 64:(h + 1) * 64], h * 64, h * 64 + 64,
                                           (po, pc[1])[p][hs_, 0:64], ["vnew"], ("po", "pc1")[p]))
                            else:
                                z.append((kTp[:, p, n * 128:(n + 1) * 128], qm[:, h, bs_], h * 64, h * 64 + 64, [K(bstg[0]), "qm"]))
                                pv.append((vp[:, n, h * 64:(h + 1) * 64], h * 64, h * 64 + 64, (po, pc[1])[p][hs_, 0:64], [K(bstg[1])], ("po", "pc1")[p]))
                        masks = [(0, 256, cstt[:, C_SM:C_SM + 64].unsqueeze(1).to_broadcast([128, 4, 64]), 4)] if n == 8 else []
                        sl_.append(sb_stages([dict(idx=0, W=256, z=z, masks=masks, pv=pv)], nk, si == 0, si == len(steps) - 1, si % 2))
                    run_steps(sl_)
                    for p in range(2):
                        tt('dve', mixT[:, 4 + p, bs_], (po, pc[1])[p][:, 0:64], gate_b[:, p, bs_], ALU.mult, [("po", "pc1")[p], "gate_b"], ["mixT"])

            if not samp:
                for it_ in prefix_items:
                    it_()
                prefix_items = []
            else:
                for b in range(4):
                    P.dma(Sin_s[:, :].rearrange("p (h v) -> p h v", h=4), sth[l, b].rearrange("h k v -> k h v"), (), [K(Ssel[0])])
                    cpy('act', Sbf[:, b, :], Sin_s[:], [K(Ssel[0])], ["Sbf"])
                    sv = Sin_s[:, :].rearrange("p (h v) -> p h v", h=4)
                    tt('pool', sv, sv, Dt[:, b, :].unsqueeze(2).to_broadcast([128, 4, 128]), ALU.mult, [K(Ssel[0]), "Dt"], [K(Ssel[0])])
                    tt('pool', Sin_s[:], Sin_s[:], Sst[:, b, :], ALU.add, [K(Ssel[0]), "Sst"], [K(Ssel[0])])
                    P.dma(o_shs[l, b].rearrange("h k v -> k h v"), Sin_s[:, :].rearrange("p (h v) -> p h v", h=4), [K(Ssel[0])], ())
            for h in range(4):
                p_, pk = nextpg()
                for ch in range(4):
                    cs_ = slice(ch * 64, (ch + 1) * 64)
                    mm(p_[:, cs_], v_tok[:, ch, h * 128:(h + 1) * 128], attT[:, h, cs_], True, False, ["v_tok", "attT%d" % h], [pk])
                    mm(p_[:, cs_], Sbf[:, ch, h * 128:(h + 1) * 128], qG[:, h, cs_], False, True, ["Sbf", "qG%d" % h], [pk])
                q = sqb[h % 2]
                act(q[:], p_[:, :NT], AF.Square, [pk], [K(q)])
                s_, sk_ = nextpg()
                mm(s_[:, :NT], oneb[:], q[:], True, True, [K(q), "oneb"], [sk_])
                act(lnv[:], s_[:, :NT], AF.Ln, [sk_, "epst"], ["lnv"], bias=epst[:, 0:1], scale=1.0 / 128)
                act(rstd[:], lnv[:], AF.Exp, ["lnv"], ["rstd"], scale=-0.5)
                stt(t1[:], p_[:, :NT], prmt[:, pb + P_OG + h:pb + P_OG + h + 1], rstd[:], ALU.mult, ALU.mult,
                    [pk, "prmt", "rstd"], ["t1"])
                tt('pool', mixT[:, h, :], t1[:], gate_a[:, h, :], ALU.mult, ["t1", "gate_a"], ["mixT"])

            if STOP == 13:
                P.finish(); return nc, P, es
            if samp:
                emit_pooling()
            if STOP == 15:
                P.finish(); return nc, P, es
            linear_norm_res(0, l, mixT, "mixT", pb + P_NPOST)
            rmsnorm_to_bf(xT, "xT", NT, pb + P_XPRE, hT, "hT")
            for g2 in range(2):
                wt, wk = load_wg(wsq_bf[l][1][:, :, g2 * 512:(g2 + 1) * 512], WQK(l, 1))
                for oc in range(4):
                    p_, pk = proj_fm(wt, wk, oc * 128, hT, "hT", NT)
                    ts('dve', qxT[:, g2 * 4 + oc, :], p_[:, :NT], 1.0 / 16, None, ALU.mult, None, [pk], ["qxT"])
            TS = 64 if samp else 128
            for st_ in range(NT // TS):
                tsl = slice(st_ * TS, (st_ + 1) * TS)
                if samp:
                    kst, vst = stg[0], stg[1]
                    P.dma(kst[:, :].rearrange("q (n d) -> q n d", n=2), cmk[l, st_].rearrange("(n q) d -> q n d", q=128), (), [K(kst)])
                    P.dma(vst[:, :].rearrange("q (n d) -> q n d", n=2), cmv[l, st_].rearrange("(n q) d -> q n d", q=128), (), [K(vst)])
                    cpy('pool', memV[:].rearrange("q n d -> q (n d)"), vst[:], [K(vst)], ["memV"])
                    for mb in range(2):
                        for hf in range(2):
                            p_, pk = nextpg()
                            for c4 in range(4):
                                c = hf * 4 + c4
                                tp(p_[:, c4 * 128:(c4 + 1) * 128], kst[:, mb * 1024 + c * 128: mb * 1024 + (c + 1) * 128], idf,
                                   [K(kst), "cstt"], [pk])
                            cpy('act', memKT[:, hf * 4:hf * 4 + 4, mb * 128:(mb + 1) * 128],
                                p_[:, :].rearrange("p (c t) -> p c t", c=4), [pk], ["memKT"])
                pt_, ptk = pz[0], "pz0"
                ptb16 = pt_[:, :].bitcast(BF16)
                for hx in range(4):
                    p_, pk = nextpg()
                    mm(p_[:TS, :256], qxT[:, 2 * hx, tsl], memKT[:, 2 * hx, :], True, False, ["qxT", "memKT"], [pk])
                    mm(p_[:TS, :256], qxT[:, 2 * hx + 1, tsl], memKT[:, 2 * hx + 1, :], False, True, ["qxT", "memKT"], [pk])
                    P.op('dve', lambda e, p_=p_, TS=TS: e.reduce_max(sm4[:TS, 0:1], p_[:TS, :256], AX.X), [pk], ["sm4"])
                    ts('dve', sm4[:TS, 1:2], sm4[:TS, 0:1], -1.0, None, ALU.mult, None, ["sm4"], ["sm4"])
                    act(pexp[:TS, :], p_[:TS, :256], AF.Exp, [pk, "sm4"], ["pexp", "sm4"], bias=sm4[:TS, 1:2], accum=sm4[:TS, 2:3])
                    P.op('dve', lambda e, TS=TS: e.reciprocal(sm4[:TS, 3:4], sm4[:TS, 2:3]), ["sm4"], ["sm4"])
                    ts('dve', pbf[:TS, :], pexp[:TS, :], sm4[:TS, 3:4], None, ALU.mult, None, ["pexp", "sm4"], ["pbf"])
                    for mc in range(2):
                        j_ = hx * 2 + mc
                        tp(ptb16[:, j_ * 128:j_ * 128 + TS], pbf[:TS, mc * 128:(mc + 1) * 128], idb[:TS, :TS], ["pbf", "idb"], [ptk])
                cpy('act', pT_sb[:, :, :TS], ptb16[:, :1024].rearrange("p (j t) -> p j t", j=8)[:, :, :TS], [ptk], ["pT_sb"])
                for hf in range(2):
                    p_, pk = nextpg()
                    for o4 in range(4):
                        oc = hf * 4 + o4; hx = oc // 2
                        for mc in range(2):
                            mm(p_[:, o4 * TS:(o4 + 1) * TS], memV[:, mc, oc * 128:(oc + 1) * 128], pT_sb[:, 2 * hx + mc, :TS],
                               mc == 0, mc == 1, ["memV", "pT_sb"], [pk])
                    cpy('dve', oxT[:, hf * 4:hf * 4 + 4, tsl], p_[:, :4 * TS].rearrange("p (o t) -> p o t", o=4), [pk], ["oxT"])
            linear_norm_res(2, l, oxT, "oxT", pb + P_XPOST)
            if STOP == 16:
                P.finish(); return nc, P, es
            if l < NLAY - 1:
                P.dma(xs[:, :, t0:t0 + NT], xT[:], ["xT"], ["xs%d" % R])
            else:
                yd = y_s if samp else y_p
                for blk in range(2):
                    for hf in range(2):
                        p_, pk = nextpg()
                        for c4 in range(4):
                            c = hf * 4 + c4
                            tp(p_[:, c4 * 128:(c4 + 1) * 128], xT[:, c, blk * 128:(blk + 1) * 128], idf, ["xT", "cstt"], [pk])
                        cpy('act' if hf else 'dve', ystage[:, hf * 512:(hf + 1) * 512], p_[:, :], [pk], [K(xin[1])])
                    P.dma(yd[ot0 + blk * 128: ot0 + (blk + 1) * 128, :], ystage[:], [K(xin[1])], ())

    P.finish()
    return nc, P, es


def _prepend_mask_copy(P):
    pass


_CACHE = {}


def _consts(core):
    j = core % 4
    c = np.zeros((128, NCST), np.float32)
    c[:, C_ID:C_ID + 128] = np.eye(128, dtype=np.float32)
    jj, ss = np.meshgrid(np.arange(128), np.arange(128), indexing="ij")
    c[:, C_TRI:C_TRI + 128] = (jj >= ss).astype(np.float32)
    for r in range(4):
        if r < j:
            m = np.ones((128, 128), np.float32)
        elif r == j:
            m = (jj < ss).astype(np.float32)
        else:
            m = np.zeros((128, 128), np.float32)
        c[:, C_SBM + r * 128:C_SBM + (r + 1) * 128] = m
    k6, q6 = np.meshgrid(np.arange(64), np.arange(64), indexing="ij")
    c[:64, C_SM:C_SM + 64] = (k6 < q6).astype(np.float32)
    c[:64, C_HM:C_HM + 64] = (k6 <= q6).astype(np.float32)
    for r in range(3):
        c[:, C_SEL + r] = 1.0 if r == j - 1 else 0.0
    c[:, C_SEL + 3] = 1.0 if j == 0 else 0.0
    for r in range(4):
        c[:, C_SEL + 4 + r] = 1.0 if r == j else 0.0
    wins = {(0, 0): 2, (0, 1): 4, (1, 0): 8, (1, 1): 16}
    pos = np.arange(128)
    for cc in range(2):
        for gg in range(2):
            w = wins[(cc, gg)]
            inv = 1.0 / np.minimum(pos + 1, w) if j == 0 else np.full(128, 1.0 / w)
            c[64 * gg:64 * gg + 64, C_INV + cc * 128:C_INV + (cc + 1) * 128] = inv[None, :].astype(np.float32)
            c[64 * gg:64 * gg + 64, C_W + cc] = 1.0 / w
    rst = np.ones(NT, np.float32); rst[::64] = 0.0
    c[:, C_RST:C_RST + NT] = rst[None, :]
    return c


def kernel(x_prompt, x_sample, mem_prompt, cache_sb_k, cache_sb_v, state_hgrn, state_pool,
           cache_mem_k, cache_mem_v, norm_mix_pre, norm_mix_post, w_in, hgrn_lb_logits,
           hgrn_onorm_g, w_pool, pool_scale, w_out, norm_x_pre, norm_x_post, norm_mem,
           w_xq, w_xk, w_xv, w_xo):
    f = lambda a: np.ascontiguousarray(np.asarray(a, dtype=np.float32))
    x_prompt, x_sample, mem_prompt = f(x_prompt), f(x_sample), f(mem_prompt)
    if "nc" not in _CACHE:
        _CACHE["nc"] = build_program()
        _CACHE["nc"][1].build()
    nc = _CACHE["nc"][0]
    prm = np.zeros((128, 2 * PRM_L), np.float32)
    for l in range(2):
        b = l * PRM_L
        for off, arr in ((P_NPRE, norm_mix_pre), (P_NPOST, norm_mix_post), (P_XPRE, norm_x_pre),
                         (P_XPOST, norm_x_post), (P_NMEM, norm_mem)):
            prm[:, b + off:b + off + 8] = f(arr)[l].reshape(8, 128).T
        prm[:, b + P_OG:b + P_OG + 4] = f(hgrn_onorm_g)[l].reshape(4, 128).T
        prm[:, b + P_PS:b + P_PS + 2] = f(pool_scale)[l].reshape(2, 128).T
    lbl = np.concatenate([f(hgrn_lb_logits)[l].reshape(4, 128).T for l in range(2)], axis=1)
    wp = f(w_pool).reshape(2, 2, 2, 64, 64)
    wpl = np.ascontiguousarray(wp.transpose(2, 3, 0, 1, 4).reshape(128, 2 * 2 * 64))
    in_maps = []
    for c in range(8):
        b, j = c // 4, c % 4
        sl = slice(4 * c, 4 * c + 4)
        in_maps.append({
            "x_p": np.ascontiguousarray(x_prompt[b].reshape(64, 128, 1024)[j::4].reshape(TOKP, 1024)),
            "x_s": np.ascontiguousarray(x_sample[sl].reshape(NT, 1024)),
            "memp": mem_prompt[b],
            "csk": np.ascontiguousarray(f(cache_sb_k)[:, sl].reshape(2, 4, 1024, 256)),
            "csv": np.ascontiguousarray(f(cache_sb_v)[:, sl].reshape(2, 4, 1024, 256)),
            "sth": np.ascontiguousarray(f(state_hgrn)[:, sl]),
            "stp": np.ascontiguousarray(f(state_pool)[:, sl]),
            "cmk": np.ascontiguousarray(f(cache_mem_k)[:, sl].reshape(2, 4, 256, 1024)),
            "cmv": np.ascontiguousarray(f(cache_mem_v)[:, sl].reshape(2, 4, 256, 1024)),
            "w_in": f(w_in), "w_out": f(w_out), "w_xq": f(w_xq), "w_xo": f(w_xo), "w_xk": f(w_xk), "w_xv": f(w_xv),
            "prm": prm, "lbl": np.ascontiguousarray(lbl), "wpl": wpl, "cst": _consts(c),
        })
    _r = run_bass_kernel_spmd(nc, in_maps, core_ids=list(range(8)), **({'trace': True} if os.environ.get('KDBG_TRACE') else {}))
    if os.environ.get('KDBG_TRACE'):
        print('EXEC_NS', _r.exec_time_ns)
    res = _r.results
    yp = np.zeros((2, 64, 128, 1024), np.float32)
    pk = np.zeros((2, 2, 64, 128, 256), np.float32); pv = np.zeros_like(pk)
    for c in range(8):
        b, j = c // 4, c % 4
        yp[b, j::4] = res[c]["y_p"].reshape(16, 128, 1024)
        pk[:, b, j::4] = res[c]["o_pk"].reshape(2, 16, 128, 256)
        pv[:, b, j::4] = res[c]["o_pv"].reshape(2, 16, 128, 256)
    cat = lambda k, shp: np.concatenate([res[c][k].reshape(shp) for c in range(8)], axis=1)
    ys = np.concatenate([res[c]["y_s"].reshape(4, 64, 1024) for c in range(8)], axis=0)
    return (yp.reshape(2, 8192, 1024), ys,
            pk.reshape(2, 2, 8192, 4, 64), pv.reshape(2, 2, 8192, 4, 64),
            np.stack([res[0]["o_phs"], res[4]["o_phs"]], axis=1),
            np.stack([res[3]["o_pps"], res[7]["o_pps"]], axis=1),
            np.stack([res[0]["o_pmk"], res[4]["o_pmk"]], axis=1).reshape(2, 2, 256, 4, 256),
            np.stack([res[0]["o_pmv"], res[4]["o_pmv"]], axis=1).reshape(2, 2, 256, 4, 256),
            cat("o_sk", (2, 4, 64, 4, 64)), cat("o_sv", (2, 4, 64, 4, 64)),
            cat("o_shs", (2, 4, 4, 128, 128)), cat("o_sps", (2, 4, 15, 256)))
```

```python
import os
import numpy as np
from contextlib import ExitStack
import concourse.bass as bass
import concourse.mybir as mybir
from concourse.bass_utils import run_bass_kernel_spmd

F32 = mybir.dt.float32
BF16 = mybir.dt.bfloat16
AF = mybir.ActivationFunctionType
ALU = mybir.AluOpType
AX = mybir.AxisListType

DEPTH = 2
NT = 256
NRP = 8
NR = NRP + 1
TOKP = NRP * NT
TOK = TOKP + NT
EPS = 1e-6
SROWS = 512
TROWS = 32
NDMA = 24
SAME_ENGINE_NOSYNC = ('pe',)

PRM_L = 50
P_NPRE, P_NPOST, P_XPRE, P_XPOST, P_NMEM, P_LB, P_OG, P_PS = 0, 8, 16, 24, 32, 40, 44, 48
C_ID = 0
C_TRI = 128
C_SBM = 256
C_SM = 768
C_HM = 832
C_SEL = 896
C_INV = 904
C_RST = 1160
C_W = 1416
NCST = 1418


class Prog:
    ENG = ('pe', 'act', 'dve', 'pool', 'sp')

    def __init__(self, nc, es):
        self.nc, self.es = nc, es
        self.streams = {e: [] for e in self.ENG}
        self.count = {e: 0 for e in self.ENG}
        self.waited = {e: {} for e in self.ENG}
        self.bufs = {}
        self.sems = {}
        for e in ('pe', 'act', 'dve', 'pool'):
            self.sems[e] = es.enter_context(nc.semaphore('s_' + e))
        self.dma_cnt = [0] * NDMA
        self.dma_next = 0
        for i in range(NDMA):
            self.sems['dq%d' % i] = es.enter_context(nc.semaphore('dq%d' % i))
        self.ncc = 0

    def _deps(self, eng, reads, writes, extra=()):
        deps = list(extra)
        for b in reads:
            st = self.bufs.get(b)
            if st and st[0]:
                deps.append(st[0])
            if st and b[0] == 'p' and b[1] in 'gzco' and (b[2:3].isdigit() or b == 'po'):
                deps.extend(st[1].items())
        for b in writes:
            st = self.bufs.get(b)
            if st:
                if st[0]:
                    deps.append(st[0])
                deps.extend(st[1].items())
        w = self.waited[eng]
        for (sk, v) in deps:
            if sk == eng and eng in SAME_ENGINE_NOSYNC:
                continue
            if w.get(sk, 0) < v:
                w[sk] = v
                self.streams[eng].append(('wait', sk, v))

    def _update(self, tok, reads, writes):
        for b in reads:
            st = self.bufs.setdefault(b, [None, {}])
            if st[1].get(tok[0], 0) < tok[1]:
                st[1][tok[0]] = tok[1]
        for b in writes:
            self.bufs[b] = [tok, {}]

    def op(self, eng, fn, reads=(), writes=()):
        self._deps(eng, reads, writes)
        self.count[eng] += 1
        tok = (eng, self.count[eng])
        self.streams[eng].append(('op', fn))
        self._update(tok, reads, writes)

    def dma(self, out, in_, reads=(), writes=(), slow=False):
        i = self.dma_next
        self.dma_next = (i + 1) % NDMA
        sk = 'dq%d' % i
        prev = self.dma_cnt[i]
        self._deps('sp', reads, writes, extra=[(sk, prev)] if prev else [])
        self.dma_cnt[i] = prev + 16
        self.streams['sp'].append(('dma', out, in_, sk, slow))
        self._update((sk, prev + 16), reads, writes)

    def allgather(self, in_ap, out_ap, reads, writes):
        sk = 'cc%d' % self.ncc
        self.ncc += 1
        self.sems[sk] = self.es.enter_context(self.nc.semaphore(sk))
        self._deps('pool', reads, writes)
        self.streams['pool'].append(('cc', in_ap, out_ap, sk))
        self._update((sk, 1), reads, writes)

    def finish(self):
        for i in range(NDMA):
            if self.dma_cnt[i]:
                self.streams['sp'].append(('wait', 'dq%d' % i, self.dma_cnt[i]))

    def build(self):
        nc = self.nc
        engs = {'pe': 'tensor', 'act': 'scalar', 'dve': 'vector', 'pool': 'gpsimd', 'sp': 'sync'}
        with nc.Block() as block:
            for ename, bname in engs.items():
                stream = self.streams[ename]
                sems = self.sems

                def body(eng, stream=stream, ename=ename):
                    for it in stream:
                        if it[0] == 'wait':
                            eng.wait_ge(sems[it[1]], it[2])
                        elif it[0] == 'op':
                            it[1](eng).then_inc(sems[ename], 1)
                        elif it[0] == 'dma':
                            if it[4]:
                                with nc.allow_non_contiguous_dma(reason="small transposing dma"):
                                    eng.dma_start(out=it[1], in_=it[2]).then_inc(sems[it[3]], 16)
                            else:
                                eng.dma_start(out=it[1], in_=it[2]).then_inc(sems[it[3]], 16)
                        elif it[0] == 'cc':
                            eng.collective_compute(
                                "AllGather", ALU.bypass,
                                replica_groups=[[0, 1, 2, 3], [4, 5, 6, 7]],
                                ins=[it[1]], outs=[it[2]]).then_inc(sems[it[3]])
                getattr(block, bname)(body)


def build_program():
    nc = bass.Bass("TRN2", target_bir_lowering=False)
    es = ExitStack()
    P = Prog(nc, es)

    def din(name, shape, dt=F32):
        return nc.dram_tensor(name, list(shape), dt, kind="ExternalInput").ap()

    def dout(name, shape, dt=F32):
        return nc.dram_tensor(name, list(shape), dt, kind="ExternalOutput").ap()

    def dscr(name, shape, dt=F32):
        return nc.dram_tensor(name, list(shape), dt).ap()

    x_p = din("x_p", [TOKP, 1024]); x_s = din("x_s", [NT, 1024]); memp = din("memp", [256, 1024])
    csk = din("csk", [2, 4, 1024, 256]); csv = din("csv", [2, 4, 1024, 256])
    sth = din("sth", [2, 4, 4, 128, 128]); stp = din("stp", [2, 4, 15, 256])
    cmk = din("cmk", [2, 4, 256, 1024]); cmv = din("cmv", [2, 4, 256, 1024])
    w_in = din("w_in", [2, 1024, 3584])
    wsq = [din(n, [2, 1024, 1024]) for n in ("w_out", "w_xq", "w_xo", "w_xk", "w_xv")]
    prm = din("prm", [128, 2 * PRM_L]); lbl = din("lbl", [128, 8]); wpl = din("wpl", [128, 2 * 2 * 64])
    cst = din("cst", [128, NCST])

    y_p = dout("y_p", [TOKP, 1024]); y_s = dout("y_s", [NT, 1024])
    o_pk = dout("o_pk", [2, TOKP, 256]); o_pv = dout("o_pv", [2, TOKP, 256])
    o_phs = dout("o_phs", [2, 4, 128, 128]); o_pps = dout("o_pps", [2, 15, 256])
    o_pmk = dout("o_pmk", [2, 256, 1024]); o_pmv = dout("o_pmv", [2, 256, 1024])
    o_sk = dout("o_sk", [2, NT, 256]); o_sv = dout("o_sv", [2, NT, 256])
    o_shs = dout("o_shs", [2, 4, 4, 128, 128]); o_sps = dout("o_sps", [2, 4, 15, 256])

    win_bf = [dscr("win_bf%d" % l, [128, 8, 3584], BF16) for l in range(2)]
    wsq_bf = [[dscr("wsq_bf%d_%d" % (l, m), [128, 8, 1024], BF16) for m in range(5)] for l in range(2)]
    xs = dscr("xs", [128, 8, TOK])
    kv_loc = [[dscr("kvl%d_%d" % (l, r), [512, 256], BF16) for r in range(NRP)] for l in range(2)]
    kv_g = [[dscr("kvg%d_%d" % (l, r), [2048, 256], BF16) for r in range(NRP)] for l in range(2)]
    s_loc = [[dscr("sl%d_%d" % (l, r), [SROWS, 512]) for r in range(NRP)] for l in range(2)]
    s_g = [[dscr("sg%d_%d" % (l, r), [4 * SROWS, 512]) for r in range(NRP)] for l in range(2)]
    t_loc = [[dscr("tl%d_%d" % (l, r), [TROWS, 512]) for r in range(NRP)] for l in range(2)]
    t_g = [[dscr("tg%d_%d" % (l, r), [4 * TROWS, 512]) for r in range(NRP)] for l in range(2)]

    def sb(name, shape, dt=F32):
        return es.enter_context(nc.sbuf_tensor(name, list(shape), dt))

    def ps(name, shape=(128, 512), dt=F32):
        return es.enter_context(nc.psum_tensor(name, list(shape), dt))

    cstt = sb("cstt", [128, NCST]); prmt = sb("prmt", [128, 2 * PRM_L]); lbt = sb("lbt", [128, 8])
    lbv = sb("lbv", [128, 2, 4]); omlv = sb("omlv", [128, 2, 4]); lbtmp = sb("lbtmp", [128, 3, 4])
    wpt = sb("wpt", [128, 2, 2, 64]); wpb = sb("wpb", [128, 2, 2, 64], BF16)
    idb = sb("idb", [128, 128], BF16); trib = sb("trib", [128, 128], BF16); oneb = sb("oneb", [128, 128], BF16)
    epst = sb("epst", [128, 1])
    stg = [sb("stg%d" % i, [128, 2048]) for i in range(2)]
    bstg = [sb("bstg%d" % i, [128, 2048], BF16) for i in range(2)]
    wg = [sb("wg%d" % i, [128, 8, 512], BF16) for i in range(2)]
    xT = sb("xT", [128, 8, NT]); hT = sb("hT", [128, 8, NT], BF16)
    xin = [sb("xin%d" % i, [128, 1024]) for i in range(2)]
    sqb = [sb("sqb%d" % i, [128, NT], BF16) for i in range(2)]
    lnv = sb("lnv", [128, NT]); rstd = sb("rstd", [128, NT])
    q_sb = sb("q_sb", [128, 4, NT]); fgate = sb("fgate", [128, 4, NT]); gate_a = sb("gate_a", [128, 4, NT], BF16)
    v_tok = sb("v_tok", [64, 4, 512], BF16); qTb = sb("qTb", [128, 2, NT], BF16); kTl = sb("kTl", [128, 2, NT], BF16)
    vnew = sb("vnew", [128, 4, 256], BF16); knew = sb("knew", [128, 2, 4, 128], BF16); qm = sb("qm", [128, 4, NT], BF16); ktok = sb("ktok", [128, 2, 256]); vtok = sb("vtok", [128, 2, 256]); vbl = sb("vbl", [128, 2, 256], BF16)
    gate_b = sb("gate_b", [128, 2, NT], BF16); gate_c = sb("gate_c", [128, 2, NT], BF16)
    Ep = sb("Ep", [128, 2, 2, 143]); Es = sb("Es", [128, 2, 4, 79])
    Sst = sb("Sst", [128, 4, 512])
    hs = {n: sb("hs_" + n, [128, NT]) for n in ("f", "lg", "kk", "G", "Gr", "Gl", "e1", "e2")}
    qt2 = [sb("qt%d" % i, [128, NT], BF16) for i in range(2)]; kt2 = [sb("kt%d" % i, [128, NT], BF16) for i in range(2)]; ke2 = [sb("ke%d" % i, [128, NT], BF16) for i in range(2)]
    qG = sb("qG", [128, 4, NT], BF16); Dt = sb("Dt", [128, 4, 4]); attT = sb("attT", [64, 4, NT], BF16)
    ketok2 = [sb("ketok%d" % i, [64, 4, 128], BF16) for i in range(2)]
    Sm = [sb("Sm%d" % i, [128, 512]) for i in range(2)]; Dg = sb("Dg", [128, 4, 16]); Srun = sb("Srun", [128, 512])
    Ssel = [sb("Ssel%d" % i, [128, 512]) for i in range(2)]; Sbf = sb("Sbf", [128, 4, 512], BF16)
    Sin_s = Ssel[0]
    Tg = sb("Tg", [128, 4, 60]); Tprev = sb("Tprev", [128, 2, 15])
    pa = [sb("pa%d" % i, [128, 640]) for i in range(4)]
    dT = sb("dT", [128, 2, NT], BF16); ptmp = sb("ptmp", [128, 128])
    e_sb = [[sb("e%d_%d" % (a, b), [128, NT]) for b in range(2)] for a in range(2)]
    sp_sb = [[sb("sp%d_%d" % (a, b), [128, NT], BF16) for b in range(2)] for a in range(2)]
    A_sb = [[sb("A%d_%d" % (a, b), [128, NT], BF16) for b in range(2)] for a in range(2)]
    Rsum = [sb("Rsum%d" % a, [128, NT]) for a in range(2)]; Rbf = [sb("Rbf%d" % a, [128, NT], BF16) for a in range(2)]
    kTg = [sb("kTg%d" % i, [128, 4, 128], BF16) for i in range(2)]; vg = [sb("vg%d" % i, [128, 4, 128], BF16) for i in range(2)]
    mixT = sb("mixT", [128, 8, NT], BF16); yT = sb("yT", [128, 8, NT]); qxT = sb("qxT", [128, 8, NT], BF16)
    oxT = sb("oxT", [128, 8, NT], BF16); pexp = sb("pexp", [128, 256]); pbf = sb("pbf", [128, 256], BF16)
    pT_sb = sb("pT_sb", [128, 8, 128], BF16); memKT = sb("memKT", [128, 8, 256], BF16); memV = sb("memV", [128, 2, 1024], BF16)
    sm4 = sb("sm4", [128, 4]); t1 = sb("t1", [128, NT]); ystage = xin[1]
    mT = xT; mhT = hT

    pg = [ps("pg%d" % i) for i in range(3)]
    pz = [ps("pz%d" % i) for i in range(2)]; pc = [ps("pc%d" % i) for i in range(2)]; po = ps("po")
    pgi = [0]

    def nextpg():
        pgi[0] = (pgi[0] + 1) % 3
        return pg[pgi[0]], "pg%d" % pgi[0]

    K = lambda t: t.name if hasattr(t, "name") else t
    WIK = lambda l: ["wi%d_%d_%d" % (l, kc, hf) for kc in range(8) for hf in range(2)]
    WQK = lambda l, m: ["wq%d_%d_%d" % (l, m, k) for k in range(4)]

    def mm(out, lhsT, rhs, start, stop, r, w):
        P.op('pe', lambda e: e.matmul(out, lhsT, rhs, start=start, stop=stop), r, w)

    def tp(out, in_, ident, r, w):
        P.op('pe', lambda e: e.transpose(out, in_, ident), r, w)

    def act(out, in_, func, r, w, bias=None, scale=None, accum=None):
        kw = {}
        if bias is not None: kw['bias'] = bias
        if scale is not None: kw['scale'] = scale
        if accum is not None: kw['accum_out'] = accum
        P.op('act', lambda e: e.activation(out, in_, func, **kw), r, w)

    def tt(eng, out, in0, in1, op, r, w):
        P.op(eng, lambda e: e.tensor_tensor(out, in0, in1, op), r, w)

    def ts(eng, out, in0, s1, s2, op0, op1, r, w):
        if op1 is None:
            P.op(eng, lambda e: e.tensor_scalar(out, in0, s1, None, op0), r, w)
        else:
            P.op(eng, lambda e: e.tensor_scalar(out, in0, s1, s2, op0, op1), r, w)

    def stt(out, in0, scalar, in1, op0, op1, r, w):
        P.op('dve', lambda e: e.scalar_tensor_tensor(out, in0, scalar, in1, op0, op1), r, w)

    def cpy(eng, out, in_, r, w):
        if eng == 'act':
            P.op('act', lambda e: e.copy(out, in_), r, w)
        else:
            P.op(eng, lambda e: e.tensor_copy(out, in_), r, w)

    def mset(eng, out, val, w):
        P.op(eng, lambda e: e.memset(out, val), (), w)

    P.dma(cstt[:], cst, (), ["cstt"]); P.dma(prmt[:], prm, (), ["prmt"]); P.dma(lbt[:], lbl, (), ["lbt"])
    P.dma(wpt[:].rearrange("p a b c -> p (a b c)"), wpl, (), ["wpt"])
    cpy('dve', idb[:], cstt[:, C_ID:C_ID + 128], ["cstt"], ["idb"])
    cpy('dve', trib[:], cstt[:, C_TRI:C_TRI + 128], ["cstt"], ["trib"])
    cpy('dve', wpb[:], wpt[:], ["wpt"], ["wpb"])
    mset('pool', oneb[:], 1.0, ["oneb"]); mset('pool', epst[:], EPS, ["epst"])
    zt = sb("zt", [128, 52]); mset('pool', zt[:], 0.0, ["zt"])
    mset('pool', vnew[:], 0.0, ["vnew"]); mset('pool', knew[:], 0.0, ["knew"]); mset('pool', qm[:], 0.0, ["qm"])
    for l_ in range(2):
        for r_ in range(NRP):
            P.dma(t_loc[l_][r_][19:32, :].rearrange("r c -> (r c)").rearrange("(p b) -> p b", b=52), zt[:], ["zt"], ["sl_pad%d_%d" % (l_, r_)])
    mset('pool', Srun[:], 0.0, ["Srun"]); mset('pool', Tprev[:], 0.0, ["Tprev"])
    idf = cstt[:, C_ID:C_ID + 128]
    STOP = int(os.environ.get('KDBG_STOP', 99))
    if STOP <= 0:
        P.finish(); return nc, P, es
    act(lbtmp[:, 0, :], lbt[:, 0:4], AF.Exp, ["lbt"], ["lbtmp"])
    act(lbtmp[:, 1, :], lbt[:, 4:8], AF.Exp, ["lbt"], ["lbtmp"])
    tt('dve', lbtmp[:, 2, :], lbtmp[:, 0, :], lbtmp[:, 1, :], ALU.add, ["lbtmp"], ["lbtmp"])
    act(lbtmp[:, 2, :], lbtmp[:, 2, :], AF.Ln, ["lbtmp"], ["lbtmp"])
    act(lbtmp[:, 2, :], lbtmp[:, 2, :], AF.Exp, ["lbtmp"], ["lbtmp"], scale=-1.0)
    tt('dve', lbtmp[:, 0, :], lbtmp[:, 0, :], lbtmp[:, 2, :], ALU.mult, ["lbtmp"], ["lbtmp"])
    tt('dve', lbtmp[:, 1, :], lbtmp[:, 1, :], lbtmp[:, 2, :], ALU.mult, ["lbtmp"], ["lbtmp"])
    tt('dve', lbv[:, 0, :], lbtmp[:, 0, :], lbtmp[:, 0, :], ALU.subtract, ["lbtmp"], ["lbv"])
    tt('dve', lbtmp[:, 2, :], lbtmp[:, 0, :], lbtmp[:, 1, :], ALU.add, ["lbtmp"], ["lbtmp"])
    tt('dve', lbv[:, 1, :], lbtmp[:, 2, :], lbtmp[:, 0, :], ALU.subtract, ["lbtmp"], ["lbv"])
    ts('dve', lbv[:], lbv[:], 0.0, None, ALU.max, None, ["lbv"], ["lbv"])
    ts('dve', omlv[:], lbv[:], -1.0, 1.0, ALU.mult, ALU.add, ["lbv"], ["omlv"])

    if STOP <= 1:
        P.finish(); return nc, P, es
    ceng = ['dve', 'act', 'dve']
    wi = [0]

    def wcast(src, dst, n, wkey):
        i = wi[0]; wi[0] += 1
        s, b = stg[i % 2], bstg[i % 2]
        P.dma(s[:, :n], src, (), [K(s)])
        cpy(ceng[i % 3], b[:, :n], s[:, :n], [K(s)], [K(b)])
        P.dma(dst, b[:, :n], [K(b)], [wkey])

    for l in range(2):
        for kc in range(8):
            for hf in range(2):
                wcast(w_in[l, kc * 128:(kc + 1) * 128, hf * 1792:(hf + 1) * 1792],
                      win_bf[l][:, kc, hf * 1792:(hf + 1) * 1792], 1792, "wi%d_%d_%d" % (l, kc, hf))
        for m in range(5):
            for kc2 in range(4):
                wcast(wsq[m][l, kc2 * 256:(kc2 + 1) * 256, :].rearrange("(k p) n -> p k n", p=128),
                      wsq_bf[l][m][:, 2 * kc2:2 * kc2 + 2, :], 2048, "wq%d_%d_%d" % (l, m, kc2))

    if STOP <= 2:
        P.finish(); return nc, P, es
    wgi = [0]

    def load_wg(src, rkeys):
        i = wgi[0] % 2; wgi[0] += 1
        P.dma(wg[i][:], src, rkeys, [K(wg[i])])
        return wg[i], K(wg[i])

    def rmsnorm_to_bf(src, skey, n, gcol, dst, dkey):
        stp_, stk = nextpg()
        for c in range(8):
            q = sqb[c % 2]
            act(q[:, :n], src[:, c, :n], AF.Square, [skey], [K(q)])
            mm(stp_[:, :n], oneb[:], q[:, :n], c == 0, c == 7, [K(q), "oneb"], [stk])
        act(lnv[:, :n], stp_[:, :n], AF.Ln, [stk, "epst"], ["lnv"], bias=epst[:, 0:1], scale=1.0 / 1024)
        act(rstd[:, :n], lnv[:, :n], AF.Exp, ["lnv"], ["rstd"], scale=-0.5)
        for c in range(8):
            stt(dst[:, c, :n], src[:, c, :n], prmt[:, gcol + c:gcol + c + 1], rstd[:, :n], ALU.mult, ALU.mult,
                [skey, "prmt", "rstd"], [dkey])

    def proj_fm(wt, wk, col, rhsT, rkey, n):
        p_, pk = nextpg()
        for kc in range(8):
            mm(p_[:, :n], wt[:, kc, col:col + 128], rhsT[:, kc, :n], kc == 0, kc == 7, [wk, rkey], [pk])
        return p_, pk

    def linear_norm_res(m, l, rhsT, rkey, gcol):
        stp_ = pc[0]; stk = "pc0"
        for g2 in range(2):
            wt, wk = load_wg(wsq_bf[l][m][:, :, g2 * 512:(g2 + 1) * 512], WQK(l, m))
            for oc in range(4):
                c = g2 * 4 + oc
                p_, pk = proj_fm(wt, wk, oc * 128, rhsT, rkey, NT)
                cpy('act', yT[:, c, :], p_[:, :NT], [pk], ["yT"])
                q = sqb[c % 2]
                act(q[:], p_[:, :NT], AF.Square, [pk], [K(q)])
                mm(stp_[:, :NT], oneb[:], q[:], c == 0, c == 7, [K(q), "oneb"], [stk])
        act(lnv[:], stp_[:, :NT], AF.Ln, [stk, "epst"], ["lnv"], bias=epst[:, 0:1], scale=1.0 / 1024)
        act(rstd[:], lnv[:], AF.Exp, ["lnv"], ["rstd"], scale=-0.5)
        for c in range(8):
            stt(t1[:], yT[:, c, :], prmt[:, gcol + c:gcol + c + 1], rstd[:], ALU.mult, ALU.mult,
                ["yT", "prmt", "rstd"], ["t1"])
            tt('pool', xT[:, c, :], xT[:, c, :], t1[:], ALU.add, ["xT", "t1"], ["xT"])

    def load_tokmajor_T(src_rows, nblk, dst, dkey):
        for blk in range(nblk):
            xi = xin[blk % 2]
            P.dma(xi[:], src_rows(blk), (), [K(xi)])
            for hf in range(2):
                p_, pk = nextpg()
                for c4 in range(4):
                    c = hf * 4 + c4
                    tp(p_[:, c4 * 128:(c4 + 1) * 128], xi[:, c * 128:(c + 1) * 128], idf, [K(xi), "cstt"], [pk])
                cpy('act' if hf else 'dve', dst[:, hf * 4:hf * 4 + 4, blk * 128:(blk + 1) * 128],
                    p_[:, :].rearrange("p (c t) -> p c t", c=4), [pk], [dkey])

    def sb_stages(lanes, nk, first, last, par, pre=None):
        def s_z():
            if pre is not None:
                pre()
            for ln in lanes:
                a = ln['idx']
                for (lh, rh, c0, c1, rk) in ln['z']:
                    mm(pz[a][:nk, c0:c1], lh, rh, True, True, rk, ["pz%d" % a])

        def s_exp():
            for ln in lanes:
                a = ln['idx']; W = ln['W']; e_ = e_sb[a][par]
                act(e_[:nk, :W], pz[a][:nk, :W], AF.Exp, ["pz%d" % a], [K(e_)])

        def s_mask():
            for ln in lanes:
                a = ln['idx']; e_ = e_sb[a][par]
                for (c0, c1, mk, bc) in ln['masks']:
                    if mk is None:
                        mset('pool', e_[:nk, c0:c1], 0.0, [K(e_)])
                    elif bc:
                        tt('pool', e_[:nk, c0:c1].rearrange("p (h t) -> p h t", h=bc),
                           e_[:nk, c0:c1].rearrange("p (h t) -> p h t", h=bc), mk, ALU.mult, [K(e_), "cstt"], [K(e_)])
                    else:
                        tt('pool', e_[:nk, c0:c1], e_[:nk, c0:c1], mk, ALU.mult, [K(e_), "cstt"], [K(e_)])

        def s_ln():
            for ln in lanes:
                a = ln['idx']; W = ln['W']; e_, sp_ = e_sb[a][par], sp_sb[a][par]
                act(sp_[:nk, :W], e_[:nk, :W], AF.Ln, [K(e_)], [K(sp_)], bias=1.0)

        def s_cs():
            for ln in lanes:
                a = ln['idx']; W = ln['W']; sp_ = sp_sb[a][par]
                mm(pc[a][:nk, :W], trib[:nk, :nk], sp_[:nk, :W], True, first, [K(sp_), "trib"], ["pc%d" % a])
                if not first:
                    mm(pc[a][:nk, :W], oneb[:, :nk], Rbf[a][:, :W], False, True, [K(Rbf[a]), "oneb"], ["pc%d" % a])

        def s_exp2():
            for ln in lanes:
                a = ln['idx']; W = ln['W']
                act(pc[a][:nk, :W], pc[a][:nk, :W], AF.Exp, ["pc%d" % a], ["pc%d" % a], scale=-1.0)

        def s_A():
            for ln in lanes:
                a = ln['idx']; W = ln['W']; e_, A_ = e_sb[a][par], A_sb[a][par]
                tt('dve', A_[:nk, :W], pc[a][:nk, :W], e_[:nk, :W], ALU.mult, ["pc%d" % a, K(e_)], [K(A_)])

        def s_R():
            if last:
                return
            for ln in lanes:
                a = ln['idx']; W = ln['W']; sp_ = sp_sb[a][par]
                if first:
                    cpy('dve', Rsum[a][:nk, :W], sp_[:nk, :W], [K(sp_)], [K(Rsum[a])])
                else:
                    tt('dve', Rsum[a][:nk, :W], Rsum[a][:nk, :W], sp_[:nk, :W], ALU.add, [K(Rsum[a]), K(sp_)], [K(Rsum[a])])
                cpy('pool', Rbf[a][:, :W], Rsum[a][:, :W], [K(Rsum[a])], [K(Rbf[a])])

        def s_pv():
            for ln in lanes:
                a = ln['idx']; A_ = A_sb[a][par]
                for (lh, c0, c1, oap, rk, ok_) in ln['pv']:
                    mm(oap, lh, A_[:nk, c0:c1], first, last, rk + [K(A_)], [ok_])

        return [s_z, s_exp, s_mask, s_ln, s_cs, s_R, s_exp2, s_A, s_pv]

    SKEW = int(os.environ.get('KDBG_SKEW', 3))

    def run_steps(step_list, extras=None):
        n = len(step_list)
        ns = 9
        T = (n - 1) * SKEW + ns
        extras = list(extras) if extras else []
        ne = len(extras)
        for slot in range(T):
            for si in range(n):
                st = slot - si * SKEW
                if 0 <= st < ns:
                    step_list[si][st]()
            if ne:
                lo = T // 2
                if slot >= lo:
                    want = ((slot - lo + 1) * ne) // (T - lo)
                    while len(extras) > ne - want:
                        extras.pop(0)()
        while extras:
            extras.pop(0)()

    NLAY = int(os.environ.get('KDBG_LAYERS', DEPTH))
    for l in range(NLAY):
        pb = l * PRM_L
        load_tokmajor_T(lambda blk: memp[blk * 128:(blk + 1) * 128, :], 2, mT, "xT")
        if STOP <= 3:
            P.dma(xs[:, :, 0:256], xT[:], ["xT"], ()); P.finish(); return nc, P, es
        rmsnorm_to_bf(mT, "xT", 256, pb + P_NMEM, mhT, "hT")
        if STOP <= 4:
            P.dma(win_bf[0][:, :, 0:256], hT[:], ["hT"], ()); P.finish(); return nc, P, es
        for which, m in (("k", 3), ("v", 4)):
            if STOP == 5 and which == "v":
                break
            for g2 in range(2):
                wt, wk = load_wg(wsq_bf[l][m][:, :, g2 * 512:(g2 + 1) * 512], WQK(l, m))
                for mb in range(2):
                    p_, pk = nextpg()
                    for kc in range(8):
                        mm(p_[:, :512], mhT[:, kc, mb * 128:(mb + 1) * 128], wt[:, kc, :], kc == 0, kc == 7, [wk, "hT"], [pk])
                    cpy('act', ystage[:, :512], p_[:, :512], [pk], [K(xin[1])])
                    dst = (o_pmk if which == "k" else o_pmv)[l, mb * 128:(mb + 1) * 128, g2 * 512:(g2 + 1) * 512]
                    P.dma(dst, ystage[:, :512], [K(xin[1])], ())
                    if which == "v":
                        cpy('pool', memV[:, mb, g2 * 512:(g2 + 1) * 512], ystage[:, :512], [K(xin[1])], ["memV"])
                if which == "k" and STOP != 6:
                    for oc in range(4):
                        p_, pk = proj_fm(wt, wk, oc * 128, mhT, "hT", 256)
                        cpy('dve', memKT[:, g2 * 4 + oc, :], p_[:, :256], [pk], ["memKT"])

        if l == 1:
            mset('pool', Srun[:], 0.0, ["Srun"]); mset('pool', Tprev[:], 0.0, ["Tprev"])

        for R in [int(x) for x in os.environ['KDBG_ROUNDS'].split(',') if x != ''] if 'KDBG_ROUNDS' in os.environ else range(NR):
            samp = (R == NRP)
            t0 = R * NT
            NBK = 4 if samp else 2
            L = 64 if samp else 128
            E = Es if samp else Ep
            Ek = K(E)
            def load_past_kv(b, l=l):
                kst, vst = stg[0], stg[1]
                P.dma(kst[:, :].rearrange("q (n d) -> q n d", n=8), csk[l, b].rearrange("(n q) d -> q n d", q=128), (), [K(kst)])
                P.dma(vst[:, :].rearrange("q (n d) -> q n d", n=8), csv[l, b].rearrange("(n q) d -> q n d", q=128), (), [K(vst)])
                kTp_ = bstg[0][:, :].rearrange("x (p t) -> x p t", p=2)
                cpy('pool', bstg[1][:], vst[:], [K(vst)], [K(bstg[1])])
                for p in range(2):
                    for n4 in range(2):
                        p_, pk = nextpg()
                        for n_ in range(4):
                            n = n4 * 4 + n_
                            tp(p_[:, n_ * 128:(n_ + 1) * 128], kst[:, n * 256 + p * 128: n * 256 + (p + 1) * 128], idf,
                               [K(kst), "cstt"], [pk])
                        cpy('act', kTp_[:, p, n4 * 512:(n4 + 1) * 512], p_[:, :], [pk], [K(bstg[0])])
            if l == 0:
                src = x_s if samp else x_p
                tb = 0 if samp else t0
                load_tokmajor_T(lambda blk: src[tb + blk * 128: tb + (blk + 1) * 128, :], 2, xT, "xT")
            else:
                P.dma(xT[:], xs[:, :, t0:t0 + NT], ["xs%d" % R], ["xT"])
            rmsnorm_to_bf(xT, "xT", NT, pb + P_NPRE, hT, "hT")
            wt, wk = load_wg(win_bf[l][:, :, 512:1024], WIK(l))
            for h in range(4):
                p_, pk = proj_fm(wt, wk, h * 128, hT, "hT", NT)
                act(fgate[:, h, :], p_[:, :NT], AF.Sigmoid, [pk], ["fg%d" % h])
            wt, wk = load_wg(win_bf[l][:, :, 0:512], WIK(l))
            for h in range(4):
                p_, pk = proj_fm(wt, wk, h * 128, hT, "hT", NT)
                act(q_sb[:, h, :], p_[:, :NT], AF.Silu, [pk], ["q_sb"])
            if samp:
                load_past_kv(0)
            wt, wk = load_wg(win_bf[l][:, :, 1536:2048], WIK(l))
            for h in range(4):
                p_, pk = proj_fm(wt, wk, h * 128, hT, "hT", NT)
                act(gate_a[:, h, :], p_[:, :NT], AF.Silu, [pk], ["gate_a"])
            wt, wk = load_wg(win_bf[l][:, :, 2048:2560], WIK(l))
            for p in range(2):
                p_, pk = proj_fm(wt, wk, p * 128, hT, "hT", NT)
                ts('dve', qTb[:, p, :], p_[:, :NT], 0.125, None, ALU.mult, None, [pk], ["qTb"])
                p_, pk = proj_fm(wt, wk, 256 + p * 128, hT, "hT", NT)
                cpy('act', kTl[:, p, :], p_[:, :NT], [pk], ["kTl"])
            for blk in range(2):
                p_, pk = nextpg()
                for kc in range(8):
                    mm(p_[:, :256], hT[:, kc, blk * 128:(blk + 1) * 128], wt[:, kc, 256:512], kc == 0, kc == 7, [wk, "hT"], [pk])
                cpy('act', ktok[:, blk, :], p_[:, :256], [pk], ["ktok"])
            okd = (o_sk if samp else o_pk); ovd = (o_sv if samp else o_pv); ot0 = 0 if samp else t0
            P.dma(okd[l, ot0:ot0 + NT, :].rearrange("(b q) d -> q b d", q=128), ktok[:], ["ktok"], ())
            wt, wk = load_wg(win_bf[l][:, :, 2560:3072], WIK(l))
            for blk in range(2):
                p_, pk = nextpg()
                for kc in range(8):
                    mm(p_[:, :256], hT[:, kc, blk * 128:(blk + 1) * 128], wt[:, kc, 0:256], kc == 0, kc == 7, [wk, "hT"], [pk])
                cpy('act', vtok[:, blk, :], p_[:, :256], [pk], ["vtok"])
                cpy('pool', vbl[:, blk, :], vtok[:, blk, :], ["vtok"], ["vbl"])
            P.dma(ovd[l, ot0:ot0 + NT, :].rearrange("(b q) d -> q b d", q=128), vtok[:], ["vtok"], ())
            if samp:
                for b in range(4):
                    p_, pk = nextpg()
                    for kc in range(8):
                        mm(p_[:64, :256], hT[:, kc, b * 64:(b + 1) * 64], wt[:, kc, 0:256], kc == 0, kc == 7, [wk, "hT"], [pk])
                    cpy('dve', vnew[:64, b, :], p_[:64, :256], [pk], ["vnew"])
            for p in range(2):
                p_, pk = proj_fm(wt, wk, 256 + p * 128, hT, "hT", NT)
                act(gate_b[:, p, :], p_[:, :NT], AF.Silu, [pk], ["gate_b"])
            if not samp:
                kl, kg = kv_loc[l][R], kv_g[l][R]
                P.dma(kl[0:256, :].rearrange("(p q) t -> q p t", q=128), kTl[:], ["kTl"], ["kvl_k"])
                P.dma(kl[256:512, :].rearrange("(b q) d -> q b d", q=128), vbl[:], ["vbl"], ["kvl_v"])
                P.allgather(kl.opt(), kg.opt(), ["kvl_k", "kvl_v"], ["kvg%d_%d" % (l, R)])
            wt, wk = load_wg(win_bf[l][:, :, 3072:3584], WIK(l))
            for cc in range(2):
                p_, pk = proj_fm(wt, wk, cc * 128, hT, "hT", NT)
                cpy('dve', E[:, cc, :, 15:15 + L], p_[:, :NT].rearrange("p (b t) -> p b t", b=NBK), [pk], [Ek])
                p_, pk = proj_fm(wt, wk, 256 + cc * 128, hT, "hT", NT)
                act(gate_c[:, cc, :], p_[:, :NT], AF.Silu, [pk], ["gate_c"])
            wt, wk = load_wg(win_bf[l][:, :, 1024:1536], WIK(l))
            for ch in range(4):
                p_, pk = nextpg()
                for kc in range(8):
                    mm(p_[:64, :512], hT[:, kc, ch * 64:(ch + 1) * 64], wt[:, kc, :], kc == 0, kc == 7, [wk, "hT"], [pk])
                cpy('act' if ch % 2 else 'dve', v_tok[:, ch, :], p_[:64, :512], [pk], ["v_tok"])

            if STOP == 10:
                P.finish(); return nc, P, es
            c3 = lambda t: t[:, :].rearrange("p (c t) -> p c t", c=4)
            QS = 128.0 ** -0.5
            for g0 in (0, 2):
                hl = (g0, g0 + 1)
                T = {h: [hs[n] for n in (("f", "lg", "kk", "G") if h % 2 == 0 else ("Gr", "Gl", "e1", "e2"))] for h in hl}
                TK = {h: [K(t) for t in T[h]] for h in hl}
                QT = {h: (qt2[h % 2], kt2[h % 2], ke2[h % 2]) for h in hl}
                for h in hl:
                    ts('dve', fgate[:, h, :], fgate[:, h, :], omlv[:, l, h:h + 1], lbv[:, l, h:h + 1], ALU.mult, ALU.add,
                       ["fg%d" % h, "omlv", "lbv"], ["fg%d" % h])
                for h in hl:
                    act(T[h][0][:], fgate[:, h, :], AF.Ln, ["fg%d" % h], [TK[h][0]])
                for h in hl:
                    P.op('dve', lambda e, G=T[h][1], lg=T[h][0]: e.tensor_tensor_scan(G[:], cstt[:, C_RST:C_RST + NT], lg[:], 0.0, ALU.mult, ALU.add),
                         [TK[h][0], "cstt"], [TK[h][1]])
                    ts('pool', fgate[:, h, :], fgate[:, h, :], -1.0, 1.0, ALU.mult, ALU.add, ["fg%d" % h, TK[h][0]], ["fg%d" % h])
                for h in hl:
                    G = T[h][1]
                    tt('pool', c3(T[h][2]), c3(G), G[:, 31::64].unsqueeze(2).to_broadcast([128, 4, 64]), ALU.subtract, [TK[h][1]], [TK[h][2]])
                    tt('pool', c3(T[h][0]), c3(G), G[:, 63::64].unsqueeze(2).to_broadcast([128, 4, 64]), ALU.subtract, [TK[h][1], TK[h][0]], [TK[h][0]])
                for h in hl:
                    act(T[h][3][:], T[h][2][:], AF.Exp, [TK[h][2]], [TK[h][3]])
                for h in hl:
                    stt(QT[h][0][:], q_sb[:, h, :], QS, T[h][3][:], ALU.mult, ALU.mult, ["q_sb", TK[h][3]], [K(QT[h][0])])
                for h in hl:
                    act(T[h][2][:], T[h][2][:], AF.Exp, [TK[h][2]], [TK[h][2]], scale=-1.0)
                for h in hl:
                    tt('pool', QT[h][1][:], fgate[:, h, :], T[h][2][:], ALU.mult, ["fg%d" % h, TK[h][2]], [K(QT[h][1])])
                for h in hl:
                    act(T[h][3][:], T[h][1][:], AF.Exp, [TK[h][1]], [TK[h][3]])
                for h in hl:
                    stt(qG[:, h, :], q_sb[:, h, :], QS, T[h][3][:], ALU.mult, ALU.mult, ["q_sb", TK[h][3]], ["qG%d" % h])
                    cpy('pool', Dt[:, :, h], T[h][3][:, 63::64], [TK[h][3]], ["Dt"])
                for h in hl:
                    act(T[h][0][:], T[h][0][:], AF.Exp, [TK[h][0]], [TK[h][0]], scale=-1.0)
                for h in hl:
                    tt('pool', QT[h][2][:], fgate[:, h, :], T[h][0][:], ALU.mult, ["fg%d" % h, TK[h][0]], [K(QT[h][2])])
                pks = {}
                for h in hl:
                    p_, pk = nextpg(); pks[h] = (p_, pk)
                    for ch in range(4):
                        cs_ = slice(ch * 64, (ch + 1) * 64)
                        mm(p_[:64, cs_], QT[h][1][:, cs_], QT[h][0][:, cs_], True, True, [K(QT[h][1]), K(QT[h][0])], [pk])
                for h in hl:
                    p_, pk = pks[h]
                    tt('dve', attT[:, h, :].rearrange("p (c t) -> p c t", c=4), p_[:64, :NT].rearrange("p (c t) -> p c t", c=4),
                       cstt[:64, C_HM:C_HM + 64].unsqueeze(1).to_broadcast([64, 4, 64]), ALU.mult, [pk, "cstt"], ["attT%d" % h])
                for h in hl:
                    p_, pk = nextpg()
                    pb16 = p_[:, :].bitcast(BF16)
                    for ch in range(4):
                        tp(pb16[:64, ch * 128:(ch + 1) * 128], QT[h][2][:, ch * 64:(ch + 1) * 64], idb[:], [K(QT[h][2]), "idb"], [pk])
                    cpy('act', ketok2[h % 2][:].rearrange("p c k -> p (c k)"), pb16[:64, :512], [pk], [K(ketok2[h % 2])])
                for h in hl:
                    p_, pk = nextpg()
                    for ch in range(4):
                        mm(p_[:, ch * 128:(ch + 1) * 128], ketok2[h % 2][:, ch, :], v_tok[:, ch, h * 128:(h + 1) * 128], True, True,
                           [K(ketok2[h % 2]), "v_tok"], [pk])
                    cpy('act', Sst[:, :, h * 128:(h + 1) * 128], p_[:, :].rearrange("p (c v) -> p c v", c=4), [pk], ["Sst"])

            if STOP == 11:
                P.finish(); return nc, P, es
            if not samp:
                sl, sg_ = s_loc[l][R], s_g[l][R]
                P.dma(sl[0:512, :].rearrange("(c k) n -> k c n", k=128), Sst[:], ["Sst"], ["sl_s"])
                tlc, tgc = t_loc[l][R], t_g[l][R]
                P.dma(tlc[0:4, :].rearrange("r (a b) -> (r a) b", b=16), Dt[:].rearrange("p c h -> p (c h)"), ["Dt"], ["sl_d"])
                tl_ = tlc[4:19, :].rearrange("r c -> (r c)").rearrange("(p b) -> p b", b=60)
                for cc in range(2):
                    P.dma(tl_[:, cc * 30:(cc + 1) * 30].rearrange("p (b t) -> p b t", b=2),
                          E[:, cc, :, 128:143], [Ek], ["sl_t%d" % cc], slow=True)
                P.allgather(sl.opt(), sg_.opt(), ["sl_s"], ["sg%d_%d" % (l, R)])
                P.allgather(tlc.opt(), tgc.opt(), ["sl_d", "sl_t0", "sl_t1", "sl_pad%d_%d" % (l, R)], ["tg%d_%d" % (l, R)])

            if STOP == 12:
                P.finish(); return nc, P, es
            def emit_pooling():
                if not samp:
                    T4 = lambda r: Tg[:, r, :].rearrange("p (c b t) -> p c b t", c=2, b=2)
                    sel = lambda i: cstt[:, C_SEL + i:C_SEL + i + 1]
                    hal = E[:, :, :, 0:15]
                    ts('dve', hal, T4(0), sel(0), None, ALU.mult, None, ["Tg", "cstt"], [Ek])
                    stt(hal, T4(1), sel(1), hal, ALU.mult, ALU.add, ["Tg", "cstt", Ek], [Ek])
                    stt(hal, T4(2), sel(2), hal, ALU.mult, ALU.add, ["Tg", "cstt", Ek], [Ek])
                    stt(E[:, :, 1, 0:15], T4(3)[:, :, 0, :], sel(3), E[:, :, 1, 0:15], ALU.mult, ALU.add, ["Tg", "cstt", Ek], [Ek])
                    stt(E[:, :, 0, 0:15], Tprev[:], sel(3), E[:, :, 0, 0:15], ALU.mult, ALU.add, ["Tprev", "cstt", Ek], [Ek])
                    cpy('pool', Tprev[:], T4(3)[:, :, 1, :], ["Tg"], ["Tprev"])
                    if R == NRP - 1:
                        for cc in range(2):
                            P.dma(o_pps[l][:, cc * 128:(cc + 1) * 128].rearrange("t p -> p t"), E[:, cc, 1, 128:143], [Ek], (), slow=True)
                else:
                    for b in range(4):
                        for cc in range(2):
                            P.dma(E[:, cc, b, 0:15], stp[l, b][:, cc * 128:(cc + 1) * 128].rearrange("t p -> p t"), (), [Ek], slow=True)
                            P.dma(o_sps[l, b][:, cc * 128:(cc + 1) * 128].rearrange("t p -> p t"), E[:, cc, b, 64:79], [Ek], (), slow=True)
                LE = L + 15
                a2, a4, a8, a16 = (pa[i][:, :2 * NBK * (L + 14)].rearrange("p (c b t) -> p c b t", c=2, b=NBK) for i in range(4))
                tt('pool', a2[:, :, :, 0:LE - 1], E[:, :, :, 1:LE], E[:, :, :, 0:LE - 1], ALU.add, [Ek], ["pa0"])
                tt('pool', a4[:, :, :, 0:LE - 3], a2[:, :, :, 2:LE - 1], a2[:, :, :, 0:LE - 3], ALU.add, ["pa0"], ["pa1"])
                tt('pool', a8[:, :, :, 0:LE - 7], a4[:, :, :, 4:LE - 3], a4[:, :, :, 0:LE - 7], ALU.add, ["pa1"], ["pa2"])
                tt('pool', a16[:, :, :, 0:LE - 15], a8[:, :, :, 8:LE - 7], a8[:, :, :, 0:LE - 15], ALU.add, ["pa2"], ["pa3"])
                grp = [(0, 0, a2, 14, 2, "pa0"), (0, 64, a4, 12, 4, "pa1"), (1, 0, a8, 8, 8, "pa2"), (1, 64, a16, 0, 16, "pa3")]
                for (cc, p0, aw, off, w, ak) in grp:
                    stt(dT[p0:p0 + 64, cc, :].rearrange("p (b t) -> p b t", b=NBK), aw[p0:p0 + 64, cc, :, off:off + L], 1.0 / w,
                        E[p0:p0 + 64, cc, :, 15:15 + L], ALU.mult, ALU.subtract, [ak, Ek], ["dT"])
                    if (not samp) and R == 0:
                        tt('pool', ptmp[p0:p0 + 64, :], aw[p0:p0 + 64, cc, 0, off:off + 128],
                           cstt[p0:p0 + 64, C_INV + cc * 128:C_INV + (cc + 1) * 128], ALU.mult, [ak, "cstt"], ["ptmp"])
                        tt('pool', dT[p0:p0 + 64, cc, 0:128], ptmp[p0:p0 + 64, :], E[p0:p0 + 64, cc, 0, 15:143], ALU.subtract,
                           ["ptmp", Ek, "dT"], ["dT"])
                for cc in range(2):
                    p_, pk = nextpg()
                    for gg in range(2):
                        mm(p_[64 * gg:64 * gg + 64, :NT], wpb[64 * gg:64 * gg + 64, l, cc, :], dT[64 * gg:64 * gg + 64, cc, :], True, True,
                           ["wpb", "dT"], [pk])
                    stt(mixT[:, 6 + cc, :], p_[:, :NT], prmt[:, pb + P_PS + cc:pb + P_PS + cc + 1], gate_c[:, cc, :], ALU.mult, ALU.mult,
                        [pk, "prmt", "gate_c"], ["mixT"])

            prefix_items = []
            if not samp:
                sgk = "sg%d_%d" % (l, R); tgk = "tg%d_%d" % (l, R)

                def _pf_loads(l=l, R=R, tgk=tgk):
                    for r in range(4):
                        P.dma(Dg[:, r, :], t_g[l][R][r * TROWS:r * TROWS + 4, :].rearrange("r (a b) -> (r a) b", b=16),
                              [tgk], ["Dg"])
                        P.dma(Tg[:, r, :], t_g[l][R][r * TROWS + 4:r * TROWS + 19, :].rearrange("r c -> (r c)").rearrange("(p b) -> p b", b=60),
                              [tgk], ["Tg"])
                prefix_items.append(_pf_loads)
                mi = 0
                for rr in range(2):
                    for r in range(4):
                        for c in range(2):
                            def _pf_chunk(l=l, R=R, rr=rr, r=r, c=c, mi=mi, sgk=sgk):
                                ch = 2 * rr + c
                                smt = Sm[mi % 2]
                                P.dma(smt[:], s_g[l][R][r * SROWS + ch * 128: r * SROWS + (ch + 1) * 128, :], [sgk], [K(smt)])
                                oh = cstt[:, C_SEL + 4 + r:C_SEL + 5 + r]
                                if r == 0:
                                    ts('dve', Ssel[c][:], Srun[:], oh, None, ALU.mult, None, ["Srun", "cstt"], [K(Ssel[c])])
                                else:
                                    stt(Ssel[c][:], Srun[:], oh, Ssel[c][:], ALU.mult, ALU.add, ["Srun", "cstt", K(Ssel[c])], [K(Ssel[c])])
                                sv = Srun[:, :].rearrange("p (h v) -> p h v", h=4)
                                tt('dve', sv, sv, Dg[:, r, ch * 4:(ch + 1) * 4].unsqueeze(2).to_broadcast([128, 4, 128]), ALU.mult,
                                   ["Srun", "Dg"], ["Srun"])
                                tt('dve', Srun[:], Srun[:], smt[:], ALU.add, ["Srun", K(smt)], ["Srun"])
                                if r == 3:
                                    cpy('act', Sbf[:, ch, :], Ssel[c][:], [K(Ssel[c])], ["Sbf"])
                            prefix_items.append(_pf_chunk)
                            mi += 1
                if R == NRP - 1:
                    prefix_items.append(lambda l=l: P.dma(o_phs[l].rearrange("h k v -> k h v"), Srun[:, :].rearrange("p (h v) -> p h v", h=4), ["Srun"], ()))
                prefix_items.append(emit_pooling)
            if STOP == 14:
                P.finish(); return nc, P, es
            if not samp:
                for p in range(2):
                    groups = list(range(2 * R + 1, -1, -1))
                    steps = [(g, r) for g in groups for r in (3, 2, 1, 0)]
                    sl_ = []
                    for si, (g, r) in enumerate(steps):
                        kt_, vt_ = kTg[(si // 4) % 2], vg[(si // 4) % 2]
                        pre = None
                        if r == 3:
                            Rg = g // 2; gl = g % 2
                            gk = "kvg%d_%d" % (l, Rg)
                            src = kv_g[l][Rg]

                            def pre(kt_=kt_, vt_=vt_, src=src, gl=gl, gk=gk, p=p):
                                P.dma(kt_[:], src.rearrange("(r x) t -> x r t", r=4)[p * 128:(p + 1) * 128, :, gl * 128:(gl + 1) * 128],
                                      [gk], [K(kt_)])
                                P.dma(vt_[:], src.rearrange("(r x) d -> x r d", r=4)[256 + gl * 128:256 + (gl + 1) * 128, :, p * 128:(p + 1) * 128],
                                      [gk], [K(vt_)])
                        lanes = []
                        for hh in range(2):
                            hs_ = slice(64 * hh, 64 * hh + 64)
                            masks = []
                            if g >= 2 * R:
                                rr = g - 2 * R
                                if rr == 1:
                                    masks.append((0, 128, None, 0))
                                masks.append((rr * 128, rr * 128 + 128, cstt[:, C_SBM + r * 128:C_SBM + (r + 1) * 128], 0))
                            lanes.append(dict(idx=hh, W=NT,
                                              z=[(kt_[hs_, r, :], qTb[hs_, p, :], 0, NT, [K(kt_), "qTb"])],
                                              masks=masks,
                                              pv=[(vt_[:, r, hs_], 0, NT, po[hs_, :NT], [K(vt_)], "po")]))
                        sl_.append(sb_stages(lanes, 128, si == 0, si == len(steps) - 1, si % 2, pre))
                    if p == 1:
                        run_steps(sl_, prefix_items); prefix_items = []
                    else:
                        run_steps(sl_)
                    tt('dve', mixT[:, 4 + p, :], po[:, :NT], gate_b[:, p, :], ALU.mult, ["po", "gate_b"], ["mixT"])
            else:
                for p in range(2):
                    cpy('pool', knew[:, p, :, 0:64], kTl[:, p, :].rearrange("x (b t) -> x b t", b=4), ["kTl"], ["knew"])
                for h in range(4):
                    hs_ = slice(64 * (h % 2), 64 * (h % 2) + 64)
                    cpy('pool', qm[hs_, h, :], qTb[hs_, h // 2, :], ["qTb"], ["qm"])
                for b in range(4):
                    kTp = bstg[0][:, :].rearrange("x (p t) -> x p t", p=2)
                    vp = bstg[1][:, :].rearrange("q (n d) -> q n d", n=8)
                    if b > 0:
                        load_past_kv(b)
                    if STOP == 150:
                        P.finish(); return nc, P, es
                    bs_ = slice(b * 64, b * 64 + 64)
                    steps = [8] + list(range(7, -1, -1))
                    sl_ = []
                    for si, n in enumerate(steps):
                        nk = 128
                        z = []; pv = []
                        for h in range(4):
                            hh, p = h % 2, h // 2
                            hs_ = slice(64 * hh, 64 * hh + 64)
                            if n == 8:
                                z.append((knew[:, p, b, :], qm[:, h, bs_], h * 64, h * 64 + 64, ["knew", "qm"]))
                                pv.append((vnew[:, b, h * 64:(h + 1) * 64], h * 64, h * 64 + 64,
                                           (po, pc[1])[p][hs_, 0:64], ["vnew"], ("po", "pc1")[p]))
                            else:
                                z.append((kTp[:, p, n * 128:(n + 1) * 128], qm[:, h, bs_], h * 64, h * 64 + 64, [K(bstg[0]), "qm"]))
                                pv.append((vp[:, n, h * 64:(h + 1) * 64], h * 64, h * 64 + 64, (po, pc[1])[p][hs_, 0:64], [K(bstg[1])], ("po", "pc1")[p]))
                        masks = [(0, 256, cstt[:, C_SM:C_SM + 64].unsqueeze(1).to_broadcast([128, 4, 64]), 4)] if n == 8 else []
                        sl_.append(sb_stages([dict(idx=0, W=256, z=z, masks=masks, pv=pv)], nk, si == 0, si == len(steps) - 1, si % 2))
                    run_steps(sl_)
                    for p in range(2):
                        tt('dve', mixT[:, 4 + p, bs_], (po, pc[1])[p][:, 0:64], gate_b[:, p, bs_], ALU.mult, [("po", "pc1")[p], "gate_b"], ["mixT"])

            if not samp:
                for it_ in prefix_items:
                    it_()
                prefix_items = []
            else:
                for b in range(4):
                    P.dma(Sin_s[:, :].rearrange("p (h v) -> p h v", h=4), sth[l, b].rearrange("h k v -> k h v"), (), [K(Ssel[0])])
                    cpy('act', Sbf[:, b, :], Sin_s[:], [K(Ssel[0])], ["Sbf"])
                    sv = Sin_s[:, :].rearrange("p (h v) -> p h v", h=4)
                    tt('pool', sv, sv, Dt[:, b, :].unsqueeze(2).to_broadcast([128, 4, 128]), ALU.mult, [K(Ssel[0]), "Dt"], [K(Ssel[0])])
                    tt('pool', Sin_s[:], Sin_s[:], Sst[:, b, :], ALU.add, [K(Ssel[0]), "Sst"], [K(Ssel[0])])
                    P.dma(o_shs[l, b].rearrange("h k v -> k h v"), Sin_s[:, :].rearrange("p (h v) -> p h v", h=4), [K(Ssel[0])], ())
            for h in range(4):
                p_, pk = nextpg()
                for ch in range(4):
                    cs_ = slice(ch * 64, (ch + 1) * 64)
                    mm(p_[:, cs_], v_tok[:, ch, h * 128:(h + 1) * 128], attT[:, h, cs_], True, False, ["v_tok", "attT%d" % h], [pk])
                    mm(p_[:, cs_], Sbf[:, ch, h * 128:(h + 1) * 128], qG[:, h, cs_], False, True, ["Sbf", "qG%d" % h], [pk])
                q = sqb[h % 2]
                act(q[:], p_[:, :NT], AF.Square, [pk], [K(q)])
                s_, sk_ = nextpg()
                mm(s_[:, :NT], oneb[:], q[:], True, True, [K(q), "oneb"], [sk_])
                act(lnv[:], s_[:, :NT], AF.Ln, [sk_, "epst"], ["lnv"], bias=epst[:, 0:1], scale=1.0 / 128)
                act(rstd[:], lnv[:], AF.Exp, ["lnv"], ["rstd"], scale=-0.5)
                stt(t1[:], p_[:, :NT], prmt[:, pb + P_OG + h:pb + P_OG + h + 1], rstd[:], ALU.mult, ALU.mult,
                    [pk, "prmt", "rstd"], ["t1"])
                tt('pool', mixT[:, h, :], t1[:], gate_a[:, h, :], ALU.mult, ["t1", "gate_a"], ["mixT"])

            if STOP == 13:
                P.finish(); return nc, P, es
            if samp:
                emit_pooling()
            if STOP == 15:
                P.finish(); return nc, P, es
            linear_norm_res(0, l, mixT, "mixT", pb + P_NPOST)
            rmsnorm_to_bf(xT, "xT", NT, pb + P_XPRE, hT, "hT")
            for g2 in range(2):
                wt, wk = load_wg(wsq_bf[l][1][:, :, g2 * 512:(g2 + 1) * 512], WQK(l, 1))
                for oc in range(4):
                    p_, pk = proj_fm(wt, wk, oc * 128, hT, "hT", NT)
                    ts('dve', qxT[:, g2 * 4 + oc, :], p_[:, :NT], 1.0 / 16, None, ALU.mult, None, [pk], ["qxT"])
            TS = 64 if samp else 128
            for st_ in range(NT // TS):
                tsl = slice(st_ * TS, (st_ + 1) * TS)
                if samp:
                    kst, vst = stg[0], stg[1]
                    P.dma(kst[:, :].rearrange("q (n d) -> q n d", n=2), cmk[l, st_].rearrange("(n q) d -> q n d", q=128), (), [K(kst)])
                    P.dma(vst[:, :].rearrange("q (n d) -> q n d", n=2), cmv[l, st_].rearrange("(n q) d -> q n d", q=128), (), [K(vst)])
                    cpy('pool', memV[:].rearrange("q n d -> q (n d)"), vst[:], [K(vst)], ["memV"])
                    for mb in range(2):
                        for hf in range(2):
                            p_, pk = nextpg()
                            for c4 in range(4):
                                c = hf * 4 + c4
                                tp(p_[:, c4 * 128:(c4 + 1) * 128], kst[:, mb * 1024 + c * 128: mb * 1024 + (c + 1) * 128], idf,
                                   [K(kst), "cstt"], [pk])
                            cpy('act', memKT[:, hf * 4:hf * 4 + 4, mb * 128:(mb + 1) * 128],
                                p_[:, :].rearrange("p (c t) -> p c t", c=4), [pk], ["memKT"])
                pt_, ptk = pz[0], "pz0"
                ptb16 = pt_[:, :].bitcast(BF16)
                for hx in range(4):
                    p_, pk = nextpg()
                    mm(p_[:TS, :256], qxT[:, 2 * hx, tsl], memKT[:, 2 * hx, :], True, False, ["qxT", "memKT"], [pk])
                    mm(p_[:TS, :256], qxT[:, 2 * hx + 1, tsl], memKT[:, 2 * hx + 1, :], False, True, ["qxT", "memKT"], [pk])
                    P.op('dve', lambda e, p_=p_, TS=TS: e.reduce_max(sm4[:TS, 0:1], p_[:TS, :256], AX.X), [pk], ["sm4"])
                    ts('dve', sm4[:TS, 1:2], sm4[:TS, 0:1], -1.0, None, ALU.mult, None, ["sm4"], ["sm4"])
                    act(pexp[:TS, :], p_[:TS, :256], AF.Exp, [pk, "sm4"], ["pexp", "sm4"], bias=sm4[:TS, 1:2], accum=sm4[:TS, 2:3])
                    P.op('dve', lambda e, TS=TS: e.reciprocal(sm4[:TS, 3:4], sm4[:TS, 2:3]), ["sm4"], ["sm4"])
                    ts('dve', pbf[:TS, :], pexp[:TS, :], sm4[:TS, 3:4], None, ALU.mult, None, ["pexp", "sm4"], ["pbf"])
                    for mc in range(2):
                        j_ = hx * 2 + mc
                        tp(ptb16[:, j_ * 128:j_ * 128 + TS], pbf[:TS, mc * 128:(mc + 1) * 128], idb[:TS, :TS], ["pbf", "idb"], [ptk])
                cpy('act', pT_sb[:, :, :TS], ptb16[:, :1024].rearrange("p (j t) -> p j t", j=8)[:, :, :TS], [ptk], ["pT_sb"])
                for hf in range(2):
                    p_, pk = nextpg()
                    for o4 in range(4):
                        oc = hf * 4 + o4; hx = oc // 2
                        for mc in range(2):
                            mm(p_[:, o4 * TS:(o4 + 1) * TS], memV[:, mc, oc * 128:(oc + 1) * 128], pT_sb[:, 2 * hx + mc, :TS],
                               mc == 0, mc == 1, ["memV", "pT_sb"], [pk])
                    cpy('dve', oxT[:, hf * 4:hf * 4 + 4, tsl], p_[:, :4 * TS].rearrange("p (o t) -> p o t", o=4), [pk], ["oxT"])
            linear_norm_res(2, l, oxT, "oxT", pb + P_XPOST)
            if STOP == 16:
                P.finish(); return nc, P, es
            if l < NLAY - 1:
                P.dma(xs[:, :, t0:t0 + NT], xT[:], ["xT"], ["xs%d" % R])
            else:
                yd = y_s if samp else y_p
                for blk in range(2):
                    for hf in range(2):
                        p_, pk = nextpg()
                        for c4 in range(4):
                            c = hf * 4 + c4
                            tp(p_[:, c4 * 128:(c4 + 1) * 128], xT[:, c, blk * 128:(blk + 1) * 128], idf, ["xT", "cstt"], [pk])
                        cpy('act' if hf else 'dve', ystage[:, hf * 512:(hf + 1) * 512], p_[:, :], [pk], [K(xin[1])])
                    P.dma(yd[ot0 + blk * 128: ot0 + (blk + 1) * 128, :], ystage[:], [K(xin[1])], ())

    P.finish()
    return nc, P, es


def _prepend_mask_copy(P):
    pass


_CACHE = {}


def _consts(core):
    j = core % 4
    c = np.zeros((128, NCST), np.float32)
    c[:, C_ID:C_ID + 128] = np.eye(128, dtype=np.float32)
    jj, ss = np.meshgrid(np.arange(128), np.arange(128), indexing="ij")
    c[:, C_TRI:C_TRI + 128] = (jj >= ss).astype(np.float32)
    for r in range(4):
        if r < j:
            m = np.ones((128, 128), np.float32)
        elif r == j:
            m = (jj < ss).astype(np.float32)
        else:
            m = np.zeros((128, 128), np.float32)
        c[:, C_SBM + r * 128:C_SBM + (r + 1) * 128] = m
    k6, q6 = np.meshgrid(np.arange(64), np.arange(64), indexing="ij")
    c[:64, C_SM:C_SM + 64] = (k6 < q6).astype(np.float32)
    c[:64, C_HM:C_HM + 64] = (k6 <= q6).astype(np.float32)
    for r in range(3):
        c[:, C_SEL + r] = 1.0 if r == j - 1 else 0.0
    c[:, C_SEL + 3] = 1.0 if j == 0 else 0.0
    for r in range(4):
        c[:, C_SEL + 4 + r] = 1.0 if r == j else 0.0
    wins = {(0, 0): 2, (0, 1): 4, (1, 0): 8, (1, 1): 16}
    pos = np.arange(128)
    for cc in range(2):
        for gg in range(2):
            w = wins[(cc, gg)]
            inv = 1.0 / np.minimum(pos + 1, w) if j == 0 else np.full(128, 1.0 / w)
            c[64 * gg:64 * gg + 64, C_INV + cc * 128:C_INV + (cc + 1) * 128] = inv[None, :].astype(np.float32)
            c[64 * gg:64 * gg + 64, C_W + cc] = 1.0 / w
    rst = np.ones(NT, np.float32); rst[::64] = 0.0
    c[:, C_RST:C_RST + NT] = rst[None, :]
    return c


def kernel(x_prompt, x_sample, mem_prompt, cache_sb_k, cache_sb_v, state_hgrn, state_pool,
           cache_mem_k, cache_mem_v, norm_mix_pre, norm_mix_post, w_in, hgrn_lb_logits,
           hgrn_onorm_g, w_pool, pool_scale, w_out, norm_x_pre, norm_x_post, norm_mem,
           w_xq, w_xk, w_xv, w_xo):
    f = lambda a: np.ascontiguousarray(np.asarray(a, dtype=np.float32))
    x_prompt, x_sample, mem_prompt = f(x_prompt), f(x_sample), f(mem_prompt)
    if "nc" not in _CACHE:
        _CACHE["nc"] = build_program()
        _CACHE["nc"][1].build()
    nc = _CACHE["nc"][0]
    prm = np.zeros((128, 2 * PRM_L), np.float32)
    for l in range(2):
        b = l * PRM_L
        for off, arr in ((P_NPRE, norm_mix_pre), (P_NPOST, norm_mix_post), (P_XPRE, norm_x_pre),
                         (P_XPOST, norm_x_post), (P_NMEM, norm_mem)):
            prm[:, b + off:b + off + 8] = f(arr)[l].reshape(8, 128).T
        prm[:, b + P_OG:b + P_OG + 4] = f(hgrn_onorm_g)[l].reshape(4, 128).T
        prm[:, b + P_PS:b + P_PS + 2] = f(pool_scale)[l].reshape(2, 128).T
    lbl = np.concatenate([f(hgrn_lb_logits)[l].reshape(4, 128).T for l in range(2)], axis=1)
    wp = f(w_pool).reshape(2, 2, 2, 64, 64)
    wpl = np.ascontiguousarray(wp.transpose(2, 3, 0, 1, 4).reshape(128, 2 * 2 * 64))
    in_maps = []
    for c in range(8):
        b, j = c // 4, c % 4
        sl = slice(4 * c, 4 * c + 4)
        in_maps.append({
            "x_p": np.ascontiguousarray(x_prompt[b].reshape(64, 128, 1024)[j::4].reshape(TOKP, 1024)),
            "x_s": np.ascontiguousarray(x_sample[sl].reshape(NT, 1024)),
            "memp": mem_prompt[b],
            "csk": np.ascontiguousarray(f(cache_sb_k)[:, sl].reshape(2, 4, 1024, 256)),
            "csv": np.ascontiguousarray(f(cache_sb_v)[:, sl].reshape(2, 4, 1024, 256)),
            "sth": np.ascontiguousarray(f(state_hgrn)[:, sl]),
            "stp": np.ascontiguousarray(f(state_pool)[:, sl]),
            "cmk": np.ascontiguousarray(f(cache_mem_k)[:, sl].reshape(2, 4, 256, 1024)),
            "cmv": np.ascontiguousarray(f(cache_mem_v)[:, sl].reshape(2, 4, 256, 1024)),
            "w_in": f(w_in), "w_out": f(w_out), "w_xq": f(w_xq), "w_xo": f(w_xo), "w_xk": f(w_xk), "w_xv": f(w_xv),
            "prm": prm, "lbl": np.ascontiguousarray(lbl), "wpl": wpl, "cst": _consts(c),
        })
    _r = run_bass_kernel_spmd(nc, in_maps, core_ids=list(range(8)), **({'trace': True} if os.environ.get('KDBG_TRACE') else {}))
    if os.environ.get('KDBG_TRACE'):
        print('EXEC_NS', _r.exec_time_ns)
    res = _r.results
    yp = np.zeros((2, 64, 128, 1024), np.float32)
    pk = np.zeros((2, 2, 64, 128, 256), np.float32); pv = np.zeros_like(pk)
    for c in range(8):
        b, j = c // 4, c % 4
        yp[b, j::4] = res[c]["y_p"].reshape(16, 128, 1024)
        pk[:, b, j::4] = res[c]["o_pk"].reshape(2, 16, 128, 256)
        pv[:, b, j::4] = res[c]["o_pv"].reshape(2, 16, 128, 256)
    cat = lambda k, shp: np.concatenate([res[c][k].reshape(shp) for c in range(8)], axis=1)
    ys = np.concatenate([res[c]["y_s"].reshape(4, 64, 1024) for c in range(8)], axis=0)
    return (yp.reshape(2, 8192, 1024), ys,
            pk.reshape(2, 2, 8192, 4, 64), pv.reshape(2, 2, 8192, 4, 64),
            np.stack([res[0]["o_phs"], res[4]["o_phs"]], axis=1),
            np.stack([res[3]["o_pps"], res[7]["o_pps"]], axis=1),
            np.stack([res[0]["o_pmk"], res[4]["o_pmk"]], axis=1).reshape(2, 2, 256, 4, 256),
            np.stack([res[0]["o_pmv"], res[4]["o_pmv"]], axis=1).reshape(2, 2, 256, 4, 256),
            cat("o_sk", (2, 4, 64, 4, 64)), cat("o_sv", (2, 4, 64, 4, 64)),
            cat("o_shs", (2, 4, 4, 128, 128)), cat("o_sps", (2, 4, 15, 256)))
```
